# Optimizing a Trainium2 kernel written in Bass

```python
import jax, jax.numpy as jnp
from jax import lax
import numpy as np

D_MODEL = 1024
BATCH = 8
SEQ = 4096
DEPTH = 4

N_META = 16
D_MIX = D_MODEL
D_CONV = D_MIX // 2
CONV_WIDTH = 31
N_HEADS = 8
HEAD_DIM = 64
D_ATTN = N_HEADS * HEAD_DIM
N_IDX_HEADS = 8
IDX_DIM = 64
TOPK_MAX = 256
ROPE_THETA = 10000.0
Q_BLOCK = 64
POS_BLOCK = 128
LN_EPS = 1e-5
IDX_SCALE = IDX_DIM ** -0.5
IDX_W_SCALE = N_IDX_HEADS ** -0.5
DEEPNORM_ALPHA = (2.0 * DEPTH) ** 0.25
DEEPNORM_BETA = (8.0 * DEPTH) ** -0.25

IN_SIZES = [D_CONV, D_CONV, D_CONV, D_ATTN, D_ATTN, D_ATTN, D_ATTN,
            N_IDX_HEADS * IDX_DIM, IDX_DIM, N_IDX_HEADS]
D_IN = sum(IN_SIZES)
IN_SPLITS = [sum(IN_SIZES[:i + 1]) for i in range(len(IN_SIZES) - 1)]

kernel_name = "hymba_conformer_dsa_deepnorm"


def layer_norm(x, g, b):
    xf = x.astype(jnp.float32)
    mu = jnp.mean(xf, axis=-1, keepdims=True)
    var = jnp.mean(jnp.square(xf - mu), axis=-1, keepdims=True)
    y = (xf - mu) * lax.rsqrt(var + LN_EPS) * g.astype(jnp.float32) + b.astype(jnp.float32)
    return y.astype(x.dtype)


def rope_tables(length, dim):
    inv_freq = ROPE_THETA ** (-jnp.arange(0, dim, 2, dtype=jnp.float32) / dim)
    ang = jnp.arange(length, dtype=jnp.float32)[:, None] * inv_freq[None, :]
    return jnp.cos(ang), jnp.sin(ang)


def apply_rope(x, cos, sin):
    half = x.shape[-1] // 2
    x1, x2 = x[..., :half], x[..., half:]
    out = jnp.concatenate([x1 * cos - x2 * sin, x2 * cos + x1 * sin], axis=-1)
    return out.astype(x.dtype)


def conformer_conv_branch(a, g, z, w_dw, b_dw, ln_g, ln_b):
    u = a * jax.nn.sigmoid(g)
    u = lax.conv_general_dilated(
        u, w_dw[:, None, :].astype(u.dtype), window_strides=(1,),
        padding=[(CONV_WIDTH - 1, 0)],
        dimension_numbers=('NWC', 'WIO', 'NWC'),
        feature_group_count=D_CONV) + b_dw
    u = jax.nn.silu(layer_norm(u, ln_g, ln_b))
    return u * jax.nn.silu(z)


def dsa_sparse_attention(q, k, v, qi, ki, wi, topk):
    B, L = q.shape[0], q.shape[1]
    nb = L // Q_BLOCK
    key_pos = jnp.arange(L)
    ki32 = ki.astype(jnp.float32)

    def to_blocks(t):
        return jnp.moveaxis(t.reshape((B, nb, Q_BLOCK) + t.shape[2:]), 1, 0)

    def one_block(args):
        qb, qib, wib, start = args
        q_pos = start + jnp.arange(Q_BLOCK)
        causal = key_pos[None, :] <= q_pos[:, None]
        logits = jnp.einsum('bqhd,bsd->bqhs', qib.astype(jnp.float32), ki32) * IDX_SCALE
        score = jnp.einsum('bqhs,bqh->bqs', jax.nn.relu(logits),
                           wib.astype(jnp.float32) * IDX_W_SCALE)
        score = jnp.where(causal[None], score, -jnp.inf)
        _, idx = lax.top_k(score, topk)
        valid = idx <= q_pos[None, :, None]
        kg = jax.vmap(lambda kk, ii: kk[ii])(k, idx)
        vg = jax.vmap(lambda vv, ii: vv[ii])(v, idx)
        s = jnp.einsum('bqhd,bqkhd->bqhk', qb, kg).astype(jnp.float32) * (HEAD_DIM ** -0.5)
        s = jnp.where(valid[:, :, None, :], s, -jnp.inf)
        p = jax.nn.softmax(s, axis=-1).astype(v.dtype)
        return jnp.einsum('bqhk,bqkhd->bqhd', p, vg)

    starts = jnp.arange(nb) * Q_BLOCK
    out = lax.map(one_block, (to_blocks(q), to_blocks(qi), to_blocks(wi), starts))
    return jnp.moveaxis(out, 0, 1).reshape(B, L, D_ATTN)


def setup_inputs(seed: int = 0) -> dict:
    key = jax.random.key(seed)
    ks = jax.random.split(key, 10)
    f32 = jnp.float32
    x = jax.random.normal(ks[0], (BATCH, SEQ, D_MODEL), f32)
    meta_tokens = jax.random.normal(ks[1], (N_META, D_MODEL), f32)
    w_in = jax.random.normal(ks[2], (DEPTH, D_MODEL, D_IN), f32) * D_MODEL ** -0.5
    conv_w = jax.random.normal(ks[3], (DEPTH, CONV_WIDTH, D_CONV), f32) * CONV_WIDTH ** -0.5
    conv_b = 0.01 * jax.random.normal(ks[4], (DEPTH, D_CONV), f32)
    conv_ln_g = 1.0 + 0.01 * jax.random.normal(ks[5], (DEPTH, D_CONV), f32)
    conv_ln_b = 0.01 * jax.random.normal(ks[6], (DEPTH, D_CONV), f32)
    w_out = jax.random.normal(ks[7], (DEPTH, D_MIX, D_MODEL), f32) * (D_MIX ** -0.5) * DEEPNORM_BETA
    post_ln_g = 1.0 + 0.01 * jax.random.normal(ks[8], (DEPTH, D_MODEL), f32)
    post_ln_b = 0.01 * jax.random.normal(ks[9], (DEPTH, D_MODEL), f32)
    return {"x": x, "meta_tokens": meta_tokens, "w_in": w_in, "conv_w": conv_w,
            "conv_b": conv_b, "conv_ln_g": conv_ln_g, "conv_ln_b": conv_ln_b,
            "w_out": w_out, "post_ln_g": post_ln_g, "post_ln_b": post_ln_b}


def reference(x, meta_tokens, w_in, conv_w, conv_b, conv_ln_g, conv_ln_b, w_out,
              post_ln_g, post_ln_b):
    B, S, D = x.shape
    L = S + N_META
    topk = min(TOPK_MAX, L // 4)
    Lp = -(-L // POS_BLOCK) * POS_BLOCK
    meta = jnp.broadcast_to(meta_tokens[None].astype(x.dtype), (B, N_META, D))
    h = jnp.concatenate([meta, x, jnp.zeros((B, Lp - L, D), x.dtype)], axis=1)
    cos, sin = rope_tables(Lp, HEAD_DIM)
    cos_h, sin_h = cos[:, None, :], sin[:, None, :]

    for l in range(DEPTH):
        proj = jnp.einsum('bld,de->ble', h, w_in[l])
        a, g, zc, q, k, v, za, qi, ki, wi = jnp.split(proj, IN_SPLITS, axis=-1)
        y_conv = conformer_conv_branch(a, g, zc, conv_w[l], conv_b[l], conv_ln_g[l], conv_ln_b[l])
        q = apply_rope(q.reshape(B, Lp, N_HEADS, HEAD_DIM), cos_h, sin_h)
        k = apply_rope(k.reshape(B, Lp, N_HEADS, HEAD_DIM), cos_h, sin_h)
        v = v.reshape(B, Lp, N_HEADS, HEAD_DIM)
        qi = apply_rope(qi.reshape(B, Lp, N_IDX_HEADS, IDX_DIM), cos_h, sin_h)
        ki = apply_rope(ki, cos, sin)
        y_attn = dsa_sparse_attention(q, k, v, qi, ki, wi, topk) * jax.nn.silu(za)
        y = jnp.einsum('ble,ed->bld', jnp.concatenate([y_conv, y_attn], axis=-1), w_out[l])
        h = layer_norm(DEEPNORM_ALPHA * h + y, post_ln_g[l], post_ln_b[l])

    return h[:, N_META:L]
```

```python
import numpy as np
from contextlib import ExitStack
import concourse.bass as bass
import concourse.mybir as mybir
from concourse.bass_utils import run_bass_kernel_spmd

F32 = mybir.dt.float32
BF16 = mybir.dt.bfloat16
AF = mybir.ActivationFunctionType
ALU = mybir.AluOpType
AX = mybir.AxisListType

D_MODEL = 1024
SEQ = 4096
N_META = 16
LP = 4224
NT = LP // 128
DEPTH = 4
TOPK = 256
KBIS = 17
LN_EPS = 1e-5
ALPHA = (2.0 * DEPTH) ** 0.25
WI_SCALE = (64 ** -0.5) * (8 ** -0.5)
NEG = -1.0e30
TCH = [(i * 512, 512) for i in range(8)] + [(4096, 128)]

PJ_U, PJ_SZC, PJ_Q, PJ_K, PJ_SZA, PJ_QI, PJ_KI = 0, 4, 8, 12, 16, 20, 24
NPJ = 25
O_A, O_G, O_ZC, O_Q, O_K, O_V, O_ZA, O_QI, O_KI, O_WI = 0, 512, 1024, 1536, 2048, 2560, 3072, 3584, 4096, 4160


def _items():
    items = []
    c = 0
    for j in range(4):
        items.append(("conv", c, 2, PJ_U + j)); c += 2
    for j in range(4):
        items.append(("silu", c, 1, PJ_SZC + j)); c += 1
    for j in range(4):
        items.append(("rope", c, 2, PJ_Q + j)); c += 2
    for j in range(4):
        items.append(("rope", c, 2, PJ_K + j)); c += 2
    for j in range(4):
        items.append(("silu", c, 1, PJ_SZA + j)); c += 1
    for j in range(4):
        items.append(("rope", c, 2, PJ_QI + j)); c += 2
    items.append(("rope", c, 2, PJ_KI)); c += 2
    for hv in range(2):
        items.append(("v", c, 2, hv)); c += 2
    return items, c


ITEMS, NWCH = _items()


def _rot_cols(base, width):
    idx = np.arange(width)
    return base + (idx // 64) * 64 + ((idx % 64) + 32) % 64


def _weight_cols():
    cols = []
    for j in range(4):
        cols.append(O_A + j * 128 + np.arange(128)); cols.append(O_G + j * 128 + np.arange(128))
    for j in range(4):
        cols.append(O_ZC + j * 128 + np.arange(128))
    for base in (O_Q, O_K):
        for j in range(4):
            cols.append(base + j * 128 + np.arange(128))
            cols.append(_rot_cols(base, 512)[j * 128:(j + 1) * 128])
    za = [O_ZA + j * 128 + np.arange(128) for j in range(4)]
    qi = []
    for j in range(4):
        qi.append(O_QI + j * 128 + np.arange(128))
        qi.append(_rot_cols(O_QI, 512)[j * 128:(j + 1) * 128])
    cols = cols + za + qi
    kic = O_KI + np.arange(64)
    kir = _rot_cols(O_KI, 64)
    cols.append(np.concatenate([kic, kic])); cols.append(np.concatenate([kir, kir]))
    return cols


class Sem:
    __slots__ = ("h", "val")

    def __init__(self, h):
        self.h = h
        self.val = 0


class Tr:
    __slots__ = ("w", "r")

    def __init__(self):
        self.w = {}
        self.r = {}


class Eng:
    def __init__(self, name, e, sem, sync_self):
        self.name = name
        self.e = e
        self.sem = sem
        self.sync_self = sync_self
        self.waited = {}


class Prog:
    def __init__(self, nc, es):
        self.nc = nc
        self.es = es
        self.sems = []
        self.PE = Eng("pe", nc.tensor, self.new_sem("s_pe"), False)
        self.ACT = Eng("act", nc.scalar, self.new_sem("s_act"), True)
        self.DVE = Eng("dve", nc.vector, self.new_sem("s_dve"), True)
        self.POOL = Eng("pool", nc.gpsimd, self.new_sem("s_pool"), True)
        self.SP = Eng("sp", nc.sync, self.new_sem("s_sp"), False)
        self.engs = [self.PE, self.ACT, self.DVE, self.POOL, self.SP]
        self.nins = 0

    def new_sem(self, name):
        s = Sem(self.es.enter_context(self.nc.semaphore(name)))
        self.sems.append(s)
        return s

    def _wait(self, E, deps):
        for s, v in deps.items():
            if s is E.sem and not E.sync_self:
                continue
            if E.waited.get(s, 0) < v:
                E.e.wait_ge(s.h, v)
                E.waited[s] = v

    @staticmethod
    def _deps(reads, writes):
        deps = {}
        for t in reads:
            for s, v in t.w.items():
                if deps.get(s, 0) < v:
                    deps[s] = v
        for t in writes:
            for dd in (t.w, t.r):
                for s, v in dd.items():
                    if deps.get(s, 0) < v:
                        deps[s] = v
        return deps

    def op(self, E, ins_fn, reads=(), writes=()):
        self._wait(E, self._deps(reads, writes))
        ins = ins_fn()
        E.sem.val += 1
        ins.then_inc(E.sem.h, 1)
        v = E.sem.val
        for t in writes:
            t.w = {E.sem: v}
            t.r = {}
        for t in reads:
            t.r[E.sem] = v
        self.nins += 1

    def dma(self, Q, sem, out, in_, reads=(), writes=()):
        deps = self._deps(reads, writes)
        deps.pop(sem, None)
        self._wait(Q, deps)
        ins = Q.e.dma_start(out=out, in_=in_)
        sem.val += 16
        ins.then_inc(sem.h, 16)
        for t in writes:
            t.w = {sem: sem.val}
            t.r = {}
        for t in reads:
            t.r[sem] = sem.val
        self.nins += 1

    def barrier(self):
        allv = {s: s.val for s in self.sems if s.val > 0}
        for E in self.engs:
            for s, v in allv.items():
                if s is E.sem:
                    continue
                if E.waited.get(s, 0) < v:
                    E.e.wait_ge(s.h, v)
                    E.waited[s] = v


class Buf:
    def __init__(self, t, tr=None, sem=None):
        self.t = t
        self.tr = tr if tr is not None else Tr()
        self.sem = sem


def build_program(n_layers, dbg=False, phases="ABCD", nitems=None, ntiles=None, cstop=9):
    nc = bass.Bass("TRN2", target_bir_lowering=False)
    es = ExitStack()
    P = Prog(nc, es)
    PE, ACT, DVE, POOL, SP = P.PE, P.ACT, P.DVE, P.POOL, P.SP
    L = n_layers

    def din(name, shape, dt=F32):
        return nc.dram_tensor(name, shape, dt, kind="ExternalInput").ap()

    skind = "ExternalOutput" if dbg else "Internal"
    h0 = din("h0", [LP, D_MODEL])
    WA = din("WA", [L, NWCH, 128, 1024])
    WWI = din("WWI", [L, 128, 64])
    CP = din("CP", [L, 128, 136])
    WO = din("WO", [L, 8, 128, 1024])
    GB = din("GB", [L, 2, 128, 1024])
    c_ident = din("c_ident", [128, 128])
    c_negmask = din("c_negmask", [128, 128])
    c_cos = din("c_cos", [128, LP])
    c_sin = din("c_sin", [128, LP])
    c_pw = din("c_pw", [128, KBIS])
    hout = nc.dram_tensor("hout", [LP, D_MODEL], F32, kind="ExternalOutput").ap()
    hbufs = [nc.dram_tensor(f"hbuf{i}", [LP, D_MODEL], F32, kind="Internal").ap() for i in range(2)] if L > 1 else []
    PJ = nc.dram_tensor("PJ", [NPJ, 128, LP], BF16, kind=skind).ap()
    VA = nc.dram_tensor("VA", [NT, 128, 520], BF16, kind=skind).ap()
    WI = nc.dram_tensor("WI", [128, NT * 8], F32, kind=skind).ap()
    YC = nc.dram_tensor("YC", [8, 128, LP], BF16, kind=skind).ap()

    semcache = {}
    cur = {"l": "g"}

    def sb(stack, name, shape, dt, dma_sem=False):
        t = stack.enter_context(nc.sbuf_tensor(f"{name}_{cur['l']}", shape, dt))
        sem = None
        if dma_sem:
            if name not in semcache:
                semcache[name] = P.new_sem("d_" + name)
            sem = semcache[name]
        return Buf(t, sem=sem)

    PS = [Buf(es.enter_context(nc.psum_tensor(f"ps{i}", [128, 512], F32))) for i in range(6)]
    PQ = [Buf(es.enter_context(nc.psum_tensor(f"pq{i}", [128, 1024], BF16))) for i in range(2)]
    identf = sb(es, "identf", [128, 128], F32, True)
    identb = sb(es, "identb", [128, 128], BF16)
    negmask = sb(es, "negmask", [128, 128], F32, True)
    pw = sb(es, "pw", [128, KBIS], F32, True)
    epst = sb(es, "epst", [128, 1], F32)
    thrneg = sb(es, "thrneg", [128, 1], F32)
    onesm = sb(es, "onesm", [128, 128], F32)
    mones = sb(es, "mones", [128, 8], F32)

    P.dma(SP, identf.sem, identf.t[:], c_ident[:, :], writes=[identf.tr])
    P.dma(SP, negmask.sem, negmask.t[:], c_negmask[:, :], writes=[negmask.tr])
    P.dma(SP, pw.sem, pw.t[:], c_pw[:, :], writes=[pw.tr])
    P.op(DVE, lambda: nc.vector.tensor_copy(out=identb.t[:], in_=identf.t[:]), reads=[identf.tr], writes=[identb.tr])
    P.op(DVE, lambda: nc.vector.memset(epst.t[:], LN_EPS), writes=[epst.tr])
    P.op(DVE, lambda: nc.vector.memset(thrneg.t[:], -1.0e29), writes=[thrneg.tr])
    P.op(DVE, lambda: nc.vector.memset(onesm.t[:], 1.0 / 512.0), writes=[onesm.tr])
    P.op(DVE, lambda: nc.vector.memset(mones.t[:], -1.0), writes=[mones.tr])

    cnt = {"ps": 0}
    LB = [PS[0], PS[1], PS[2], Buf(PQ[1].t[:, :].bitcast(F32), tr=PQ[1].tr)]

    def next_ps3():
        b = PS[cnt["ps"] % 3]
        cnt["ps"] += 1
        return b

    for l in range(L):
        cur["l"] = f"L{l}"
        h_in = h0 if l == 0 else hbufs[(l - 1) % 2]
        h_out = hout if l == L - 1 else hbufs[l % 2]

        with ExitStack() as sA:
            hT = sb(sA, "hT", [128, 8, LP], BF16)
            hT_tr = [Tr() for _ in range(NT)]
            with ExitStack() as s0:
                hld = [sb(s0, f"hld{i}", [128, 1024], F32, True) for i in range(2)]
                hb = [sb(s0, f"hb{i}", [128, 1024], BF16) for i in range(2)]
                for i in range(NT):
                    s = i % 2
                    P.dma(SP, hld[s].sem, hld[s].t[:], h_in[i * 128:(i + 1) * 128, :], writes=[hld[s].tr])
                    if i % 2 == 0:
                        P.op(ACT, lambda s=s: nc.scalar.copy(out=hb[s].t[:], in_=hld[s].t[:]), reads=[hld[s].tr], writes=[hb[s].tr])
                    else:
                        P.op(DVE, lambda s=s: nc.vector.tensor_copy(out=hb[s].t[:], in_=hld[s].t[:]), reads=[hld[s].tr], writes=[hb[s].tr])
                    q = PQ[i % 2]
                    for kc in range(8):
                        P.op(PE, lambda s=s, kc=kc, q=q: nc.tensor.transpose(out=q.t[:, kc * 128:(kc + 1) * 128], in_=hb[s].t[:, kc * 128:(kc + 1) * 128], identity=identb.t[:]),
                             reads=[hb[s].tr, identb.tr], writes=[q.tr])
                    src = q.t[:, :].rearrange("p (k t) -> p k t", k=8)
                    if i % 2 == 0:
                        P.op(DVE, lambda i=i, src=src: nc.vector.tensor_copy(out=hT.t[:, :, i * 128:(i + 1) * 128], in_=src), reads=[q.tr], writes=[hT_tr[i]])
                    else:
                        P.op(ACT, lambda i=i, src=src: nc.scalar.copy(out=hT.t[:, :, i * 128:(i + 1) * 128], in_=src), reads=[q.tr], writes=[hT_tr[i]])
                P.barrier()

            with ExitStack() as s1:
                if "A" not in phases:
                    ITEMS_ = []
                else:
                    ITEMS_ = ITEMS if nitems is None else ITEMS[:nitems]
                cosT = sb(s1, "cosT", [128, LP], F32, True)
                sinT = sb(s1, "sinT", [128, LP], F32, True)
                wld = [sb(s1, f"wld{i}", [128, 2048], F32, True) for i in range(2)]
                wbf = [sb(s1, f"wbf{i}", [128, 2048], BF16) for i in range(2)]
                stage = [sb(s1, f"stage{i}", [128, LP], BF16, True) for i in range(2)]
                sig = [sb(s1, f"sig{i}", [128, 512], F32) for i in range(2)]
                tm1 = [sb(s1, f"tm1{i}", [128, 512], F32) for i in range(2)]
                tm2 = [sb(s1, f"tm2{i}", [128, 512], F32) for i in range(2)]
                vst = [sb(s1, f"vst{i}", [128, 8, 4, 65], BF16, True) for i in range(2)]
                wwif = sb(s1, "wwif", [128, 64], F32, True)
                wwib = sb(s1, "wwib", [128, 64], BF16)
                wisb = sb(s1, "wisb", [128, NT * 8], F32, True)

                P.dma(SP, cosT.sem, cosT.t[:], c_cos[:, :], writes=[cosT.tr])
                P.dma(SP, sinT.sem, sinT.t[:], c_sin[:, :], writes=[sinT.tr])
                P.dma(SP, wwif.sem, wwif.t[:], WWI[l], writes=[wwif.tr])
                P.op(POOL, lambda: nc.gpsimd.tensor_copy(out=wwib.t[:], in_=wwif.t[:]), reads=[wwif.tr], writes=[wwib.tr])
                for s in range(2):
                    P.op(POOL, lambda s=s: nc.gpsimd.memset(vst[s].t[:], 1.0), writes=[vst[s].tr])

                def load_w(it_idx):
                    kind, c0, nch, _ = ITEMS[it_idx]
                    s = it_idx % 2
                    P.dma(SP, wld[s].sem, wld[s].t[:, :nch * 1024].rearrange("p (c f) -> p c f", c=nch), WA[l, c0:c0 + nch].rearrange("c p f -> p c f"), writes=[wld[s].tr])
                    P.op(POOL, lambda s=s, nch=nch: nc.gpsimd.tensor_copy(out=wbf[s].t[:, :nch * 1024], in_=wld[s].t[:, :nch * 1024]),
                         reads=[wld[s].tr], writes=[wbf[s].tr])

                if ITEMS_:
                    load_w(0)
                gctr = 0
                st_ctr = 0
                for it_idx, (kind, c0, nch, pj) in enumerate(ITEMS_):
                    if it_idx + 1 < len(ITEMS_):
                        load_w(it_idx + 1)
                    ws = it_idx % 2
                    wv_ = wbf[ws]
                    if kind != "v":
                        stg = stage[st_ctr % 2]
                        st_ctr += 1
                        for (t0, n) in TCH:
                            bA = PS[(gctr % 2) * 2]
                            bB = PS[(gctr % 2) * 2 + 1]
                            g2 = gctr % 2
                            gctr += 1
                            hts = hT_tr[t0 // 128:(t0 + n) // 128]
                            for kc in range(8):
                                P.op(PE, lambda kc=kc, bA=bA, t0=t0, n=n: nc.tensor.matmul(bA.t[:, :n], lhsT=wv_.t[:, kc * 128:(kc + 1) * 128], rhs=hT.t[:, kc, t0:t0 + n], start=(kc == 0), stop=(kc == 7)),
                                     reads=[wv_.tr] + hts, writes=[bA.tr])
                            if nch == 2:
                                for kc in range(8):
                                    P.op(PE, lambda kc=kc, bB=bB, t0=t0, n=n: nc.tensor.matmul(bB.t[:, :n], lhsT=wv_.t[:, 1024 + kc * 128:1024 + (kc + 1) * 128], rhs=hT.t[:, kc, t0:t0 + n], start=(kc == 0), stop=(kc == 7)),
                                         reads=[wv_.tr] + hts, writes=[bB.tr])
                            if kind == "conv":
                                P.op(ACT, lambda bB=bB, g2=g2, n=n: nc.scalar.activation(out=sig[g2].t[:, :n], in_=bB.t[:, :n], func=AF.Sigmoid), reads=[bB.tr], writes=[sig[g2].tr])
                                P.op(DVE, lambda bA=bA, g2=g2, t0=t0, n=n, stg=stg: nc.vector.tensor_tensor(out=stg.t[:, t0:t0 + n], in0=bA.t[:, :n], in1=sig[g2].t[:, :n], op=ALU.mult),
                                     reads=[bA.tr, sig[g2].tr], writes=[stg.tr])
                            elif kind == "silu":
                                P.op(ACT, lambda bA=bA, t0=t0, n=n, stg=stg: nc.scalar.activation(out=stg.t[:, t0:t0 + n], in_=bA.t[:, :n], func=AF.Silu), reads=[bA.tr], writes=[stg.tr])
                            else:
                                P.op(DVE, lambda bA=bA, g2=g2, t0=t0, n=n: nc.vector.tensor_tensor(out=tm1[g2].t[:, :n], in0=bA.t[:, :n], in1=cosT.t[:, t0:t0 + n], op=ALU.mult),
                                     reads=[bA.tr, cosT.tr], writes=[tm1[g2].tr])
                                P.op(DVE, lambda bB=bB, g2=g2, t0=t0, n=n: nc.vector.tensor_tensor(out=tm2[g2].t[:, :n], in0=bB.t[:, :n], in1=sinT.t[:, t0:t0 + n], op=ALU.mult),
                                     reads=[bB.tr, sinT.tr], writes=[tm2[g2].tr])
                                P.op(POOL, lambda g2=g2, t0=t0, n=n, stg=stg: nc.gpsimd.tensor_tensor(out=stg.t[:, t0:t0 + n], in0=tm1[g2].t[:, :n], in1=tm2[g2].t[:, :n], op=ALU.add),
                                     reads=[tm1[g2].tr, tm2[g2].tr], writes=[stg.tr])
                        P.dma(POOL, stg.sem, PJ[pj], stg.t[:], reads=[stg.tr])
                    else:
                        hv = pj
                        wv3 = wv_.t[:, :].rearrange("p (k e) -> p k e", k=8)
                        for i in range(NT):
                            b = PS[(gctr % 2) * 2]
                            gctr += 1
                            gi, gs = i // 8, (i // 8) % 2
                            for kc in range(8):
                                P.op(PE, lambda kc=kc, b=b, i=i: nc.tensor.matmul(b.t[:, :256], lhsT=hT.t[:, kc, i * 128:(i + 1) * 128], rhs=wv3[:, kc, :], start=(kc == 0), stop=(kc == 7)),
                                     reads=[wv_.tr, hT_tr[i]], writes=[b.tr])
                            P.op(ACT, lambda b=b, i=i, gs=gs: nc.scalar.copy(out=vst[gs].t[:, i % 8, :, 0:64], in_=b.t[:, :256].rearrange("p (h d) -> p h d", h=4)),
                                 reads=[b.tr], writes=[vst[gs].tr])
                            if i % 8 == 7 or i == NT - 1:
                                ng = i % 8 + 1
                                i0 = gi * 8
                                dst = VA[i0:i0 + ng].rearrange("i p (v f) -> p i v f", v=2)[:, :, hv, :]
                                P.dma(POOL, vst[gs].sem, dst, vst[gs].t[:, :ng].rearrange("p i h d -> p i (h d)"), reads=[vst[gs].tr])
                            if hv == 0:
                                b5 = PS[4 + (i % 2)]
                                for kc in range(8):
                                    P.op(PE, lambda kc=kc, b5=b5, i=i: nc.tensor.matmul(b5.t[:, :8], lhsT=hT.t[:, kc, i * 128:(i + 1) * 128], rhs=wwib.t[:, kc * 8:(kc + 1) * 8], start=(kc == 0), stop=(kc == 7)),
                                         reads=[wwib.tr, hT_tr[i]], writes=[b5.tr])
                                P.op(DVE, lambda b5=b5, i=i: nc.vector.tensor_scalar(out=wisb.t[:, i * 8:(i + 1) * 8], in0=b5.t[:, :8], scalar1=WI_SCALE, scalar2=None, op0=ALU.mult),
                                     reads=[b5.tr], writes=[wisb.tr])
                P.dma(POOL, wisb.sem, WI[:, :], wisb.t[:], reads=[wisb.tr])
                P.barrier()

        with ExitStack() as sB:
            upad = [sb(sB, f"upad{j}", [128, 30 + LP], BF16, True) for j in range(4)]
            cp = sb(sB, "cp", [128, 136], F32, True)
            dg = sb(sB, "dg", [128, 4 * 31, 128], BF16)
            szc = [sb(sB, f"szc{i}", [128, 4, 512], BF16, True) for i in range(2)]
            yst = [sb(sB, f"yst{i}", [128, 4, 512], BF16, True) for i in range(2)]
            cf = [sb(sB, f"cf{j}", [128, 512], F32) for j in range(4)]
            sq = [sb(sB, f"sq{j}", [128, 512], F32) for j in range(4)]
            mean_sb = sb(sB, "mean_sb", [128, 512], F32)
            msq = sb(sB, "msq", [128, 512], F32)
            var = sb(sB, "var", [128, 512], F32)
            sd = sb(sB, "sd", [128, 512], F32)
            rstd = sb(sB, "rstd", [128, 512], F32)
            y1 = [sb(sB, f"y1{i}", [128, 512], F32) for i in range(2)]
            y2 = [sb(sB, f"y2{i}", [128, 512], F32) for i in range(2)]
            zz = [sb(sB, f"zz{i}", [128, 512], F32) for i in range(2)]

            P.dma(SP, cp.sem, cp.t[:], CP[l], writes=[cp.tr])
            for j in range(4):
                P.op(POOL, lambda j=j: nc.gpsimd.memset(upad[j].t[:, 0:30], 0.0), writes=[upad[j].tr])
                P.dma(SP, upad[j].sem, upad[j].t[:, 30:], PJ[PJ_U + j], reads=[upad[j].tr], writes=[upad[j].tr])
            for j in range(4):
                for tap in range(31):
                    P.op(DVE, lambda j=j, tap=tap: nc.vector.tensor_scalar(out=dg.t[:, j * 31 + tap, :], in0=identb.t[:], scalar1=cp.t[:, j * 31 + tap:j * 31 + tap + 1], scalar2=None, op0=ALU.mult),
                         reads=[identb.tr, cp.tr], writes=[dg.tr])
            for ci, (t0, n) in enumerate(TCH if "B" in phases else []):
                s = ci % 2
                P.dma(SP, szc[s].sem, szc[s].t[:, :, :n], PJ[PJ_SZC:PJ_SZC + 4, :, t0:t0 + n].rearrange("j p t -> p j t"), writes=[szc[s].tr])
                for j in range(4):
                    for tap in range(31):
                        P.op(PE, lambda j=j, tap=tap, t0=t0, n=n: nc.tensor.matmul(PS[j].t[:, :n], lhsT=dg.t[:, j * 31 + tap, :], rhs=upad[j].t[:, t0 + tap:t0 + tap + n], start=(tap == 0), stop=(tap == 30)),
                             reads=[dg.tr, upad[j].tr], writes=[PS[j].tr])
                for j in range(4):
                    P.op(ACT, lambda j=j, n=n: nc.scalar.activation(out=cf[j].t[:, :n], in_=PS[j].t[:, :n], func=AF.Identity, bias=cp.t[:, 124 + j:125 + j]), reads=[PS[j].tr, cp.tr], writes=[cf[j].tr])
                    P.op(ACT, lambda j=j, n=n: nc.scalar.activation(out=sq[j].t[:, :n], in_=PS[j].t[:, :n], func=AF.Square, bias=cp.t[:, 124 + j:125 + j]), reads=[PS[j].tr, cp.tr], writes=[sq[j].tr])
                for j in range(4):
                    P.op(PE, lambda j=j, n=n: nc.tensor.matmul(PS[4].t[:, :n], lhsT=onesm.t[:], rhs=cf[j].t[:, :n], start=(j == 0), stop=(j == 3)), reads=[onesm.tr, cf[j].tr], writes=[PS[4].tr])
                for j in range(4):
                    P.op(PE, lambda j=j, n=n: nc.tensor.matmul(PS[5].t[:, :n], lhsT=onesm.t[:], rhs=sq[j].t[:, :n], start=(j == 0), stop=(j == 3)), reads=[onesm.tr, sq[j].tr], writes=[PS[5].tr])
                P.op(ACT, lambda n=n: nc.scalar.copy(out=mean_sb.t[:, :n], in_=PS[4].t[:, :n]), reads=[PS[4].tr], writes=[mean_sb.tr])
                P.op(ACT, lambda n=n: nc.scalar.activation(out=msq.t[:, :n], in_=PS[4].t[:, :n], func=AF.Square), reads=[PS[4].tr], writes=[msq.tr])
                P.op(DVE, lambda n=n: nc.vector.tensor_tensor(out=var.t[:, :n], in0=PS[5].t[:, :n], in1=msq.t[:, :n], op=ALU.subtract), reads=[PS[5].tr, msq.tr], writes=[var.tr])
                P.op(ACT, lambda n=n: nc.scalar.activation(out=sd.t[:, :n], in_=var.t[:, :n], func=AF.Sqrt, bias=epst.t[:, 0:1]), reads=[var.tr, epst.tr], writes=[sd.tr])
                P.op(DVE, lambda n=n: nc.vector.reciprocal(out=rstd.t[:, :n], in_=sd.t[:, :n]), reads=[sd.tr], writes=[rstd.tr])
                for j in range(4):
                    k2 = j % 2
                    P.op(DVE, lambda j=j, n=n, k2=k2: nc.vector.tensor_tensor(out=y1[k2].t[:, :n], in0=cf[j].t[:, :n], in1=mean_sb.t[:, :n], op=ALU.subtract), reads=[cf[j].tr, mean_sb.tr], writes=[y1[k2].tr])
                    P.op(POOL, lambda n=n, k2=k2: nc.gpsimd.tensor_tensor(out=y2[k2].t[:, :n], in0=y1[k2].t[:, :n], in1=rstd.t[:, :n], op=ALU.mult), reads=[y1[k2].tr, rstd.tr], writes=[y2[k2].tr])
                    P.op(ACT, lambda j=j, n=n, k2=k2: nc.scalar.activation(out=zz[k2].t[:, :n], in_=y2[k2].t[:, :n], func=AF.Silu, scale=cp.t[:, 128 + j:129 + j], bias=cp.t[:, 132 + j:133 + j]),
                         reads=[y2[k2].tr, cp.tr], writes=[zz[k2].tr])
                    P.op(DVE, lambda j=j, n=n, k2=k2, s=s: nc.vector.tensor_tensor(out=yst[s].t[:, j, :n], in0=zz[k2].t[:, :n], in1=szc[s].t[:, j, :n], op=ALU.mult), reads=[zz[k2].tr, szc[s].tr], writes=[yst[s].tr])
                P.dma(POOL, yst[s].sem, YC[0:4, :, t0:t0 + n].rearrange("j p t -> p j t"), yst[s].t[:, :, :n], reads=[yst[s].tr])
            P.barrier()

        with ExitStack() as sC:
            kT = sb(sC, "kT", [128, 4, LP], BF16, True)
            kiT = sb(sC, "kiT", [128, LP], BF16, True)
            vaug = sb(sC, "vaug", [128, NT, 520], BF16, True)
            wi_all = sb(sC, "wi_all", [128, NT * 8], F32, True)
            qt = [sb(sC, f"qt{i}", [128, 4, 128], BF16, True) for i in range(2)]
            qit = [sb(sC, f"qit{i}", [128, 4, 128], BF16, True) for i in range(2)]
            szat = [sb(sC, f"szat{i}", [128, 4, 128], BF16, True) for i in range(2)]
            dgw = [sb(sC, f"dgw{i}", [128, 8, 128], BF16) for i in range(2)]
            score = [sb(sC, f"score{i}", [128, LP], F32) for i in range(2)]
            score_tr = [[Tr() for _ in range(9)] for _ in range(2)]
            junk = sb(sC, "junk", [128, LP], mybir.dt.uint8)
            mask = sb(sC, "mask", [128, LP], BF16)
            maskT = [sb(sC, f"maskT{i}", [128, LP], BF16) for i in range(2)]
            maskT_tr = [[Tr() for _ in range(5)] for _ in range(2)]
            Rr = [sb(sC, f"Rr{i}", [128, 512], BF16) for i in range(8)]
            Eb = [sb(sC, f"Eb{i}", [128, 512], BF16) for i in range(4)]
            PTb = [sb(sC, f"PTb{i}", [128, 512], BF16) for i in range(6)]
            hi = sb(sC, "hi", [128, 1], F32)
            lo = sb(sC, "lo", [128, 1], F32)
            w0 = sb(sC, "w0", [128, 1], F32)
            Wk = sb(sC, "Wk", [128, KBIS], F32)
            mid = [sb(sC, f"mid{i}", [128, 1], F32) for i in range(2)]
            cntb = sb(sC, "cntb", [128, 1], F32)
            dd = sb(sC, "dd", [128, 1], F32)
            rinv = sb(sC, "rinv", [128, 8], F32)
            rinv_tr = [Tr(), Tr()]
            rsum = sb(sC, "rsum", [128, 8], F32)
            rsum_tr = [Tr(), Tr()]
            osb = sb(sC, "osb", [128, 520], F32)
            osb_tr = [Tr(), Tr()]
            ya = sb(sC, "ya", [128, 512], BF16)
            ya_tr = [Tr() for _ in range(8)]
            yaT = sb(sC, "yaT", [128, 4, 128], BF16)
            yat = [sb(sC, f"yat{i}", [128, 4, 128], BF16, True) for i in range(2)]

            for j in range(4):
                P.dma(SP, kT.sem, kT.t[:, j, :], PJ[PJ_K + j], writes=[kT.tr])
            P.dma(SP, kiT.sem, kiT.t[:], PJ[PJ_KI], writes=[kiT.tr])
            for i0 in range(0, NT, 11):
                P.dma(SP, vaug.sem, vaug.t[:, i0:i0 + 11, :], VA[i0:i0 + 11].rearrange("i p f -> p i f"), writes=[vaug.tr])
            P.dma(SP, wi_all.sem, wi_all.t[:], WI[:, :], writes=[wi_all.tr])

            ctr = {"r": 0, "e": 0, "m": 0}
            NTC = (NT if ntiles is None else ntiles) if "C" in phases else 0

            def load_idx(i):
                s = i % 2
                c0, c1 = i * 128, (i + 1) * 128
                P.dma(SP, qit[s].sem, qit[s].t[:], PJ[PJ_QI:PJ_QI + 4, :, c0:c1].rearrange("j p t -> p j t"), writes=[qit[s].tr])

            def load_att(i):
                s = i % 2
                c0, c1 = i * 128, (i + 1) * 128
                P.dma(SP, qt[s].sem, qt[s].t[:], PJ[PJ_Q:PJ_Q + 4, :, c0:c1].rearrange("j p t -> p j t"), writes=[qt[s].tr])
                P.dma(SP, szat[s].sem, szat[s].t[:], PJ[PJ_SZA:PJ_SZA + 4, :, c0:c1].rearrange("j p t -> p j t"), writes=[szat[s].tr])

            def S1a(i):
                s = i % 2
                N = 128 * (i + 1)
                sc, sctr = score[s], score_tr[s]
                for h in range(8):
                    P.op(POOL, lambda h=h: nc.gpsimd.tensor_scalar(out=dgw[s].t[:, h, :], in0=identb.t[:], scalar1=wi_all.t[:, i * 8 + h:i * 8 + h + 1], scalar2=1.0, op0=ALU.mult, op1=ALU.mult),
                         reads=[identb.tr, wi_all.tr], writes=[dgw[s].tr])
                chunks = [(s0, min(512, N - s0)) for s0 in range(0, N, 512)]
                pendD = []

                def flush_diag():
                    c_, grp_, rbs_, s0_, n_ = pendD.pop(0)
                    for (h, rb) in rbs_:
                        P.op(PE, lambda: nc.tensor.matmul(PS[3].t[:, :n_], lhsT=dgw[s].t[:, h, :], rhs=rb.t[:, :n_], start=(h == 0), stop=(h == 7)),
                             reads=[dgw[s].tr, rb.tr], writes=[PS[3].tr])
                    if grp_ == 1:
                        P.op(ACT, lambda: nc.scalar.copy(out=sc.t[:, s0_:s0_ + n_], in_=PS[3].t[:, :n_]), reads=[PS[3].tr], writes=[sctr[c_]])

                for c, (s0, n) in enumerate(chunks):
                    for grp in range(2):
                        rbs = []
                        for hh in range(4):
                            h = grp * 4 + hh
                            Lb = LB[hh]
                            po = (h % 2) * 64
                            P.op(PE, lambda: nc.tensor.matmul(Lb.t[:, :n], lhsT=qit[s].t[po:po + 64, h // 2, :], rhs=kiT.t[po:po + 64, s0:s0 + n], start=True, stop=True),
                                 reads=[qit[s].tr, kiT.tr], writes=[Lb.tr])
                            rb = Rr[ctr["r"] % 8]
                            ctr["r"] += 1
                            P.op(ACT, lambda: nc.scalar.activation(out=rb.t[:, :n], in_=Lb.t[:, :n], func=AF.Relu), reads=[Lb.tr], writes=[rb.tr])
                            rbs.append((h, rb))
                        if pendD:
                            flush_diag()
                        pendD.append((c, grp, rbs, s0, n))
                while pendD:
                    flush_diag()
                nsc = len(chunks)
                P.op(POOL, lambda: nc.gpsimd.tensor_tensor(out=sc.t[:, N - 128:N], in0=sc.t[:, N - 128:N], in1=negmask.t[:], op=ALU.add),
                     reads=[sctr[nsc - 1], negmask.tr], writes=[sctr[nsc - 1]])

            def S1b(i):
                s = i % 2
                N = 128 * (i + 1)
                sc = score[s]
                sc_trs = score_tr[s][:(N + 511) // 512]
                if cstop < 2:
                    return
                if i < 2:
                    thr = thrneg
                else:
                    P.op(DVE, lambda: nc.vector.tensor_reduce(out=hi.t[:], in_=sc.t[:, :N], axis=AX.X, op=ALU.max), reads=sc_trs, writes=[hi.tr])
                    P.op(DVE, lambda: nc.vector.tensor_reduce(out=lo.t[:], in_=sc.t[:, :256], axis=AX.X, op=ALU.min), reads=sc_trs, writes=[lo.tr])
                    P.op(DVE, lambda: nc.vector.tensor_tensor(out=w0.t[:], in0=hi.t[:], in1=lo.t[:], op=ALU.subtract), reads=[hi.tr, lo.tr], writes=[w0.tr])
                    P.op(DVE, lambda: nc.vector.tensor_scalar(out=Wk.t[:], in0=pw.t[:], scalar1=w0.t[:, 0:1], scalar2=None, op0=ALU.mult), reads=[pw.tr, w0.tr], writes=[Wk.tr])
                    P.op(DVE, lambda: nc.vector.tensor_tensor(out=mid[0].t[:], in0=lo.t[:], in1=Wk.t[:, 0:1], op=ALU.add), reads=[lo.tr, Wk.tr], writes=[mid[0].tr])
                    for k in range(KBIS):
                        mc, mn = mid[k % 2], mid[(k + 1) % 2]
                        kb = k + 1 if k < KBIS - 1 else k
                        P.op(DVE, lambda: nc.vector.tensor_scalar(out=junk.t[:, :N], in0=sc.t[:, :N], scalar1=mc.t[:, 0:1], scalar2=None, op0=ALU.is_ge, op1=ALU.add, accum_out=cntb.t[:, 0:1]),
                             reads=sc_trs + [mc.tr], writes=[junk.tr, cntb.tr])
                        P.op(DVE, lambda: nc.vector.tensor_scalar(out=dd.t[:], in0=cntb.t[:], scalar1=TOPK - 0.5, scalar2=Wk.t[:, k:k + 1], op0=ALU.is_ge, op1=ALU.mult),
                             reads=[cntb.tr, Wk.tr], writes=[dd.tr])
                        P.op(DVE, lambda: nc.vector.scalar_tensor_tensor(out=mn.t[:], in0=dd.t[:], scalar=Wk.t[:, kb:kb + 1], in1=mc.t[:], op0=ALU.subtract, op1=ALU.add),
                             reads=[dd.tr, Wk.tr, mc.tr], writes=[mn.tr])
                    thr = mid[KBIS % 2]
                P.op(DVE, lambda: nc.vector.tensor_scalar(out=mask.t[:, :N], in0=sc.t[:, :N], scalar1=thr.t[:, 0:1], scalar2=None, op0=ALU.is_ge),
                     reads=sc_trs + [thr.tr], writes=[mask.tr])

            def S1c(i):
                if cstop < 3:
                    return
                s = i % 2
                nb = i + 1
                for g in range((nb + 7) // 8):
                    q = PQ[0]
                    m = min(8, nb - g * 8)
                    for bi in range(m):
                        b = g * 8 + bi
                        P.op(PE, lambda: nc.tensor.transpose(out=q.t[:, bi * 128:(bi + 1) * 128], in_=mask.t[:, b * 128:(b + 1) * 128], identity=identb.t[:]),
                             reads=[mask.tr, identb.tr], writes=[q.tr])
                    P.op(ACT, lambda: nc.scalar.copy(out=maskT[s].t[:, g * 1024:g * 1024 + m * 128], in_=q.t[:, :m * 128]), reads=[q.tr], writes=[maskT_tr[s][g]])

            def S2(i):
                if cstop < 4:
                    return
                s = i % 2
                nb = i + 1
                c0, c1 = i * 128, (i + 1) * 128
                mT, mTtr = maskT[s], maskT_tr[s]
                units = [(hp, b0, min(4, nb - b0)) for hp in range(4) for b0 in range(0, nb, 4)]
                pend = []

                def pv(hp, b0, m, ptbs):
                    for hi_, ptb in enumerate(ptbs):
                        h = 2 * hp + hi_
                        Ob = PS[4 + hi_]
                        hh = hp
                        for bi in range(m):
                            b = b0 + bi
                            P.op(PE, lambda: nc.tensor.matmul(Ob.t[:, hh * 65:(hh + 1) * 65], lhsT=ptb.t[:, bi * 128:(bi + 1) * 128], rhs=vaug.t[:, b, h * 65:(h + 1) * 65], start=(b == 0), stop=(b == nb - 1)),
                                 reads=[ptb.tr, vaug.tr], writes=[Ob.tr])

                for (hp, b0, m) in units:
                    Sbs = [LB[ctr["m"] % 4], LB[(ctr["m"] + 1) % 4]]
                    ctr["m"] += 2
                    for bi in range(m):
                        b = b0 + bi
                        for hi_ in range(2):
                            po = hi_ * 64
                            Sb = Sbs[hi_]
                            P.op(PE, lambda: nc.tensor.matmul(Sb.t[:, bi * 128:(bi + 1) * 128], lhsT=kT.t[po:po + 64, hp, b * 128:(b + 1) * 128], rhs=qt[s].t[po:po + 64, hp, :], start=True, stop=True),
                                 reads=[kT.tr, qt[s].tr], writes=[Sb.tr])
                    mtrs = list({id(mTtr[b // 8]): mTtr[b // 8] for b in range(b0, b0 + m)}.values())
                    ptbs = []
                    for hi_ in range(2):
                        Sb = Sbs[hi_]
                        eb = Eb[ctr["e"] % 4]
                        ptb = PTb[ctr["e"] % 6]
                        ctr["e"] += 1
                        P.op(ACT, lambda: nc.scalar.activation(out=eb.t[:, :m * 128], in_=Sb.t[:, :m * 128], func=AF.Exp, scale=0.125), reads=[Sb.tr], writes=[eb.tr])
                        P.op(POOL, lambda: nc.gpsimd.tensor_tensor(out=ptb.t[:, :m * 128], in0=eb.t[:, :m * 128], in1=mT.t[:, b0 * 128:(b0 + m) * 128], op=ALU.mult),
                             reads=[eb.tr] + mtrs, writes=[ptb.tr])
                        ptbs.append(ptb)
                    pend.append((hp, b0, m, ptbs))
                    if len(pend) > 1:
                        pv(*pend.pop(0))
                while pend:
                    pv(*pend.pop(0))
                if cstop < 5:
                    return
                for half in range(2):
                    Ob = PS[4 + half]
                    P.op(ACT, lambda: nc.scalar.copy(out=osb.t[:, half * 260:(half + 1) * 260], in_=Ob.t[:, :260]), reads=[Ob.tr], writes=[osb_tr[half]])
                    P.op(POOL, lambda: nc.gpsimd.tensor_copy(out=rsum.t[:, half * 4:(half + 1) * 4], in_=osb.t[:, half * 260:(half + 1) * 260].rearrange("p (h d) -> p h d", h=4)[:, :, 64]),
                         reads=[osb_tr[half]], writes=[rsum_tr[half]])
                    P.op(POOL, lambda: nc.gpsimd.tensor_tensor(out=rinv.t[:, half * 4:(half + 1) * 4], in0=rsum.t[:, half * 4:(half + 1) * 4], in1=mones.t[:, 0:4], op=ALU.pow),
                         reads=[rsum_tr[half], mones.tr], writes=[rinv_tr[half]])
                for h in range(8):
                    oc = (h % 2) * 260 + (h // 2) * 65
                    ri = (h % 2) * 4 + h // 2
                    P.op(ACT, lambda: nc.scalar.activation(out=ya.t[:, h * 64:(h + 1) * 64], in_=osb.t[:, oc:oc + 64], func=AF.Identity, scale=rinv.t[:, ri:ri + 1]),
                         reads=[osb_tr[h % 2], rinv_tr[h % 2]], writes=[ya_tr[h]])
                if cstop < 6:
                    return
                q = PQ[0]
                for j in range(4):
                    P.op(PE, lambda: nc.tensor.transpose(out=q.t[:, j * 128:(j + 1) * 128], in_=ya.t[:, j * 128:(j + 1) * 128], identity=identb.t[:]),
                         reads=[ya_tr[2 * j], ya_tr[2 * j + 1], identb.tr], writes=[q.tr])
                P.op(ACT, lambda: nc.scalar.copy(out=yaT.t[:], in_=q.t[:, :512].rearrange("p (j t) -> p j t", j=4)), reads=[q.tr], writes=[yaT.tr])
                P.op(POOL, lambda: nc.gpsimd.tensor_tensor(out=yat[s].t[:], in0=yaT.t[:], in1=szat[s].t[:], op=ALU.mult),
                     reads=[yaT.tr, szat[s].tr], writes=[yat[s].tr])
                if cstop < 7:
                    return
                P.dma(POOL, yat[s].sem, YC[4:8, :, c0:c1].rearrange("j p t -> p j t"), yat[s].t[:], reads=[yat[s].tr])

            if NTC > 0:
                load_idx(0)
                if NTC > 1:
                    load_idx(1)
                S1a(0)
            for i in range(NTC):
                if i + 2 < NTC:
                    load_idx(i + 2)
                load_att(i)
                if i + 1 < NTC:
                    S1a(i + 1)
                if i >= 1:
                    S2(i - 1)
                S1b(i)
                S1c(i)
            if NTC > 0:
                S2(NTC - 1)
            P.barrier()

        with ExitStack() as sD:
            yc = sb(sD, "yc", [128, 8, LP], BF16, True)
            wol = [sb(sD, f"wol{i}", [128, 1024], F32, True) for i in range(2)]
            wob = sb(sD, "wob", [128, 8, 1024], BF16)
            wob_tr = [Tr() for _ in range(8)]
            grep = sb(sD, "grep", [128, 1024], F32, True)
            brep = sb(sD, "brep", [128, 1024], F32, True)
            htl = [sb(sD, f"htl{i}", [128, 1024], F32, True) for i in range(2)]
            zt = [sb(sD, f"zt{i}", [128, 1024], F32) for i in range(2)]
            zn = [sb(sD, f"zn{i}", [128, 1024], F32) for i in range(2)]
            o1 = [sb(sD, f"o1{i}", [128, 1024], F32) for i in range(2)]
            ho = [sb(sD, f"ho{i}", [128, 1024], F32, True) for i in range(2)]
            st6 = [sb(sD, f"st6{i}", [128, 12], F32) for i in range(2)]
            mv = [sb(sD, f"mv{i}", [128, 2], F32) for i in range(2)]
            sdv = [sb(sD, f"sdv{i}", [128, 1], F32) for i in range(2)]
            rs = [sb(sD, f"rs{i}", [128, 1], F32) for i in range(2)]
            nmr = [sb(sD, f"nmr{i}", [128, 1], F32) for i in range(2)]

            for ec in range(8):
                P.dma(SP, yc.sem, yc.t[:, ec, :], YC[ec], writes=[yc.tr])
            P.dma(SP, grep.sem, grep.t[:], GB[l, 0], writes=[grep.tr])
            P.dma(SP, brep.sem, brep.t[:], GB[l, 1], writes=[brep.tr])
            for ec in range(8):
                s = ec % 2
                P.dma(SP, wol[s].sem, wol[s].t[:], WO[l, ec], writes=[wol[s].tr])
                P.op(POOL, lambda s=s, ec=ec: nc.gpsimd.tensor_copy(out=wob.t[:, ec, :], in_=wol[s].t[:]), reads=[wol[s].tr], writes=[wob_tr[ec]])
            for i in range(NT if "D" in phases else 0):
                s = i % 2
                P.dma(SP, htl[s].sem, htl[s].t[:], h_in[i * 128:(i + 1) * 128, :], writes=[htl[s].tr])
                for half in range(2):
                    b = PS[(i % 2) * 2 + half]
                    for ec in range(8):
                        P.op(PE, lambda b=b, ec=ec, half=half, i=i: nc.tensor.matmul(b.t[:, :], lhsT=yc.t[:, ec, i * 128:(i + 1) * 128], rhs=wob.t[:, ec, half * 512:(half + 1) * 512], start=(ec == 0), stop=(ec == 7)),
                             reads=[yc.tr, wob_tr[ec]], writes=[b.tr])
                    P.op(DVE, lambda b=b, half=half, s=s: nc.vector.scalar_tensor_tensor(out=zt[s].t[:, half * 512:(half + 1) * 512], in0=htl[s].t[:, half * 512:(half + 1) * 512], scalar=ALPHA, in1=b.t[:, :], op0=ALU.mult, op1=ALU.add),
                         reads=[htl[s].tr, b.tr], writes=[zt[s].tr])
                for half in range(2):
                    P.op(DVE, lambda half=half, s=s: nc.vector.bn_stats(out=st6[s].t[:, half * 6:(half + 1) * 6], in_=zt[s].t[:, half * 512:(half + 1) * 512]), reads=[zt[s].tr], writes=[st6[s].tr])
                P.op(DVE, lambda s=s: nc.vector.bn_aggr(out=mv[s].t[:], in_=st6[s].t[:]), reads=[st6[s].tr], writes=[mv[s].tr])
                P.op(ACT, lambda s=s: nc.scalar.activation(out=sdv[s].t[:], in_=mv[s].t[:, 1:2], func=AF.Sqrt, bias=epst.t[:, 0:1]), reads=[mv[s].tr, epst.tr], writes=[sdv[s].tr])
                P.op(DVE, lambda s=s: nc.vector.reciprocal(out=rs[s].t[:], in_=sdv[s].t[:]), reads=[sdv[s].tr], writes=[rs[s].tr])
                P.op(DVE, lambda s=s: nc.vector.scalar_tensor_tensor(out=nmr[s].t[:], in0=mv[s].t[:, 0:1], scalar=-1.0, in1=rs[s].t[:], op0=ALU.mult, op1=ALU.mult), reads=[mv[s].tr, rs[s].tr], writes=[nmr[s].tr])
                P.op(ACT, lambda s=s: nc.scalar.activation(out=zn[s].t[:], in_=zt[s].t[:], func=AF.Identity, scale=rs[s].t[:, 0:1], bias=nmr[s].t[:, 0:1]), reads=[zt[s].tr, rs[s].tr, nmr[s].tr], writes=[zn[s].tr])
                P.op(DVE, lambda s=s: nc.vector.tensor_tensor(out=o1[s].t[:], in0=zn[s].t[:], in1=grep.t[:], op=ALU.mult), reads=[zn[s].tr, grep.tr], writes=[o1[s].tr])
                P.op(POOL, lambda s=s: nc.gpsimd.tensor_tensor(out=ho[s].t[:], in0=o1[s].t[:], in1=brep.t[:], op=ALU.add), reads=[o1[s].tr, brep.tr], writes=[ho[s].tr])
                P.dma(POOL, ho[s].sem, h_out[i * 128:(i + 1) * 128, :], ho[s].t[:], reads=[ho[s].tr])
            P.barrier()

    es.close()
    return nc, P.nins


def _rope_tables():
    inv_freq = (10000.0 ** (-np.arange(0, 64, 2, dtype=np.float32) / np.float32(64))).astype(np.float32)
    ang = np.arange(LP, dtype=np.float32)[:, None] * inv_freq[None, :]
    cos = np.cos(ang).astype(np.float32).T
    sin = np.sin(ang).astype(np.float32).T
    p = np.arange(128)
    d = p % 64
    cosT = cos[d % 32]
    sinT = np.where((d < 32)[:, None], -sin[d % 32], sin[d % 32])
    return np.ascontiguousarray(cosT, dtype=np.float32), np.ascontiguousarray(sinT, dtype=np.float32)


def _prep_weights(w_in, conv_w, conv_b, conv_ln_g, conv_ln_b, w_out, post_ln_g, post_ln_b):
    Ld = w_in.shape[0]
    cols = _weight_cols()
    WA = np.empty((Ld, NWCH, 128, 1024), np.float32)
    for l in range(Ld):
        for c, cc in enumerate(cols):
            blk = w_in[l][:, cc]
            WA[l, c] = blk.reshape(8, 128, 128).transpose(1, 0, 2).reshape(128, 1024)
        for hv in range(2):
            blk = w_in[l][:, O_V + hv * 256:O_V + (hv + 1) * 256]
            arr = blk.reshape(8, 128, 256).transpose(1, 0, 2).reshape(128, 2048)
            WA[l, len(cols) + 2 * hv] = arr[:, :1024]
            WA[l, len(cols) + 2 * hv + 1] = arr[:, 1024:]
    WWI = np.ascontiguousarray(w_in[:, :, O_WI:O_WI + 8].reshape(Ld, 8, 128, 8).transpose(0, 2, 1, 3).reshape(Ld, 128, 64))
    CPa = np.empty((Ld, 128, 136), np.float32)
    CPa[:, :, 0:124] = conv_w.reshape(Ld, 31, 4, 128).transpose(0, 3, 2, 1).reshape(Ld, 128, 124)
    CPa[:, :, 124:128] = conv_b.reshape(Ld, 4, 128).transpose(0, 2, 1)
    CPa[:, :, 128:132] = conv_ln_g.reshape(Ld, 4, 128).transpose(0, 2, 1)
    CPa[:, :, 132:136] = conv_ln_b.reshape(Ld, 4, 128).transpose(0, 2, 1)
    WOa = np.ascontiguousarray(w_out.reshape(Ld, 8, 128, 1024))
    GBa = np.empty((Ld, 2, 128, 1024), np.float32)
    GBa[:, 0] = post_ln_g[:, None, :]
    GBa[:, 1] = post_ln_b[:, None, :]
    return WA, WWI, CPa, WOa, GBa


def _consts():
    ident = np.eye(128, dtype=np.float32)
    t = np.arange(128)
    negmask = np.where(t[None, :] <= t[:, None], 0.0, NEG).astype(np.float32)
    cosT, sinT = _rope_tables()
    pwr = np.broadcast_to((0.5 ** np.arange(1, KBIS + 1)).astype(np.float32)[None, :], (128, KBIS)).copy()
    return {"c_ident": ident, "c_negmask": negmask, "c_cos": cosT, "c_sin": sinT, "c_pw": pwr}


_CACHE = {}
FUSED = True


def _get_prog(n_layers):
    if n_layers not in _CACHE:
        _CACHE[n_layers] = build_program(n_layers)[0]
    return _CACHE[n_layers]


def kernel(x, meta_tokens, w_in, conv_w, conv_b, conv_ln_g, conv_ln_b, w_out, post_ln_g, post_ln_b):
    x = np.asarray(x, np.float32)
    B = x.shape[0]
    f = lambda a: np.asarray(a, np.float32)
    WA, WWI, CPa, WOa, GBa = _prep_weights(f(w_in), f(conv_w), f(conv_b), f(conv_ln_g), f(conv_ln_b), f(w_out), f(post_ln_g), f(post_ln_b))
    consts = _consts()
    hs = []
    for b in range(B):
        h = np.zeros((LP, D_MODEL), np.float32)
        h[:N_META] = f(meta_tokens)
        h[N_META:N_META + SEQ] = x[b]
        hs.append(h)
    if FUSED:
        nc = _get_prog(DEPTH)
        in_maps = [dict(h0=hs[b], WA=WA, WWI=WWI, CP=CPa, WO=WOa, GB=GBa, **consts) for b in range(B)]
        res = run_bass_kernel_spmd(nc, in_maps, core_ids=list(range(B)))
        hs = [res.results[b]["hout"] for b in range(B)]
    else:
        nc = _get_prog(1)
        for l in range(DEPTH):
            in_maps = [dict(h0=hs[b], WA=WA[l:l + 1], WWI=WWI[l:l + 1], CP=CPa[l:l + 1], WO=WOa[l:l + 1], GB=GBa[l:l + 1], **consts) for b in range(B)]
            res = run_bass_kernel_spmd(nc, in_maps, core_ids=list(range(B)))
            hs = [res.results[b]["hout"] for b in range(B)]
    out = np.stack([hs[b][N_META:N_META + SEQ] for b in range(B)], axis=0)
    return np.ascontiguousarray(out, dtype=np.float32)
```

```python
import numpy as np
from contextlib import ExitStack
import concourse.bass as bass
import concourse.mybir as mybir
from concourse.bass_utils import run_bass_kernel_spmd

F32 = mybir.dt.float32
BF16 = mybir.dt.bfloat16
AF = mybir.ActivationFunctionType
ALU = mybir.AluOpType
AX = mybir.AxisListType

D_MODEL = 1024
SEQ = 4096
N_META = 16
LP = 4224
NT = LP // 128
DEPTH = 4
TOPK = 256
KBIS = 17
LN_EPS = 1e-5
ALPHA = (2.0 * DEPTH) ** 0.25
WI_SCALE = (64 ** -0.5) * (8 ** -0.5)
NEG = -1.0e30
TCH = [(i * 512, 512) for i in range(8)] + [(4096, 128)]

PJ_U, PJ_SZC, PJ_Q, PJ_K, PJ_SZA, PJ_QI, PJ_KI = 0, 4, 8, 12, 16, 20, 24
NPJ = 25
O_A, O_G, O_ZC, O_Q, O_K, O_V, O_ZA, O_QI, O_KI, O_WI = 0, 512, 1024, 1536, 2048, 2560, 3072, 3584, 4096, 4160


def _items():
    items = []
    c = 0
    for j in range(4):
        items.append(("conv", c, 2, PJ_U + j)); c += 2
    for j in range(4):
        items.append(("silu", c, 1, PJ_SZC + j)); c += 1
    for j in range(4):
        items.append(("rope", c, 2, PJ_Q + j)); c += 2
    for j in range(4):
        items.append(("rope", c, 2, PJ_K + j)); c += 2
    for j in range(4):
        items.append(("silu", c, 1, PJ_SZA + j)); c += 1
    for j in range(4):
        items.append(("rope", c, 2, PJ_QI + j)); c += 2
    items.append(("rope", c, 2, PJ_KI)); c += 2
    for hv in range(2):
        items.append(("v", c, 2, hv)); c += 2
    return items, c


ITEMS, NWCH = _items()


def _rot_cols(base, width):
    idx = np.arange(width)
    return base + (idx // 64) * 64 + ((idx % 64) + 32) % 64


def _weight_cols():
    cols = []
    for j in range(4):
        cols.append(O_A + j * 128 + np.arange(128)); cols.append(O_G + j * 128 + np.arange(128))
    for j in range(4):
        cols.append(O_ZC + j * 128 + np.arange(128))
    for base in (O_Q, O_K):
        for j in range(4):
            cols.append(base + j * 128 + np.arange(128))
            cols.append(_rot_cols(base, 512)[j * 128:(j + 1) * 128])
    za = [O_ZA + j * 128 + np.arange(128) for j in range(4)]
    qi = []
    for j in range(4):
        qi.append(O_QI + j * 128 + np.arange(128))
        qi.append(_rot_cols(O_QI, 512)[j * 128:(j + 1) * 128])
    cols = cols + za + qi
    kic = O_KI + np.arange(64)
    kir = _rot_cols(O_KI, 64)
    cols.append(np.concatenate([kic, kic])); cols.append(np.concatenate([kir, kir]))
    return cols


class Sem:
    __slots__ = ("h", "val")

    def __init__(self, h):
        self.h = h
        self.val = 0


class Tr:
    __slots__ = ("w", "r")

    def __init__(self):
        self.w = {}
        self.r = {}


class Eng:
    def __init__(self, name, e, sem, sync_self):
        self.name = name
        self.e = e
        self.sem = sem
        self.sync_self = sync_self
        self.waited = {}


class Prog:
    def __init__(self, nc, es):
        self.nc = nc
        self.es = es
        self.sems = []
        self.PE = Eng("pe", nc.tensor, self.new_sem("s_pe"), False)
        self.ACT = Eng("act", nc.scalar, self.new_sem("s_act"), True)
        self.DVE = Eng("dve", nc.vector, self.new_sem("s_dve"), True)
        self.POOL = Eng("pool", nc.gpsimd, self.new_sem("s_pool"), True)
        self.SP = Eng("sp", nc.sync, self.new_sem("s_sp"), False)
        self.engs = [self.PE, self.ACT, self.DVE, self.POOL, self.SP]
        self.nins = 0

    def new_sem(self, name):
        s = Sem(self.es.enter_context(self.nc.semaphore(name)))
        self.sems.append(s)
        return s

    def _wait(self, E, deps):
        for s, v in deps.items():
            if s is E.sem and not E.sync_self:
                continue
            if E.waited.get(s, 0) < v:
                E.e.wait_ge(s.h, v)
                E.waited[s] = v

    @staticmethod
    def _deps(reads, writes):
        deps = {}
        for t in reads:
            for s, v in t.w.items():
                if deps.get(s, 0) < v:
                    deps[s] = v
        for t in writes:
            for dd in (t.w, t.r):
                for s, v in dd.items():
                    if deps.get(s, 0) < v:
                        deps[s] = v
        return deps

    def op(self, E, ins_fn, reads=(), writes=()):
        self._wait(E, self._deps(reads, writes))
        ins = ins_fn()
        E.sem.val += 1
        ins.then_inc(E.sem.h, 1)
        v = E.sem.val
        for t in writes:
            t.w = {E.sem: v}
            t.r = {}
        for t in reads:
            t.r[E.sem] = v
        self.nins += 1

    def dma(self, Q, sem, out, in_, reads=(), writes=()):
        deps = self._deps(reads, writes)
        deps.pop(sem, None)
        self._wait(Q, deps)
        ins = Q.e.dma_start(out=out, in_=in_)
        sem.val += 16
        ins.then_inc(sem.h, 16)
        for t in writes:
            t.w = {sem: sem.val}
            t.r = {}
        for t in reads:
            t.r[sem] = sem.val
        self.nins += 1

    def barrier(self):
        allv = {s: s.val for s in self.sems if s.val > 0}
        for E in self.engs:
            for s, v in allv.items():
                if s is E.sem:
                    continue
                if E.waited.get(s, 0) < v:
                    E.e.wait_ge(s.h, v)
                    E.waited[s] = v


class Buf:
    def __init__(self, t, tr=None, sem=None):
        self.t = t
        self.tr = tr if tr is not None else Tr()
        self.sem = sem


def build_program(n_layers, dbg=False, phases="ABCD", nitems=None, ntiles=None, cstop=9):
    nc = bass.Bass("TRN2", target_bir_lowering=False)
    es = ExitStack()
    P = Prog(nc, es)
    PE, ACT, DVE, POOL, SP = P.PE, P.ACT, P.DVE, P.POOL, P.SP
    L = n_layers

    def din(name, shape, dt=F32):
        return nc.dram_tensor(name, shape, dt, kind="ExternalInput").ap()

    skind = "ExternalOutput" if dbg else "Internal"
    h0 = din("h0", [LP, D_MODEL])
    WA = din("WA", [L, NWCH, 128, 1024])
    WWI = din("WWI", [L, 128, 64])
    CP = din("CP", [L, 128, 136])
    WO = din("WO", [L, 8, 128, 1024])
    GB = din("GB", [L, 2, 128, 1024])
    c_ident = din("c_ident", [128, 128])
    c_negmask = din("c_negmask", [128, 128])
    c_cos = din("c_cos", [128, LP])
    c_sin = din("c_sin", [128, LP])
    c_pw = din("c_pw", [128, KBIS])
    hout = nc.dram_tensor("hout", [LP, D_MODEL], F32, kind="ExternalOutput").ap()
    hbufs = [nc.dram_tensor(f"hbuf{i}", [LP, D_MODEL], F32, kind="Internal").ap() for i in range(2)] if L > 1 else []
    PJ = nc.dram_tensor("PJ", [NPJ, 128, LP], BF16, kind=skind).ap()
    VA = nc.dram_tensor("VA", [NT, 128, 520], BF16, kind=skind).ap()
    WI = nc.dram_tensor("WI", [128, NT * 8], F32, kind=skind).ap()
    YC = nc.dram_tensor("YC", [8, 128, LP], BF16, kind=skind).ap()

    semcache = {}
    cur = {"l": "g"}

    def sb(stack, name, shape, dt, dma_sem=False):
        t = stack.enter_context(nc.sbuf_tensor(f"{name}_{cur['l']}", shape, dt))
        sem = None
        if dma_sem:
            if name not in semcache:
                semcache[name] = P.new_sem("d_" + name)
            sem = semcache[name]
        return Buf(t, sem=sem)

    PS = [Buf(es.enter_context(nc.psum_tensor(f"ps{i}", [128, 512], F32))) for i in range(6)]
    PQ = [Buf(es.enter_context(nc.psum_tensor(f"pq{i}", [128, 1024], BF16))) for i in range(2)]
    identf = sb(es, "identf", [128, 128], F32, True)
    identb = sb(es, "identb", [128, 128], BF16)
    negmask = sb(es, "negmask", [128, 128], F32, True)
    pw = sb(es, "pw", [128, KBIS], F32, True)
    epst = sb(es, "epst", [128, 1], F32)
    thrneg = sb(es, "thrneg", [128, 1], F32)
    onesm = sb(es, "onesm", [128, 128], F32)
    mones = sb(es, "mones", [128, 8], F32)

    P.dma(SP, identf.sem, identf.t[:], c_ident[:, :], writes=[identf.tr])
    P.dma(SP, negmask.sem, negmask.t[:], c_negmask[:, :], writes=[negmask.tr])
    P.dma(SP, pw.sem, pw.t[:], c_pw[:, :], writes=[pw.tr])
    P.op(DVE, lambda: nc.vector.tensor_copy(out=identb.t[:], in_=identf.t[:]), reads=[identf.tr], writes=[identb.tr])
    P.op(DVE, lambda: nc.vector.memset(epst.t[:], LN_EPS), writes=[epst.tr])
    P.op(DVE, lambda: nc.vector.memset(thrneg.t[:], -1.0e29), writes=[thrneg.tr])
    P.op(DVE, lambda: nc.vector.memset(onesm.t[:], 1.0 / 512.0), writes=[onesm.tr])
    P.op(DVE, lambda: nc.vector.memset(mones.t[:], -1.0), writes=[mones.tr])

    cnt = {"ps": 0}
    LB = [PS[0], PS[1], PS[2], Buf(PQ[1].t[:, :].bitcast(F32), tr=PQ[1].tr)]

    def next_ps3():
        b = PS[cnt["ps"] % 3]
        cnt["ps"] += 1
        return b

    for l in range(L):
        cur["l"] = f"L{l}"
        h_in = h0 if l == 0 else hbufs[(l - 1) % 2]
        h_out = hout if l == L - 1 else hbufs[l % 2]

        with ExitStack() as sA:
            hT = sb(sA, "hT", [128, 8, LP], BF16)
            hT_tr = [Tr() for _ in range(NT)]
            with ExitStack() as s0:
                hld = [sb(s0, f"hld{i}", [128, 1024], F32, True) for i in range(2)]
                hb = [sb(s0, f"hb{i}", [128, 1024], BF16) for i in range(2)]
                for i in range(NT):
                    s = i % 2
                    P.dma(SP, hld[s].sem, hld[s].t[:], h_in[i * 128:(i + 1) * 128, :], writes=[hld[s].tr])
                    if i % 2 == 0:
                        P.op(ACT, lambda s=s: nc.scalar.copy(out=hb[s].t[:], in_=hld[s].t[:]), reads=[hld[s].tr], writes=[hb[s].tr])
                    else:
                        P.op(DVE, lambda s=s: nc.vector.tensor_copy(out=hb[s].t[:], in_=hld[s].t[:]), reads=[hld[s].tr], writes=[hb[s].tr])
                    q = PQ[i % 2]
                    for kc in range(8):
                        P.op(PE, lambda s=s, kc=kc, q=q: nc.tensor.transpose(out=q.t[:, kc * 128:(kc + 1) * 128], in_=hb[s].t[:, kc * 128:(kc + 1) * 128], identity=identb.t[:]),
                             reads=[hb[s].tr, identb.tr], writes=[q.tr])
                    src = q.t[:, :].rearrange("p (k t) -> p k t", k=8)
                    if i % 2 == 0:
                        P.op(DVE, lambda i=i, src=src: nc.vector.tensor_copy(out=hT.t[:, :, i * 128:(i + 1) * 128], in_=src), reads=[q.tr], writes=[hT_tr[i]])
                    else:
                        P.op(ACT, lambda i=i, src=src: nc.scalar.copy(out=hT.t[:, :, i * 128:(i + 1) * 128], in_=src), reads=[q.tr], writes=[hT_tr[i]])
                P.barrier()

            with ExitStack() as s1:
                if "A" not in phases:
                    ITEMS_ = []
                else:
                    ITEMS_ = ITEMS if nitems is None else ITEMS[:nitems]
                cosT = sb(s1, "cosT", [128, LP], F32, True)
                sinT = sb(s1, "sinT", [128, LP], F32, True)
                wld = [sb(s1, f"wld{i}", [128, 2048], F32, True) for i in range(2)]
                wbf = [sb(s1, f"wbf{i}", [128, 2048], BF16) for i in range(2)]
                stage = [sb(s1, f"stage{i}", [128, LP], BF16, True) for i in range(2)]
                sig = [sb(s1, f"sig{i}", [128, 512], F32) for i in range(2)]
                tm1 = [sb(s1, f"tm1{i}", [128, 512], F32) for i in range(2)]
                tm2 = [sb(s1, f"tm2{i}", [128, 512], F32) for i in range(2)]
                vst = [sb(s1, f"vst{i}", [128, 8, 4, 65], BF16, True) for i in range(2)]
                wwif = sb(s1, "wwif", [128, 64], F32, True)
                wwib = sb(s1, "wwib", [128, 64], BF16)
                wisb = sb(s1, "wisb", [128, NT * 8], F32, True)

                P.dma(SP, cosT.sem, cosT.t[:], c_cos[:, :], writes=[cosT.tr])
                P.dma(SP, sinT.sem, sinT.t[:], c_sin[:, :], writes=[sinT.tr])
                P.dma(SP, wwif.sem, wwif.t[:], WWI[l], writes=[wwif.tr])
                P.op(POOL, lambda: nc.gpsimd.tensor_copy(out=wwib.t[:], in_=wwif.t[:]), reads=[wwif.tr], writes=[wwib.tr])
                for s in range(2):
                    P.op(POOL, lambda s=s: nc.gpsimd.memset(vst[s].t[:], 1.0), writes=[vst[s].tr])

                def load_w(it_idx):
                    kind, c0, nch, _ = ITEMS[it_idx]
                    s = it_idx % 2
                    P.dma(SP, wld[s].sem, wld[s].t[:, :nch * 1024].rearrange("p (c f) -> p c f", c=nch), WA[l, c0:c0 + nch].rearrange("c p f -> p c f"), writes=[wld[s].tr])
                    P.op(POOL, lambda s=s, nch=nch: nc.gpsimd.tensor_copy(out=wbf[s].t[:, :nch * 1024], in_=wld[s].t[:, :nch * 1024]),
                         reads=[wld[s].tr], writes=[wbf[s].tr])

                if ITEMS_:
                    load_w(0)
                gctr = 0
                st_ctr = 0
                for it_idx, (kind, c0, nch, pj) in enumerate(ITEMS_):
                    if it_idx + 1 < len(ITEMS_):
                        load_w(it_idx + 1)
                    ws = it_idx % 2
                    wv_ = wbf[ws]
                    if kind != "v":
                        stg = stage[st_ctr % 2]
                        st_ctr += 1
                        for (t0, n) in TCH:
                            bA = PS[(gctr % 2) * 2]
                            bB = PS[(gctr % 2) * 2 + 1]
                            g2 = gctr % 2
                            gctr += 1
                            hts = hT_tr[t0 // 128:(t0 + n) // 128]
                            for kc in range(8):
                                P.op(PE, lambda kc=kc, bA=bA, t0=t0, n=n: nc.tensor.matmul(bA.t[:, :n], lhsT=wv_.t[:, kc * 128:(kc + 1) * 128], rhs=hT.t[:, kc, t0:t0 + n], start=(kc == 0), stop=(kc == 7)),
                                     reads=[wv_.tr] + hts, writes=[bA.tr])
                            if nch == 2:
                                for kc in range(8):
                                    P.op(PE, lambda kc=kc, bB=bB, t0=t0, n=n: nc.tensor.matmul(bB.t[:, :n], lhsT=wv_.t[:, 1024 + kc * 128:1024 + (kc + 1) * 128], rhs=hT.t[:, kc, t0:t0 + n], start=(kc == 0), stop=(kc == 7)),
                                         reads=[wv_.tr] + hts, writes=[bB.tr])
                            if kind == "conv":
                                P.op(ACT, lambda bB=bB, g2=g2, n=n: nc.scalar.activation(out=sig[g2].t[:, :n], in_=bB.t[:, :n], func=AF.Sigmoid), reads=[bB.tr], writes=[sig[g2].tr])
                                P.op(DVE, lambda bA=bA, g2=g2, t0=t0, n=n, stg=stg: nc.vector.tensor_tensor(out=stg.t[:, t0:t0 + n], in0=bA.t[:, :n], in1=sig[g2].t[:, :n], op=ALU.mult),
                                     reads=[bA.tr, sig[g2].tr], writes=[stg.tr])
                            elif kind == "silu":
                                P.op(ACT, lambda bA=bA, t0=t0, n=n, stg=stg: nc.scalar.activation(out=stg.t[:, t0:t0 + n], in_=bA.t[:, :n], func=AF.Silu), reads=[bA.tr], writes=[stg.tr])
                            else:
                                P.op(DVE, lambda bA=bA, g2=g2, t0=t0, n=n: nc.vector.tensor_tensor(out=tm1[g2].t[:, :n], in0=bA.t[:, :n], in1=cosT.t[:, t0:t0 + n], op=ALU.mult),
                                     reads=[bA.tr, cosT.tr], writes=[tm1[g2].tr])
                                P.op(DVE, lambda bB=bB, g2=g2, t0=t0, n=n: nc.vector.tensor_tensor(out=tm2[g2].t[:, :n], in0=bB.t[:, :n], in1=sinT.t[:, t0:t0 + n], op=ALU.mult),
                                     reads=[bB.tr, sinT.tr], writes=[tm2[g2].tr])
                                P.op(POOL, lambda g2=g2, t0=t0, n=n, stg=stg: nc.gpsimd.tensor_tensor(out=stg.t[:, t0:t0 + n], in0=tm1[g2].t[:, :n], in1=tm2[g2].t[:, :n], op=ALU.add),
                                     reads=[tm1[g2].tr, tm2[g2].tr], writes=[stg.tr])
                        P.dma(POOL, stg.sem, PJ[pj], stg.t[:], reads=[stg.tr])
                    else:
                        hv = pj
                        wv3 = wv_.t[:, :].rearrange("p (k e) -> p k e", k=8)
                        for i in range(NT):
                            b = PS[(gctr % 2) * 2]
                            gctr += 1
                            gi, gs = i // 8, (i // 8) % 2
                            for kc in range(8):
                                P.op(PE, lambda kc=kc, b=b, i=i: nc.tensor.matmul(b.t[:, :256], lhsT=hT.t[:, kc, i * 128:(i + 1) * 128], rhs=wv3[:, kc, :], start=(kc == 0), stop=(kc == 7)),
                                     reads=[wv_.tr, hT_tr[i]], writes=[b.tr])
                            P.op(ACT, lambda b=b, i=i, gs=gs: nc.scalar.copy(out=vst[gs].t[:, i % 8, :, 0:64], in_=b.t[:, :256].rearrange("p (h d) -> p h d", h=4)),
                                 reads=[b.tr], writes=[vst[gs].tr])
                            if i % 8 == 7 or i == NT - 1:
                                ng = i % 8 + 1
                                i0 = gi * 8
                                dst = VA[i0:i0 + ng].rearrange("i p (v f) -> p i v f", v=2)[:, :, hv, :]
                                P.dma(POOL, vst[gs].sem, dst, vst[gs].t[:, :ng].rearrange("p i h d -> p i (h d)"), reads=[vst[gs].tr])
                            if hv == 0:
                                b5 = PS[4 + (i % 2)]
                                for kc in range(8):
                                    P.op(PE, lambda kc=kc, b5=b5, i=i: nc.tensor.matmul(b5.t[:, :8], lhsT=hT.t[:, kc, i * 128:(i + 1) * 128], rhs=wwib.t[:, kc * 8:(kc + 1) * 8], start=(kc == 0), stop=(kc == 7)),
                                         reads=[wwib.tr, hT_tr[i]], writes=[b5.tr])
                                P.op(DVE, lambda b5=b5, i=i: nc.vector.tensor_scalar(out=wisb.t[:, i * 8:(i + 1) * 8], in0=b5.t[:, :8], scalar1=WI_SCALE, scalar2=None, op0=ALU.mult),
                                     reads=[b5.tr], writes=[wisb.tr])
                P.dma(POOL, wisb.sem, WI[:, :], wisb.t[:], reads=[wisb.tr])
                P.barrier()

        with ExitStack() as sB:
            upad = [sb(sB, f"upad{j}", [128, 30 + LP], BF16, True) for j in range(4)]
            cp = sb(sB, "cp", [128, 136], F32, True)
            dg = sb(sB, "dg", [128, 4 * 31, 128], BF16)
            szc = [sb(sB, f"szc{i}", [128, 4, 512], BF16, True) for i in range(2)]
            yst = [sb(sB, f"yst{i}", [128, 4, 512], BF16, True) for i in range(2)]
            cf = [sb(sB, f"cf{j}", [128, 512], F32) for j in range(4)]
            sq = [sb(sB, f"sq{j}", [128, 512], F32) for j in range(4)]
            mean_sb = sb(sB, "mean_sb", [128, 512], F32)
            msq = sb(sB, "msq", [128, 512], F32)
            var = sb(sB, "var", [128, 512], F32)
            sd = sb(sB, "sd", [128, 512], F32)
            rstd = sb(sB, "rstd", [128, 512], F32)
            y1 = [sb(sB, f"y1{i}", [128, 512], F32) for i in range(2)]
            y2 = [sb(sB, f"y2{i}", [128, 512], F32) for i in range(2)]
            zz = [sb(sB, f"zz{i}", [128, 512], F32) for i in range(2)]

            P.dma(SP, cp.sem, cp.t[:], CP[l], writes=[cp.tr])
            for j in range(4):
                P.op(POOL, lambda j=j: nc.gpsimd.memset(upad[j].t[:, 0:30], 0.0), writes=[upad[j].tr])
                P.dma(SP, upad[j].sem, upad[j].t[:, 30:], PJ[PJ_U + j], reads=[upad[j].tr], writes=[upad[j].tr])
            for j in range(4):
                for tap in range(31):
                    P.op(DVE, lambda j=j, tap=tap: nc.vector.tensor_scalar(out=dg.t[:, j * 31 + tap, :], in0=identb.t[:], scalar1=cp.t[:, j * 31 + tap:j * 31 + tap + 1], scalar2=None, op0=ALU.mult),
                         reads=[identb.tr, cp.tr], writes=[dg.tr])
            for ci, (t0, n) in enumerate(TCH if "B" in phases else []):
                s = ci % 2
                P.dma(SP, szc[s].sem, szc[s].t[:, :, :n], PJ[PJ_SZC:PJ_SZC + 4, :, t0:t0 + n].rearrange("j p t -> p j t"), writes=[szc[s].tr])
                for j in range(4):
                    for tap in range(31):
                        P.op(PE, lambda j=j, tap=tap, t0=t0, n=n: nc.tensor.matmul(PS[j].t[:, :n], lhsT=dg.t[:, j * 31 + tap, :], rhs=upad[j].t[:, t0 + tap:t0 + tap + n], start=(tap == 0), stop=(tap == 30)),
                             reads=[dg.tr, upad[j].tr], writes=[PS[j].tr])
                for j in range(4):
                    P.op(ACT, lambda j=j, n=n: nc.scalar.activation(out=cf[j].t[:, :n], in_=PS[j].t[:, :n], func=AF.Identity, bias=cp.t[:, 124 + j:125 + j]), reads=[PS[j].tr, cp.tr], writes=[cf[j].tr])
                    P.op(ACT, lambda j=j, n=n: nc.scalar.activation(out=sq[j].t[:, :n], in_=PS[j].t[:, :n], func=AF.Square, bias=cp.t[:, 124 + j:125 + j]), reads=[PS[j].tr, cp.tr], writes=[sq[j].tr])
                for j in range(4):
                    P.op(PE, lambda j=j, n=n: nc.tensor.matmul(PS[4].t[:, :n], lhsT=onesm.t[:], rhs=cf[j].t[:, :n], start=(j == 0), stop=(j == 3)), reads=[onesm.tr, cf[j].tr], writes=[PS[4].tr])
                for j in range(4):
                    P.op(PE, lambda j=j, n=n: nc.tensor.matmul(PS[5].t[:, :n], lhsT=onesm.t[:], rhs=sq[j].t[:, :n], start=(j == 0), stop=(j == 3)), reads=[onesm.tr, sq[j].tr], writes=[PS[5].tr])
                P.op(ACT, lambda n=n: nc.scalar.copy(out=mean_sb.t[:, :n], in_=PS[4].t[:, :n]), reads=[PS[4].tr], writes=[mean_sb.tr])
                P.op(ACT, lambda n=n: nc.scalar.activation(out=msq.t[:, :n], in_=PS[4].t[:, :n], func=AF.Square), reads=[PS[4].tr], writes=[msq.tr])
                P.op(DVE, lambda n=n: nc.vector.tensor_tensor(out=var.t[:, :n], in0=PS[5].t[:, :n], in1=msq.t[:, :n], op=ALU.subtract), reads=[PS[5].tr, msq.tr], writes=[var.tr])
                P.op(ACT, lambda n=n: nc.scalar.activation(out=sd.t[:, :n], in_=var.t[:, :n], func=AF.Sqrt, bias=epst.t[:, 0:1]), reads=[var.tr, epst.tr], writes=[sd.tr])
                P.op(DVE, lambda n=n: nc.vector.reciprocal(out=rstd.t[:, :n], in_=sd.t[:, :n]), reads=[sd.tr], writes=[rstd.tr])
                for j in range(4):
                    k2 = j % 2
                    P.op(DVE, lambda j=j, n=n, k2=k2: nc.vector.tensor_tensor(out=y1[k2].t[:, :n], in0=cf[j].t[:, :n], in1=mean_sb.t[:, :n], op=ALU.subtract), reads=[cf[j].tr, mean_sb.tr], writes=[y1[k2].tr])
                    P.op(POOL, lambda n=n, k2=k2: nc.gpsimd.tensor_tensor(out=y2[k2].t[:, :n], in0=y1[k2].t[:, :n], in1=rstd.t[:, :n], op=ALU.mult), reads=[y1[k2].tr, rstd.tr], writes=[y2[k2].tr])
                    P.op(ACT, lambda j=j, n=n, k2=k2: nc.scalar.activation(out=zz[k2].t[:, :n], in_=y2[k2].t[:, :n], func=AF.Silu, scale=cp.t[:, 128 + j:129 + j], bias=cp.t[:, 132 + j:133 + j]),
                         reads=[y2[k2].tr, cp.tr], writes=[zz[k2].tr])
                    P.op(DVE, lambda j=j, n=n, k2=k2, s=s: nc.vector.tensor_tensor(out=yst[s].t[:, j, :n], in0=zz[k2].t[:, :n], in1=szc[s].t[:, j, :n], op=ALU.mult), reads=[zz[k2].tr, szc[s].tr], writes=[yst[s].tr])
                P.dma(POOL, yst[s].sem, YC[0:4, :, t0:t0 + n].rearrange("j p t -> p j t"), yst[s].t[:, :, :n], reads=[yst[s].tr])
            P.barrier()

        with ExitStack() as sC:
            kT = sb(sC, "kT", [128, 4, LP], BF16, True)
            kiT = sb(sC, "kiT", [128, LP], BF16, True)
            vaug = sb(sC, "vaug", [128, NT, 520], BF16, True)
            wi_all = sb(sC, "wi_all", [128, NT * 8], F32, True)
            qt = [sb(sC, f"qt{i}", [128, 4, 128], BF16, True) for i in range(2)]
            qit = [sb(sC, f"qit{i}", [128, 4, 128], BF16, True) for i in range(2)]
            szat = [sb(sC, f"szat{i}", [128, 4, 128], BF16, True) for i in range(2)]
            dgw = [sb(sC, f"dgw{i}", [128, 8, 128], BF16) for i in range(2)]
            score = [sb(sC, f"score{i}", [128, LP], F32) for i in range(2)]
            score_tr = [[Tr() for _ in range(9)] for _ in range(2)]
            junk = sb(sC, "junk", [128, LP], mybir.dt.uint8)
            mask = sb(sC, "mask", [128, LP], BF16)
            maskT = [sb(sC, f"maskT{i}", [128, LP], BF16) for i in range(2)]
            maskT_tr = [[Tr() for _ in range(5)] for _ in range(2)]
            Rr = [sb(sC, f"Rr{i}", [128, 512], BF16) for i in range(8)]
            Eb = [sb(sC, f"Eb{i}", [128, 512], BF16) for i in range(4)]
            PTb = [sb(sC, f"PTb{i}", [128, 512], BF16) for i in range(6)]
            hi = sb(sC, "hi", [128, 1], F32)
            lo = sb(sC, "lo", [128, 1], F32)
            w0 = sb(sC, "w0", [128, 1], F32)
            Wk = sb(sC, "Wk", [128, KBIS], F32)
            mid = [sb(sC, f"mid{i}", [128, 1], F32) for i in range(2)]
            cntb = sb(sC, "cntb", [128, 1], F32)
            dd = sb(sC, "dd", [128, 1], F32)
            rinv = sb(sC, "rinv", [128, 8], F32)
            rinv_tr = [Tr(), Tr()]
            rsum = sb(sC, "rsum", [128, 8], F32)
            rsum_tr = [Tr(), Tr()]
            osb = sb(sC, "osb", [128, 520], F32)
            osb_tr = [Tr(), Tr()]
            ya = sb(sC, "ya", [128, 512], BF16)
            ya_tr = [Tr() for _ in range(8)]
            yaT = sb(sC, "yaT", [128, 4, 128], BF16)
            yat = [sb(sC, f"yat{i}", [128, 4, 128], BF16, True) for i in range(2)]

            for j in range(4):
                P.dma(SP, kT.sem, kT.t[:, j, :], PJ[PJ_K + j], writes=[kT.tr])
            P.dma(SP, kiT.sem, kiT.t[:], PJ[PJ_KI], writes=[kiT.tr])
            for i0 in range(0, NT, 11):
                P.dma(SP, vaug.sem, vaug.t[:, i0:i0 + 11, :], VA[i0:i0 + 11].rearrange("i p f -> p i f"), writes=[vaug.tr])
            P.dma(SP, wi_all.sem, wi_all.t[:], WI[:, :], writes=[wi_all.tr])

            ctr = {"r": 0, "e": 0, "m": 0}
            NTC = (NT if ntiles is None else ntiles) if "C" in phases else 0

            def load_idx(i):
                s = i % 2
                c0, c1 = i * 128, (i + 1) * 128
                P.dma(SP, qit[s].sem, qit[s].t[:], PJ[PJ_QI:PJ_QI + 4, :, c0:c1].rearrange("j p t -> p j t"), writes=[qit[s].tr])

            def load_att(i):
                s = i % 2
                c0, c1 = i * 128, (i + 1) * 128
                P.dma(SP, qt[s].sem, qt[s].t[:], PJ[PJ_Q:PJ_Q + 4, :, c0:c1].rearrange("j p t -> p j t"), writes=[qt[s].tr])
                P.dma(SP, szat[s].sem, szat[s].t[:], PJ[PJ_SZA:PJ_SZA + 4, :, c0:c1].rearrange("j p t -> p j t"), writes=[szat[s].tr])

            def S1a(i):
                s = i % 2
                N = 128 * (i + 1)
                sc, sctr = score[s], score_tr[s]
                for h in range(8):
                    P.op(POOL, lambda h=h: nc.gpsimd.tensor_scalar(out=dgw[s].t[:, h, :], in0=identb.t[:], scalar1=wi_all.t[:, i * 8 + h:i * 8 + h + 1], scalar2=1.0, op0=ALU.mult, op1=ALU.mult),
                         reads=[identb.tr, wi_all.tr], writes=[dgw[s].tr])
                chunks = [(s0, min(512, N - s0)) for s0 in range(0, N, 512)]
                pendD = []

                def flush_diag():
                    c_, grp_, rbs_, s0_, n_ = pendD.pop(0)
                    for (h, rb) in rbs_:
                        P.op(PE, lambda: nc.tensor.matmul(PS[3].t[:, :n_], lhsT=dgw[s].t[:, h, :], rhs=rb.t[:, :n_], start=(h == 0), stop=(h == 7)),
                             reads=[dgw[s].tr, rb.tr], writes=[PS[3].tr])
                    if grp_ == 1:
                        P.op(ACT, lambda: nc.scalar.copy(out=sc.t[:, s0_:s0_ + n_], in_=PS[3].t[:, :n_]), reads=[PS[3].tr], writes=[sctr[c_]])

                for c, (s0, n) in enumerate(chunks):
                    for grp in range(2):
                        rbs = []
                        for hh in range(4):
                            h = grp * 4 + hh
                            Lb = LB[hh]
                            po = (h % 2) * 64
                            P.op(PE, lambda: nc.tensor.matmul(Lb.t[:, :n], lhsT=qit[s].t[po:po + 64, h // 2, :], rhs=kiT.t[po:po + 64, s0:s0 + n], start=True, stop=True),
                                 reads=[qit[s].tr, kiT.tr], writes=[Lb.tr])
                            rb = Rr[ctr["r"] % 8]
                            ctr["r"] += 1
                            P.op(ACT, lambda: nc.scalar.activation(out=rb.t[:, :n], in_=Lb.t[:, :n], func=AF.Relu), reads=[Lb.tr], writes=[rb.tr])
                            rbs.append((h, rb))
                        if pendD:
                            flush_diag()
                        pendD.append((c, grp, rbs, s0, n))
                while pendD:
                    flush_diag()
                nsc = len(chunks)
                P.op(POOL, lambda: nc.gpsimd.tensor_tensor(out=sc.t[:, N - 128:N], in0=sc.t[:, N - 128:N], in1=negmask.t[:], op=ALU.add),
                     reads=[sctr[nsc - 1], negmask.tr], writes=[sctr[nsc - 1]])

            def S1b(i):
                s = i % 2
                N = 128 * (i + 1)
                sc = score[s]
                sc_trs = score_tr[s][:(N + 511) // 512]
                if cstop < 2:
                    return
                if i < 2:
                    thr = thrneg
                else:
                    P.op(DVE, lambda: nc.vector.tensor_reduce(out=hi.t[:], in_=sc.t[:, :N], axis=AX.X, op=ALU.max), reads=sc_trs, writes=[hi.tr])
                    P.op(DVE, lambda: nc.vector.tensor_reduce(out=lo.t[:], in_=sc.t[:, :N - 128], axis=AX.X, op=ALU.min), reads=sc_trs, writes=[lo.tr])
                    P.op(DVE, lambda: nc.vector.tensor_tensor(out=w0.t[:], in0=hi.t[:], in1=lo.t[:], op=ALU.subtract), reads=[hi.tr, lo.tr], writes=[w0.tr])
                    P.op(DVE, lambda: nc.vector.tensor_scalar(out=Wk.t[:], in0=pw.t[:], scalar1=w0.t[:, 0:1], scalar2=None, op0=ALU.mult), reads=[pw.tr, w0.tr], writes=[Wk.tr])
                    P.op(DVE, lambda: nc.vector.tensor_tensor(out=mid[0].t[:], in0=lo.t[:], in1=Wk.t[:, 0:1], op=ALU.add), reads=[lo.tr, Wk.tr], writes=[mid[0].tr])
                    for k in range(KBIS):
                        mc, mn = mid[k % 2], mid[(k + 1) % 2]
                        kb = k + 1 if k < KBIS - 1 else k
                        P.op(DVE, lambda: nc.vector.tensor_scalar(out=junk.t[:, :N], in0=sc.t[:, :N], scalar1=mc.t[:, 0:1], scalar2=None, op0=ALU.is_ge, op1=ALU.add, accum_out=cntb.t[:, 0:1]),
                             reads=sc_trs + [mc.tr], writes=[junk.tr, cntb.tr])
                        P.op(DVE, lambda: nc.vector.tensor_scalar(out=dd.t[:], in0=cntb.t[:], scalar1=TOPK - 0.5, scalar2=Wk.t[:, k:k + 1], op0=ALU.is_ge, op1=ALU.mult),
                             reads=[cntb.tr, Wk.tr], writes=[dd.tr])
                        P.op(DVE, lambda: nc.vector.scalar_tensor_tensor(out=mn.t[:], in0=dd.t[:], scalar=Wk.t[:, kb:kb + 1], in1=mc.t[:], op0=ALU.subtract, op1=ALU.add),
                             reads=[dd.tr, Wk.tr, mc.tr], writes=[mn.tr])
                    thr = mid[KBIS % 2]
                P.op(DVE, lambda: nc.vector.tensor_scalar(out=mask.t[:, :N], in0=sc.t[:, :N], scalar1=thr.t[:, 0:1], scalar2=None, op0=ALU.is_ge),
                     reads=sc_trs + [thr.tr], writes=[mask.tr])

            def S1c(i):
                if cstop < 3:
                    return
                s = i % 2
                nb = i + 1
                for g in range((nb + 7) // 8):
                    q = PQ[0]
                    m = min(8, nb - g * 8)
                    for bi in range(m):
                        b = g * 8 + bi
                        P.op(PE, lambda: nc.tensor.transpose(out=q.t[:, bi * 128:(bi + 1) * 128], in_=mask.t[:, b * 128:(b + 1) * 128], identity=identb.t[:]),
                             reads=[mask.tr, identb.tr], writes=[q.tr])
                    P.op(ACT, lambda: nc.scalar.copy(out=maskT[s].t[:, g * 1024:g * 1024 + m * 128], in_=q.t[:, :m * 128]), reads=[q.tr], writes=[maskT_tr[s][g]])

            def S2(i):
                if cstop < 4:
                    return
                s = i % 2
                nb = i + 1
                c0, c1 = i * 128, (i + 1) * 128
                mT, mTtr = maskT[s], maskT_tr[s]
                units = [(hp, b0, min(4, nb - b0)) for hp in range(4) for b0 in range(0, nb, 4)]
                pend = []

                def pv(hp, b0, m, ptbs):
                    for hi_, ptb in enumerate(ptbs):
                        h = 2 * hp + hi_
                        Ob = PS[4 + hi_]
                        hh = hp
                        for bi in range(m):
                            b = b0 + bi
                            P.op(PE, lambda: nc.tensor.matmul(Ob.t[:, hh * 65:(hh + 1) * 65], lhsT=ptb.t[:, bi * 128:(bi + 1) * 128], rhs=vaug.t[:, b, h * 65:(h + 1) * 65], start=(b == 0), stop=(b == nb - 1)),
                                 reads=[ptb.tr, vaug.tr], writes=[Ob.tr])

                for (hp, b0, m) in units:
                    Sbs = [LB[ctr["m"] % 4], LB[(ctr["m"] + 1) % 4]]
                    ctr["m"] += 2
                    for bi in range(m):
                        b = b0 + bi
                        for hi_ in range(2):
                            po = hi_ * 64
                            Sb = Sbs[hi_]
                            P.op(PE, lambda: nc.tensor.matmul(Sb.t[:, bi * 128:(bi + 1) * 128], lhsT=kT.t[po:po + 64, hp, b * 128:(b + 1) * 128], rhs=qt[s].t[po:po + 64, hp, :], start=True, stop=True),
                                 reads=[kT.tr, qt[s].tr], writes=[Sb.tr])
                    mtrs = list({id(mTtr[b // 8]): mTtr[b // 8] for b in range(b0, b0 + m)}.values())
                    ptbs = []
                    for hi_ in range(2):
                        Sb = Sbs[hi_]
                        eb = Eb[ctr["e"] % 4]
                        ptb = PTb[ctr["e"] % 6]
                        ctr["e"] += 1
                        P.op(ACT, lambda: nc.scalar.activation(out=eb.t[:, :m * 128], in_=Sb.t[:, :m * 128], func=AF.Exp, scale=0.125), reads=[Sb.tr], writes=[eb.tr])
                        P.op(POOL, lambda: nc.gpsimd.tensor_tensor(out=ptb.t[:, :m * 128], in0=eb.t[:, :m * 128], in1=mT.t[:, b0 * 128:(b0 + m) * 128], op=ALU.mult),
                             reads=[eb.tr] + mtrs, writes=[ptb.tr])
                        ptbs.append(ptb)
                    pend.append((hp, b0, m, ptbs))
                    if len(pend) > 1:
                        pv(*pend.pop(0))
                while pend:
                    pv(*pend.pop(0))
                if cstop < 5:
                    return
                for half in range(2):
                    Ob = PS[4 + half]
                    P.op(ACT, lambda: nc.scalar.copy(out=osb.t[:, half * 260:(half + 1) * 260], in_=Ob.t[:, :260]), reads=[Ob.tr], writes=[osb_tr[half]])
                    P.op(POOL, lambda: nc.gpsimd.tensor_copy(out=rsum.t[:, half * 4:(half + 1) * 4], in_=osb.t[:, half * 260:(half + 1) * 260].rearrange("p (h d) -> p h d", h=4)[:, :, 64]),
                         reads=[osb_tr[half]], writes=[rsum_tr[half]])
                    P.op(POOL, lambda: nc.gpsimd.tensor_tensor(out=rinv.t[:, half * 4:(half + 1) * 4], in0=rsum.t[:, half * 4:(half + 1) * 4], in1=mones.t[:, 0:4], op=ALU.pow),
                         reads=[rsum_tr[half], mones.tr], writes=[rinv_tr[half]])
                for h in range(8):
                    oc = (h % 2) * 260 + (h // 2) * 65
                    ri = (h % 2) * 4 + h // 2
                    P.op(ACT, lambda: nc.scalar.activation(out=ya.t[:, h * 64:(h + 1) * 64], in_=osb.t[:, oc:oc + 64], func=AF.Identity, scale=rinv.t[:, ri:ri + 1]),
                         reads=[osb_tr[h % 2], rinv_tr[h % 2]], writes=[ya_tr[h]])
                if cstop < 6:
                    return
                q = PQ[0]
                for j in range(4):
                    P.op(PE, lambda: nc.tensor.transpose(out=q.t[:, j * 128:(j + 1) * 128], in_=ya.t[:, j * 128:(j + 1) * 128], identity=identb.t[:]),
                         reads=[ya_tr[2 * j], ya_tr[2 * j + 1], identb.tr], writes=[q.tr])
                P.op(ACT, lambda: nc.scalar.copy(out=yaT.t[:], in_=q.t[:, :512].rearrange("p (j t) -> p j t", j=4)), reads=[q.tr], writes=[yaT.tr])
                P.op(POOL, lambda: nc.gpsimd.tensor_tensor(out=yat[s].t[:], in0=yaT.t[:], in1=szat[s].t[:], op=ALU.mult),
                     reads=[yaT.tr, szat[s].tr], writes=[yat[s].tr])
                if cstop < 7:
                    return
                P.dma(POOL, yat[s].sem, YC[4:8, :, c0:c1].rearrange("j p t -> p j t"), yat[s].t[:], reads=[yat[s].tr])

            if NTC > 0:
                load_idx(0)
                if NTC > 1:
                    load_idx(1)
                S1a(0)
            for i in range(NTC):
                if i + 2 < NTC:
                    load_idx(i + 2)
                load_att(i)
                if i + 1 < NTC:
                    S1a(i + 1)
                if i >= 1:
                    S2(i - 1)
                S1b(i)
                S1c(i)
            if NTC > 0:
                S2(NTC - 1)
            P.barrier()

        with ExitStack() as sD:
            yc = sb(sD, "yc", [128, 8, LP], BF16, True)
            wol = [sb(sD, f"wol{i}", [128, 1024], F32, True) for i in range(2)]
            wob = sb(sD, "wob", [128, 8, 1024], BF16)
            wob_tr = [Tr() for _ in range(8)]
            grep = sb(sD, "grep", [128, 1024], F32, True)
            brep = sb(sD, "brep", [128, 1024], F32, True)
            htl = [sb(sD, f"htl{i}", [128, 1024], F32, True) for i in range(3)]
            zt = [sb(sD, f"zt{i}", [128, 1024], F32) for i in range(3)]
            zn = [sb(sD, f"zn{i}", [128, 1024], F32) for i in range(3)]
            o1 = [sb(sD, f"o1{i}", [128, 1024], F32) for i in range(3)]
            ho = [sb(sD, f"ho{i}", [128, 1024], F32, True) for i in range(3)]
            st6 = [sb(sD, f"st6{i}", [128, 12], F32) for i in range(3)]
            mv = [sb(sD, f"mv{i}", [128, 2], F32) for i in range(3)]
            sdv = [sb(sD, f"sdv{i}", [128, 1], F32) for i in range(3)]
            rs = [sb(sD, f"rs{i}", [128, 1], F32) for i in range(3)]
            nmr = [sb(sD, f"nmr{i}", [128, 1], F32) for i in range(3)]

            for ec in range(8):
                P.dma(SP, yc.sem, yc.t[:, ec, :], YC[ec], writes=[yc.tr])
            P.dma(SP, grep.sem, grep.t[:], GB[l, 0], writes=[grep.tr])
            P.dma(SP, brep.sem, brep.t[:], GB[l, 1], writes=[brep.tr])
            for ec in range(8):
                s = ec % 2
                P.dma(SP, wol[s].sem, wol[s].t[:], WO[l, ec], writes=[wol[s].tr])
                P.op(POOL, lambda s=s, ec=ec: nc.gpsimd.tensor_copy(out=wob.t[:, ec, :], in_=wol[s].t[:]), reads=[wol[s].tr], writes=[wob_tr[ec]])
            for i in range(NT if "D" in phases else 0):
                s = i % 3
                P.dma(SP, htl[s].sem, htl[s].t[:], h_in[i * 128:(i + 1) * 128, :], writes=[htl[s].tr])
                for half in range(2):
                    b = PS[(i % 3) * 2 + half]
                    for ec in range(8):
                        P.op(PE, lambda b=b, ec=ec, half=half, i=i: nc.tensor.matmul(b.t[:, :], lhsT=yc.t[:, ec, i * 128:(i + 1) * 128], rhs=wob.t[:, ec, half * 512:(half + 1) * 512], start=(ec == 0), stop=(ec == 7)),
                             reads=[yc.tr, wob_tr[ec]], writes=[b.tr])
                    P.op(DVE, lambda b=b, half=half, s=s: nc.vector.scalar_tensor_tensor(out=zt[s].t[:, half * 512:(half + 1) * 512], in0=htl[s].t[:, half * 512:(half + 1) * 512], scalar=ALPHA, in1=b.t[:, :], op0=ALU.mult, op1=ALU.add),
                         reads=[htl[s].tr, b.tr], writes=[zt[s].tr])
                for half in range(2):
                    P.op(DVE, lambda half=half, s=s: nc.vector.bn_stats(out=st6[s].t[:, half * 6:(half + 1) * 6], in_=zt[s].t[:, half * 512:(half + 1) * 512]), reads=[zt[s].tr], writes=[st6[s].tr])
                P.op(DVE, lambda s=s: nc.vector.bn_aggr(out=mv[s].t[:], in_=st6[s].t[:]), reads=[st6[s].tr], writes=[mv[s].tr])
                P.op(ACT, lambda s=s: nc.scalar.activation(out=sdv[s].t[:], in_=mv[s].t[:, 1:2], func=AF.Sqrt, bias=epst.t[:, 0:1]), reads=[mv[s].tr, epst.tr], writes=[sdv[s].tr])
                P.op(DVE, lambda s=s: nc.vector.reciprocal(out=rs[s].t[:], in_=sdv[s].t[:]), reads=[sdv[s].tr], writes=[rs[s].tr])
                P.op(DVE, lambda s=s: nc.vector.scalar_tensor_tensor(out=nmr[s].t[:], in0=mv[s].t[:, 0:1], scalar=-1.0, in1=rs[s].t[:], op0=ALU.mult, op1=ALU.mult), reads=[mv[s].tr, rs[s].tr], writes=[nmr[s].tr])
                P.op(ACT, lambda s=s: nc.scalar.activation(out=zn[s].t[:], in_=zt[s].t[:], func=AF.Identity, scale=rs[s].t[:, 0:1], bias=nmr[s].t[:, 0:1]), reads=[zt[s].tr, rs[s].tr, nmr[s].tr], writes=[zn[s].tr])
                P.op(POOL, lambda s=s: nc.gpsimd.tensor_tensor(out=o1[s].t[:], in0=zn[s].t[:], in1=grep.t[:], op=ALU.mult), reads=[zn[s].tr, grep.tr], writes=[o1[s].tr])
                P.op(POOL, lambda s=s: nc.gpsimd.tensor_tensor(out=ho[s].t[:], in0=o1[s].t[:], in1=brep.t[:], op=ALU.add), reads=[o1[s].tr, brep.tr], writes=[ho[s].tr])
                P.dma(POOL, ho[s].sem, h_out[i * 128:(i + 1) * 128, :], ho[s].t[:], reads=[ho[s].tr])
            P.barrier()

    es.close()
    return nc, P.nins


def _rope_tables():
    inv_freq = (10000.0 ** (-np.arange(0, 64, 2, dtype=np.float32) / np.float32(64))).astype(np.float32)
    ang = np.arange(LP, dtype=np.float32)[:, None] * inv_freq[None, :]
    cos = np.cos(ang).astype(np.float32).T
    sin = np.sin(ang).astype(np.float32).T
    p = np.arange(128)
    d = p % 64
    cosT = cos[d % 32]
    sinT = np.where((d < 32)[:, None], -sin[d % 32], sin[d % 32])
    return np.ascontiguousarray(cosT, dtype=np.float32), np.ascontiguousarray(sinT, dtype=np.float32)


def _prep_weights(w_in, conv_w, conv_b, conv_ln_g, conv_ln_b, w_out, post_ln_g, post_ln_b):
    Ld = w_in.shape[0]
    cols = _weight_cols()
    WA = np.empty((Ld, NWCH, 128, 1024), np.float32)
    for l in range(Ld):
        for c, cc in enumerate(cols):
            blk = w_in[l][:, cc]
            WA[l, c] = blk.reshape(8, 128, 128).transpose(1, 0, 2).reshape(128, 1024)
        for hv in range(2):
            blk = w_in[l][:, O_V + hv * 256:O_V + (hv + 1) * 256]
            arr = blk.reshape(8, 128, 256).transpose(1, 0, 2).reshape(128, 2048)
            WA[l, len(cols) + 2 * hv] = arr[:, :1024]
            WA[l, len(cols) + 2 * hv + 1] = arr[:, 1024:]
    WWI = np.ascontiguousarray(w_in[:, :, O_WI:O_WI + 8].reshape(Ld, 8, 128, 8).transpose(0, 2, 1, 3).reshape(Ld, 128, 64))
    CPa = np.empty((Ld, 128, 136), np.float32)
    CPa[:, :, 0:124] = conv_w.reshape(Ld, 31, 4, 128).transpose(0, 3, 2, 1).reshape(Ld, 128, 124)
    CPa[:, :, 124:128] = conv_b.reshape(Ld, 4, 128).transpose(0, 2, 1)
    CPa[:, :, 128:132] = conv_ln_g.reshape(Ld, 4, 128).transpose(0, 2, 1)
    CPa[:, :, 132:136] = conv_ln_b.reshape(Ld, 4, 128).transpose(0, 2, 1)
    WOa = np.ascontiguousarray(w_out.reshape(Ld, 8, 128, 1024))
    GBa = np.empty((Ld, 2, 128, 1024), np.float32)
    GBa[:, 0] = post_ln_g[:, None, :]
    GBa[:, 1] = post_ln_b[:, None, :]
    return WA, WWI, CPa, WOa, GBa


def _consts():
    ident = np.eye(128, dtype=np.float32)
    t = np.arange(128)
    negmask = np.where(t[None, :] <= t[:, None], 0.0, NEG).astype(np.float32)
    cosT, sinT = _rope_tables()
    pwr = np.broadcast_to((0.5 ** np.arange(1, KBIS + 1)).astype(np.float32)[None, :], (128, KBIS)).copy()
    return {"c_ident": ident, "c_negmask": negmask, "c_cos": cosT, "c_sin": sinT, "c_pw": pwr}


_CACHE = {}
FUSED = True


def _get_prog(n_layers):
    if n_layers not in _CACHE:
        _CACHE[n_layers] = build_program(n_layers)[0]
    return _CACHE[n_layers]


def kernel(x, meta_tokens, w_in, conv_w, conv_b, conv_ln_g, conv_ln_b, w_out, post_ln_g, post_ln_b):
    x = np.asarray(x, np.float32)
    B = x.shape[0]
    f = lambda a: np.asarray(a, np.float32)
    WA, WWI, CPa, WOa, GBa = _prep_weights(f(w_in), f(conv_w), f(conv_b), f(conv_ln_g), f(conv_ln_b), f(w_out), f(post_ln_g), f(post_ln_b))
    consts = _consts()
    hs = []
    for b in range(B):
        h = np.zeros((LP, D_MODEL), np.float32)
        h[:N_META] = f(meta_tokens)
        h[N_META:N_META + SEQ] = x[b]
        hs.append(h)
    if FUSED:
        nc = _get_prog(DEPTH)
        in_maps = [dict(h0=hs[b], WA=WA, WWI=WWI, CP=CPa, WO=WOa, GB=GBa, **consts) for b in range(B)]
        res = run_bass_kernel_spmd(nc, in_maps, core_ids=list(range(B)))
        hs = [res.results[b]["hout"] for b in range(B)]
    else:
        nc = _get_prog(1)
        for l in range(DEPTH):
            in_maps = [dict(h0=hs[b], WA=WA[l:l + 1], WWI=WWI[l:l + 1], CP=CPa[l:l + 1], WO=WOa[l:l + 1], GB=GBa[l:l + 1], **consts) for b in range(B)]
            res = run_bass_kernel_spmd(nc, in_maps, core_ids=list(range(B)))
            hs = [res.results[b]["hout"] for b in range(B)]
    out = np.stack([hs[b][N_META:N_META + SEQ] for b in range(B)], axis=0)
    return np.ascontiguousarray(out, dtype=np.float32)
```

```python
import numpy as np
from contextlib import ExitStack
import concourse.bass as bass
import concourse.mybir as mybir
from concourse.bass_utils import run_bass_kernel_spmd

F32 = mybir.dt.float32
BF16 = mybir.dt.bfloat16
AF = mybir.ActivationFunctionType
ALU = mybir.AluOpType
AX = mybir.AxisListType

D_MODEL = 1024
SEQ = 4096
N_META = 16
LP = 4224
NT = LP // 128
DEPTH = 4
TOPK = 256
KBIS = 17
LN_EPS = 1e-5
ALPHA = (2.0 * DEPTH) ** 0.25
WI_SCALE = (64 ** -0.5) * (8 ** -0.5)
NEG = -1.0e30
TCH = [(i * 512, 512) for i in range(8)] + [(4096, 128)]

PJ_U, PJ_SZC, PJ_Q, PJ_K, PJ_SZA, PJ_QI, PJ_KI = 0, 4, 8, 12, 16, 20, 24
NPJ = 25
O_A, O_G, O_ZC, O_Q, O_K, O_V, O_ZA, O_QI, O_KI, O_WI = 0, 512, 1024, 1536, 2048, 2560, 3072, 3584, 4096, 4160


def _items():
    items = []
    c = 0
    for j in range(4):
        items.append(("conv", c, 2, PJ_U + j)); c += 2
    for j in range(4):
        items.append(("silu", c, 1, PJ_SZC + j)); c += 1
    for j in range(4):
        items.append(("rope", c, 2, PJ_Q + j)); c += 2
    for j in range(4):
        items.append(("rope", c, 2, PJ_K + j)); c += 2
    for j in range(4):
        items.append(("silu", c, 1, PJ_SZA + j)); c += 1
    for j in range(4):
        items.append(("rope", c, 2, PJ_QI + j)); c += 2
    items.append(("rope", c, 2, PJ_KI)); c += 2
    for hv in range(2):
        items.append(("v", c, 2, hv)); c += 2
    return items, c


ITEMS, NWCH = _items()


def _rot_cols(base, width):
    idx = np.arange(width)
    return base + (idx // 64) * 64 + ((idx % 64) + 32) % 64


def _weight_cols():
    cols = []
    for j in range(4):
        cols.append(O_A + j * 128 + np.arange(128)); cols.append(O_G + j * 128 + np.arange(128))
    for j in range(4):
        cols.append(O_ZC + j * 128 + np.arange(128))
    for base in (O_Q, O_K):
        for j in range(4):
            cols.append(base + j * 128 + np.arange(128))
            cols.append(_rot_cols(base, 512)[j * 128:(j + 1) * 128])
    za = [O_ZA + j * 128 + np.arange(128) for j in range(4)]
    qi = []
    for j in range(4):
        qi.append(O_QI + j * 128 + np.arange(128))
        qi.append(_rot_cols(O_QI, 512)[j * 128:(j + 1) * 128])
    cols = cols + za + qi
    kic = O_KI + np.arange(64)
    kir = _rot_cols(O_KI, 64)
    cols.append(np.concatenate([kic, kic])); cols.append(np.concatenate([kir, kir]))
    return cols


class Sem:
    __slots__ = ("h", "val")

    def __init__(self, h):
        self.h = h
        self.val = 0


class Tr:
    __slots__ = ("w", "r")

    def __init__(self):
        self.w = {}
        self.r = {}


class Eng:
    def __init__(self, name, e, sem, sync_self):
        self.name = name
        self.e = e
        self.sem = sem
        self.sync_self = sync_self
        self.waited = {}


class Prog:
    def __init__(self, nc, es):
        self.nc = nc
        self.es = es
        self.sems = []
        self.PE = Eng("pe", nc.tensor, self.new_sem("s_pe"), False)
        self.ACT = Eng("act", nc.scalar, self.new_sem("s_act"), True)
        self.DVE = Eng("dve", nc.vector, self.new_sem("s_dve"), True)
        self.POOL = Eng("pool", nc.gpsimd, self.new_sem("s_pool"), True)
        self.SP = Eng("sp", nc.sync, self.new_sem("s_sp"), False)
        self.engs = [self.PE, self.ACT, self.DVE, self.POOL, self.SP]
        self.nins = 0

    def new_sem(self, name):
        s = Sem(self.es.enter_context(self.nc.semaphore(name)))
        self.sems.append(s)
        return s

    def _wait(self, E, deps):
        for s, v in deps.items():
            if s is E.sem and not E.sync_self:
                continue
            if E.waited.get(s, 0) < v:
                E.e.wait_ge(s.h, v)
                E.waited[s] = v

    @staticmethod
    def _deps(reads, writes):
        deps = {}
        for t in reads:
            for s, v in t.w.items():
                if deps.get(s, 0) < v:
                    deps[s] = v
        for t in writes:
            for dd in (t.w, t.r):
                for s, v in dd.items():
                    if deps.get(s, 0) < v:
                        deps[s] = v
        return deps

    def op(self, E, ins_fn, reads=(), writes=()):
        self._wait(E, self._deps(reads, writes))
        ins = ins_fn()
        E.sem.val += 1
        ins.then_inc(E.sem.h, 1)
        v = E.sem.val
        for t in writes:
            t.w = {E.sem: v}
            t.r = {}
        for t in reads:
            t.r[E.sem] = v
        self.nins += 1

    def dma(self, Q, sem, out, in_, reads=(), writes=()):
        deps = self._deps(reads, writes)
        deps.pop(sem, None)
        self._wait(Q, deps)
        ins = Q.e.dma_start(out=out, in_=in_)
        sem.val += 16
        ins.then_inc(sem.h, 16)
        for t in writes:
            t.w = {sem: sem.val}
            t.r = {}
        for t in reads:
            t.r[sem] = sem.val
        self.nins += 1

    def barrier(self):
        allv = {s: s.val for s in self.sems if s.val > 0}
        for E in self.engs:
            for s, v in allv.items():
                if s is E.sem:
                    continue
                if E.waited.get(s, 0) < v:
                    E.e.wait_ge(s.h, v)
                    E.waited[s] = v


class Buf:
    def __init__(self, t, tr=None, sem=None):
        self.t = t
        self.tr = tr if tr is not None else Tr()
        self.sem = sem


def build_program(n_layers, dbg=False, phases="ABCD", nitems=None, ntiles=None, cstop=9):
    nc = bass.Bass("TRN2", target_bir_lowering=False)
    es = ExitStack()
    P = Prog(nc, es)
    PE, ACT, DVE, POOL, SP = P.PE, P.ACT, P.DVE, P.POOL, P.SP
    L = n_layers

    def din(name, shape, dt=F32):
        return nc.dram_tensor(name, shape, dt, kind="ExternalInput").ap()

    skind = "ExternalOutput" if dbg else "Internal"
    h0 = din("h0", [LP, D_MODEL])
    WA = din("WA", [L, NWCH, 128, 1024])
    WWI = din("WWI", [L, 128, 64])
    CP = din("CP", [L, 128, 136])
    WO = din("WO", [L, 8, 128, 1024])
    GB = din("GB", [L, 2, 128, 1024])
    c_ident = din("c_ident", [128, 128])
    c_negmask = din("c_negmask", [128, 128])
    c_cos = din("c_cos", [128, LP])
    c_sin = din("c_sin", [128, LP])
    c_pw = din("c_pw", [128, KBIS])
    hout = nc.dram_tensor("hout", [LP, D_MODEL], F32, kind="ExternalOutput").ap()
    hbufs = [nc.dram_tensor(f"hbuf{i}", [LP, D_MODEL], F32, kind="Internal").ap() for i in range(2)] if L > 1 else []
    PJ = nc.dram_tensor("PJ", [NPJ, 128, LP], BF16, kind=skind).ap()
    VA = nc.dram_tensor("VA", [NT, 128, 520], BF16, kind=skind).ap()
    WI = nc.dram_tensor("WI", [128, NT * 8], F32, kind=skind).ap()
    YC = nc.dram_tensor("YC", [8, 128, LP], BF16, kind=skind).ap()

    semcache = {}
    cur = {"l": "g"}

    def sb(stack, name, shape, dt, dma_sem=False):
        t = stack.enter_context(nc.sbuf_tensor(f"{name}_{cur['l']}", shape, dt))
        sem = None
        if dma_sem:
            if name not in semcache:
                semcache[name] = P.new_sem("d_" + name)
            sem = semcache[name]
        return Buf(t, sem=sem)

    PS = [Buf(es.enter_context(nc.psum_tensor(f"ps{i}", [128, 512], F32))) for i in range(6)]
    PQ = [Buf(es.enter_context(nc.psum_tensor(f"pq{i}", [128, 1024], BF16))) for i in range(2)]
    identf = sb(es, "identf", [128, 128], F32, True)
    identb = sb(es, "identb", [128, 128], BF16)
    negmask = sb(es, "negmask", [128, 128], F32, True)
    pw = sb(es, "pw", [128, KBIS], F32, True)
    epst = sb(es, "epst", [128, 1], F32)
    thrneg = sb(es, "thrneg", [128, 1], F32)
    onesm = sb(es, "onesm", [128, 128], F32)
    mones = sb(es, "mones", [128, 8], F32)

    P.dma(SP, identf.sem, identf.t[:], c_ident[:, :], writes=[identf.tr])
    P.dma(SP, negmask.sem, negmask.t[:], c_negmask[:, :], writes=[negmask.tr])
    P.dma(SP, pw.sem, pw.t[:], c_pw[:, :], writes=[pw.tr])
    P.op(DVE, lambda: nc.vector.tensor_copy(out=identb.t[:], in_=identf.t[:]), reads=[identf.tr], writes=[identb.tr])
    P.op(DVE, lambda: nc.vector.memset(epst.t[:], LN_EPS), writes=[epst.tr])
    P.op(DVE, lambda: nc.vector.memset(thrneg.t[:], -1.0e29), writes=[thrneg.tr])
    P.op(DVE, lambda: nc.vector.memset(onesm.t[:], 1.0 / 512.0), writes=[onesm.tr])
    P.op(DVE, lambda: nc.vector.memset(mones.t[:], -1.0), writes=[mones.tr])

    cnt = {"ps": 0}
    LB = [PS[0], PS[1], PS[2], Buf(PQ[1].t[:, :].bitcast(F32), tr=PQ[1].tr)]

    def next_ps3():
        b = PS[cnt["ps"] % 3]
        cnt["ps"] += 1
        return b

    for l in range(L):
        cur["l"] = f"L{l}"
        h_in = h0 if l == 0 else hbufs[(l - 1) % 2]
        h_out = hout if l == L - 1 else hbufs[l % 2]

        with ExitStack() as sA:
            hT = sb(sA, "hT", [128, 8, LP], BF16)
            hT_tr = [Tr() for _ in range(NT)]
            with ExitStack() as s0:
                hld = [sb(s0, f"hld{i}", [128, 1024], F32, True) for i in range(2)]
                hb = [sb(s0, f"hb{i}", [128, 1024], BF16) for i in range(2)]
                for i in range(NT):
                    s = i % 2
                    P.dma(SP, hld[s].sem, hld[s].t[:], h_in[i * 128:(i + 1) * 128, :], writes=[hld[s].tr])
                    if i % 2 == 0:
                        P.op(ACT, lambda s=s: nc.scalar.copy(out=hb[s].t[:], in_=hld[s].t[:]), reads=[hld[s].tr], writes=[hb[s].tr])
                    else:
                        P.op(DVE, lambda s=s: nc.vector.tensor_copy(out=hb[s].t[:], in_=hld[s].t[:]), reads=[hld[s].tr], writes=[hb[s].tr])
                    q = PQ[i % 2]
                    for kc in range(8):
                        P.op(PE, lambda s=s, kc=kc, q=q: nc.tensor.transpose(out=q.t[:, kc * 128:(kc + 1) * 128], in_=hb[s].t[:, kc * 128:(kc + 1) * 128], identity=identb.t[:]),
                             reads=[hb[s].tr, identb.tr], writes=[q.tr])
                    src = q.t[:, :].rearrange("p (k t) -> p k t", k=8)
                    if i % 2 == 0:
                        P.op(DVE, lambda i=i, src=src: nc.vector.tensor_copy(out=hT.t[:, :, i * 128:(i + 1) * 128], in_=src), reads=[q.tr], writes=[hT_tr[i]])
                    else:
                        P.op(ACT, lambda i=i, src=src: nc.scalar.copy(out=hT.t[:, :, i * 128:(i + 1) * 128], in_=src), reads=[q.tr], writes=[hT_tr[i]])
                P.barrier()

            with ExitStack() as s1:
                if "A" not in phases:
                    ITEMS_ = []
                else:
                    ITEMS_ = ITEMS if nitems is None else ITEMS[:nitems]
                cosT = sb(s1, "cosT", [128, LP], F32, True)
                sinT = sb(s1, "sinT", [128, LP], F32, True)
                wld = [sb(s1, f"wld{i}", [128, 2048], F32, True) for i in range(2)]
                wbf = [sb(s1, f"wbf{i}", [128, 2048], BF16) for i in range(2)]
                stage = [sb(s1, f"stage{i}", [128, LP], BF16, True) for i in range(2)]
                sig = [sb(s1, f"sig{i}", [128, 512], F32) for i in range(2)]
                tm1 = [sb(s1, f"tm1{i}", [128, 512], F32) for i in range(2)]
                tm2 = [sb(s1, f"tm2{i}", [128, 512], F32) for i in range(2)]
                vst = [sb(s1, f"vst{i}", [128, 8, 4, 65], BF16, True) for i in range(2)]
                wwif = sb(s1, "wwif", [128, 64], F32, True)
                wwib = sb(s1, "wwib", [128, 64], BF16)
                wisb = sb(s1, "wisb", [128, NT * 8], F32, True)

                P.dma(SP, cosT.sem, cosT.t[:], c_cos[:, :], writes=[cosT.tr])
                P.dma(SP, sinT.sem, sinT.t[:], c_sin[:, :], writes=[sinT.tr])
                P.dma(SP, wwif.sem, wwif.t[:], WWI[l], writes=[wwif.tr])
                P.op(POOL, lambda: nc.gpsimd.tensor_copy(out=wwib.t[:], in_=wwif.t[:]), reads=[wwif.tr], writes=[wwib.tr])
                for s in range(2):
                    P.op(POOL, lambda s=s: nc.gpsimd.memset(vst[s].t[:], 1.0), writes=[vst[s].tr])

                def load_w(it_idx):
                    kind, c0, nch, _ = ITEMS[it_idx]
                    s = it_idx % 2
                    P.dma(SP, wld[s].sem, wld[s].t[:, :nch * 1024].rearrange("p (c f) -> p c f", c=nch), WA[l, c0:c0 + nch].rearrange("c p f -> p c f"), writes=[wld[s].tr])
                    P.op(POOL, lambda s=s, nch=nch: nc.gpsimd.tensor_copy(out=wbf[s].t[:, :nch * 1024], in_=wld[s].t[:, :nch * 1024]),
                         reads=[wld[s].tr], writes=[wbf[s].tr])

                if ITEMS_:
                    load_w(0)
                gctr = 0
                st_ctr = 0
                for it_idx, (kind, c0, nch, pj) in enumerate(ITEMS_):
                    if it_idx + 1 < len(ITEMS_):
                        load_w(it_idx + 1)
                    ws = it_idx % 2
                    wv_ = wbf[ws]
                    if kind != "v":
                        stg = stage[st_ctr % 2]
                        st_ctr += 1
                        for (t0, n) in TCH:
                            bA = PS[(gctr % 2) * 2]
                            bB = PS[(gctr % 2) * 2 + 1]
                            g2 = gctr % 2
                            gctr += 1
                            hts = hT_tr[t0 // 128:(t0 + n) // 128]
                            for kc in range(8):
                                P.op(PE, lambda kc=kc, bA=bA, t0=t0, n=n: nc.tensor.matmul(bA.t[:, :n], lhsT=wv_.t[:, kc * 128:(kc + 1) * 128], rhs=hT.t[:, kc, t0:t0 + n], start=(kc == 0), stop=(kc == 7)),
                                     reads=[wv_.tr] + hts, writes=[bA.tr])
                            if nch == 2:
                                for kc in range(8):
                                    P.op(PE, lambda kc=kc, bB=bB, t0=t0, n=n: nc.tensor.matmul(bB.t[:, :n], lhsT=wv_.t[:, 1024 + kc * 128:1024 + (kc + 1) * 128], rhs=hT.t[:, kc, t0:t0 + n], start=(kc == 0), stop=(kc == 7)),
                                         reads=[wv_.tr] + hts, writes=[bB.tr])
                            if kind == "conv":
                                P.op(ACT, lambda bB=bB, g2=g2, n=n: nc.scalar.activation(out=sig[g2].t[:, :n], in_=bB.t[:, :n], func=AF.Sigmoid), reads=[bB.tr], writes=[sig[g2].tr])
                                P.op(DVE, lambda bA=bA, g2=g2, t0=t0, n=n, stg=stg: nc.vector.tensor_tensor(out=stg.t[:, t0:t0 + n], in0=bA.t[:, :n], in1=sig[g2].t[:, :n], op=ALU.mult),
                                     reads=[bA.tr, sig[g2].tr], writes=[stg.tr])
                            elif kind == "silu":
                                P.op(ACT, lambda bA=bA, t0=t0, n=n, stg=stg: nc.scalar.activation(out=stg.t[:, t0:t0 + n], in_=bA.t[:, :n], func=AF.Silu), reads=[bA.tr], writes=[stg.tr])
                            else:
                                P.op(DVE, lambda bA=bA, g2=g2, t0=t0, n=n: nc.vector.tensor_tensor(out=tm1[g2].t[:, :n], in0=bA.t[:, :n], in1=cosT.t[:, t0:t0 + n], op=ALU.mult),
                                     reads=[bA.tr, cosT.tr], writes=[tm1[g2].tr])
                                P.op(DVE, lambda bB=bB, g2=g2, t0=t0, n=n: nc.vector.tensor_tensor(out=tm2[g2].t[:, :n], in0=bB.t[:, :n], in1=sinT.t[:, t0:t0 + n], op=ALU.mult),
                                     reads=[bB.tr, sinT.tr], writes=[tm2[g2].tr])
                                P.op(POOL, lambda g2=g2, t0=t0, n=n, stg=stg: nc.gpsimd.tensor_tensor(out=stg.t[:, t0:t0 + n], in0=tm1[g2].t[:, :n], in1=tm2[g2].t[:, :n], op=ALU.add),
                                     reads=[tm1[g2].tr, tm2[g2].tr], writes=[stg.tr])
                        P.dma(POOL, stg.sem, PJ[pj], stg.t[:], reads=[stg.tr])
                    else:
                        hv = pj
                        wv3 = wv_.t[:, :].rearrange("p (k e) -> p k e", k=8)
                        for i in range(NT):
                            b = PS[(gctr % 2) * 2]
                            gctr += 1
                            gi, gs = i // 8, (i // 8) % 2
                            for kc in range(8):
                                P.op(PE, lambda kc=kc, b=b, i=i: nc.tensor.matmul(b.t[:, :256], lhsT=hT.t[:, kc, i * 128:(i + 1) * 128], rhs=wv3[:, kc, :], start=(kc == 0), stop=(kc == 7)),
                                     reads=[wv_.tr, hT_tr[i]], writes=[b.tr])
                            P.op(ACT, lambda b=b, i=i, gs=gs: nc.scalar.copy(out=vst[gs].t[:, i % 8, :, 0:64], in_=b.t[:, :256].rearrange("p (h d) -> p h d", h=4)),
                                 reads=[b.tr], writes=[vst[gs].tr])
                            if i % 8 == 7 or i == NT - 1:
                                ng = i % 8 + 1
                                i0 = gi * 8
                                dst = VA[i0:i0 + ng].rearrange("i p (v f) -> p i v f", v=2)[:, :, hv, :]
                                P.dma(POOL, vst[gs].sem, dst, vst[gs].t[:, :ng].rearrange("p i h d -> p i (h d)"), reads=[vst[gs].tr])
                            if hv == 0:
                                b5 = PS[4 + (i % 2)]
                                for kc in range(8):
                                    P.op(PE, lambda kc=kc, b5=b5, i=i: nc.tensor.matmul(b5.t[:, :8], lhsT=hT.t[:, kc, i * 128:(i + 1) * 128], rhs=wwib.t[:, kc * 8:(kc + 1) * 8], start=(kc == 0), stop=(kc == 7)),
                                         reads=[wwib.tr, hT_tr[i]], writes=[b5.tr])
                                P.op(DVE, lambda b5=b5, i=i: nc.vector.tensor_scalar(out=wisb.t[:, i * 8:(i + 1) * 8], in0=b5.t[:, :8], scalar1=WI_SCALE, scalar2=None, op0=ALU.mult),
                                     reads=[b5.tr], writes=[wisb.tr])
                P.dma(POOL, wisb.sem, WI[:, :], wisb.t[:], reads=[wisb.tr])
                P.barrier()

        with ExitStack() as sB:
            upad = [sb(sB, f"upad{j}", [128, 30 + LP], BF16, True) for j in range(4)]
            cp = sb(sB, "cp", [128, 136], F32, True)
            dg = sb(sB, "dg", [128, 4 * 31, 128], BF16)
            szc = [sb(sB, f"szc{i}", [128, 4, 512], BF16, True) for i in range(2)]
            yst = [sb(sB, f"yst{i}", [128, 4, 512], BF16, True) for i in range(2)]
            cf = [sb(sB, f"cf{j}", [128, 512], F32) for j in range(4)]
            sq = [sb(sB, f"sq{j}", [128, 512], F32) for j in range(4)]
            mean_sb = sb(sB, "mean_sb", [128, 512], F32)
            msq = sb(sB, "msq", [128, 512], F32)
            var = sb(sB, "var", [128, 512], F32)
            sd = sb(sB, "sd", [128, 512], F32)
            rstd = sb(sB, "rstd", [128, 512], F32)
            y1 = [sb(sB, f"y1{i}", [128, 512], F32) for i in range(2)]
            y2 = [sb(sB, f"y2{i}", [128, 512], F32) for i in range(2)]
            zz = [sb(sB, f"zz{i}", [128, 512], F32) for i in range(2)]

            P.dma(SP, cp.sem, cp.t[:], CP[l], writes=[cp.tr])
            for j in range(4):
                P.op(POOL, lambda j=j: nc.gpsimd.memset(upad[j].t[:, 0:30], 0.0), writes=[upad[j].tr])
                P.dma(SP, upad[j].sem, upad[j].t[:, 30:], PJ[PJ_U + j], reads=[upad[j].tr], writes=[upad[j].tr])
            for j in range(4):
                for tap in range(31):
                    P.op(DVE, lambda j=j, tap=tap: nc.vector.tensor_scalar(out=dg.t[:, j * 31 + tap, :], in0=identb.t[:], scalar1=cp.t[:, j * 31 + tap:j * 31 + tap + 1], scalar2=None, op0=ALU.mult),
                         reads=[identb.tr, cp.tr], writes=[dg.tr])
            for ci, (t0, n) in enumerate(TCH if "B" in phases else []):
                s = ci % 2
                P.dma(SP, szc[s].sem, szc[s].t[:, :, :n], PJ[PJ_SZC:PJ_SZC + 4, :, t0:t0 + n].rearrange("j p t -> p j t"), writes=[szc[s].tr])
                for j in range(4):
                    for tap in range(31):
                        P.op(PE, lambda j=j, tap=tap, t0=t0, n=n: nc.tensor.matmul(PS[j].t[:, :n], lhsT=dg.t[:, j * 31 + tap, :], rhs=upad[j].t[:, t0 + tap:t0 + tap + n], start=(tap == 0), stop=(tap == 30)),
                             reads=[dg.tr, upad[j].tr], writes=[PS[j].tr])
                for j in range(4):
                    P.op(ACT, lambda j=j, n=n: nc.scalar.activation(out=cf[j].t[:, :n], in_=PS[j].t[:, :n], func=AF.Identity, bias=cp.t[:, 124 + j:125 + j]), reads=[PS[j].tr, cp.tr], writes=[cf[j].tr])
                    P.op(ACT, lambda j=j, n=n: nc.scalar.activation(out=sq[j].t[:, :n], in_=PS[j].t[:, :n], func=AF.Square, bias=cp.t[:, 124 + j:125 + j]), reads=[PS[j].tr, cp.tr], writes=[sq[j].tr])
                for j in range(4):
                    P.op(PE, lambda j=j, n=n: nc.tensor.matmul(PS[4].t[:, :n], lhsT=onesm.t[:], rhs=cf[j].t[:, :n], start=(j == 0), stop=(j == 3)), reads=[onesm.tr, cf[j].tr], writes=[PS[4].tr])
                for j in range(4):
                    P.op(PE, lambda j=j, n=n: nc.tensor.matmul(PS[5].t[:, :n], lhsT=onesm.t[:], rhs=sq[j].t[:, :n], start=(j == 0), stop=(j == 3)), reads=[onesm.tr, sq[j].tr], writes=[PS[5].tr])
                P.op(ACT, lambda n=n: nc.scalar.copy(out=mean_sb.t[:, :n], in_=PS[4].t[:, :n]), reads=[PS[4].tr], writes=[mean_sb.tr])
                P.op(ACT, lambda n=n: nc.scalar.activation(out=msq.t[:, :n], in_=PS[4].t[:, :n], func=AF.Square), reads=[PS[4].tr], writes=[msq.tr])
                P.op(DVE, lambda n=n: nc.vector.tensor_tensor(out=var.t[:, :n], in0=PS[5].t[:, :n], in1=msq.t[:, :n], op=ALU.subtract), reads=[PS[5].tr, msq.tr], writes=[var.tr])
                P.op(ACT, lambda n=n: nc.scalar.activation(out=sd.t[:, :n], in_=var.t[:, :n], func=AF.Sqrt, bias=epst.t[:, 0:1]), reads=[var.tr, epst.tr], writes=[sd.tr])
                P.op(DVE, lambda n=n: nc.vector.reciprocal(out=rstd.t[:, :n], in_=sd.t[:, :n]), reads=[sd.tr], writes=[rstd.tr])
                for j in range(4):
                    k2 = j % 2
                    P.op(DVE, lambda j=j, n=n, k2=k2: nc.vector.tensor_tensor(out=y1[k2].t[:, :n], in0=cf[j].t[:, :n], in1=mean_sb.t[:, :n], op=ALU.subtract), reads=[cf[j].tr, mean_sb.tr], writes=[y1[k2].tr])
                    P.op(POOL, lambda n=n, k2=k2: nc.gpsimd.tensor_tensor(out=y2[k2].t[:, :n], in0=y1[k2].t[:, :n], in1=rstd.t[:, :n], op=ALU.mult), reads=[y1[k2].tr, rstd.tr], writes=[y2[k2].tr])
                    P.op(ACT, lambda j=j, n=n, k2=k2: nc.scalar.activation(out=zz[k2].t[:, :n], in_=y2[k2].t[:, :n], func=AF.Silu, scale=cp.t[:, 128 + j:129 + j], bias=cp.t[:, 132 + j:133 + j]),
                         reads=[y2[k2].tr, cp.tr], writes=[zz[k2].tr])
                    P.op(DVE, lambda j=j, n=n, k2=k2, s=s: nc.vector.tensor_tensor(out=yst[s].t[:, j, :n], in0=zz[k2].t[:, :n], in1=szc[s].t[:, j, :n], op=ALU.mult), reads=[zz[k2].tr, szc[s].tr], writes=[yst[s].tr])
                P.dma(POOL, yst[s].sem, YC[0:4, :, t0:t0 + n].rearrange("j p t -> p j t"), yst[s].t[:, :, :n], reads=[yst[s].tr])
            P.barrier()

        with ExitStack() as sC:
            kT = sb(sC, "kT", [128, 4, LP], BF16, True)
            kiT = sb(sC, "kiT", [128, LP], BF16, True)
            vaug = sb(sC, "vaug", [128, NT, 520], BF16, True)
            wi_all = sb(sC, "wi_all", [128, NT * 8], F32, True)
            qt = [sb(sC, f"qt{i}", [128, 4, 128], BF16, True) for i in range(2)]
            qit = [sb(sC, f"qit{i}", [128, 4, 128], BF16, True) for i in range(2)]
            szat = [sb(sC, f"szat{i}", [128, 4, 128], BF16, True) for i in range(2)]
            dgw = [sb(sC, f"dgw{i}", [128, 8, 128], BF16) for i in range(2)]
            score = [sb(sC, f"score{i}", [128, LP], F32) for i in range(2)]
            score_tr = [[Tr() for _ in range(9)] for _ in range(2)]
            junk = sb(sC, "junk", [128, LP], mybir.dt.uint8)
            mask = sb(sC, "mask", [128, LP], BF16)
            maskT = [sb(sC, f"maskT{i}", [128, LP], BF16) for i in range(2)]
            maskT_tr = [[Tr() for _ in range(5)] for _ in range(2)]
            Rr = [sb(sC, f"Rr{i}", [128, 512], BF16) for i in range(8)]
            Eb = [sb(sC, f"Eb{i}", [128, 512], BF16) for i in range(4)]
            PTb = [sb(sC, f"PTb{i}", [128, 512], BF16) for i in range(6)]
            hi = sb(sC, "hi", [128, 1], F32)
            lo = sb(sC, "lo", [128, 1], F32)
            w0 = sb(sC, "w0", [128, 1], F32)
            Wk = sb(sC, "Wk", [128, KBIS], F32)
            mid = [sb(sC, f"mid{i}", [128, 1], F32) for i in range(2)]
            cntb = sb(sC, "cntb", [128, 1], F32)
            dd = sb(sC, "dd", [128, 1], F32)
            rinv = sb(sC, "rinv", [128, 8], F32)
            rinv_tr = [Tr(), Tr()]
            rsum = sb(sC, "rsum", [128, 8], F32)
            rsum_tr = [Tr(), Tr()]
            osb = sb(sC, "osb", [128, 520], F32)
            osb_tr = [Tr(), Tr()]
            ya = sb(sC, "ya", [128, 512], BF16)
            ya_tr = [Tr() for _ in range(8)]
            yaT = sb(sC, "yaT", [128, 4, 128], BF16)
            yat = [sb(sC, f"yat{i}", [128, 4, 128], BF16, True) for i in range(2)]

            for j in range(4):
                P.dma(SP, kT.sem, kT.t[:, j, :], PJ[PJ_K + j], writes=[kT.tr])
            P.dma(SP, kiT.sem, kiT.t[:], PJ[PJ_KI], writes=[kiT.tr])
            for i0 in range(0, NT, 11):
                P.dma(SP, vaug.sem, vaug.t[:, i0:i0 + 11, :], VA[i0:i0 + 11].rearrange("i p f -> p i f"), writes=[vaug.tr])
            P.dma(SP, wi_all.sem, wi_all.t[:], WI[:, :], writes=[wi_all.tr])

            ctr = {"r": 0, "e": 0, "m": 0}
            NTC = (NT if ntiles is None else ntiles) if "C" in phases else 0

            def load_idx(i):
                s = i % 2
                c0, c1 = i * 128, (i + 1) * 128
                P.dma(SP, qit[s].sem, qit[s].t[:], PJ[PJ_QI:PJ_QI + 4, :, c0:c1].rearrange("j p t -> p j t"), writes=[qit[s].tr])

            def load_att(i):
                s = i % 2
                c0, c1 = i * 128, (i + 1) * 128
                P.dma(SP, qt[s].sem, qt[s].t[:], PJ[PJ_Q:PJ_Q + 4, :, c0:c1].rearrange("j p t -> p j t"), writes=[qt[s].tr])
                P.dma(SP, szat[s].sem, szat[s].t[:], PJ[PJ_SZA:PJ_SZA + 4, :, c0:c1].rearrange("j p t -> p j t"), writes=[szat[s].tr])

            def S1a(i):
                s = i % 2
                N = 128 * (i + 1)
                sc, sctr = score[s], score_tr[s]
                for h in range(8):
                    P.op(POOL, lambda h=h: nc.gpsimd.tensor_scalar(out=dgw[s].t[:, h, :], in0=identb.t[:], scalar1=wi_all.t[:, i * 8 + h:i * 8 + h + 1], scalar2=1.0, op0=ALU.mult, op1=ALU.mult),
                         reads=[identb.tr, wi_all.tr], writes=[dgw[s].tr])
                chunks = [(s0, min(512, N - s0)) for s0 in range(0, N, 512)]
                pendD = []

                def flush_diag():
                    c_, grp_, rbs_, s0_, n_ = pendD.pop(0)
                    for (h, rb) in rbs_:
                        P.op(PE, lambda: nc.tensor.matmul(PS[3].t[:, :n_], lhsT=dgw[s].t[:, h, :], rhs=rb.t[:, :n_], start=(h == 0), stop=(h == 7)),
                             reads=[dgw[s].tr, rb.tr], writes=[PS[3].tr])
                    if grp_ == 1:
                        P.op(ACT, lambda: nc.scalar.copy(out=sc.t[:, s0_:s0_ + n_], in_=PS[3].t[:, :n_]), reads=[PS[3].tr], writes=[sctr[c_]])

                for c, (s0, n) in enumerate(chunks):
                    for grp in range(2):
                        rbs = []
                        for hh in range(4):
                            h = grp * 4 + hh
                            Lb = LB[hh]
                            po = (h % 2) * 64
                            P.op(PE, lambda: nc.tensor.matmul(Lb.t[:, :n], lhsT=qit[s].t[po:po + 64, h // 2, :], rhs=kiT.t[po:po + 64, s0:s0 + n], start=True, stop=True),
                                 reads=[qit[s].tr, kiT.tr], writes=[Lb.tr])
                            rb = Rr[ctr["r"] % 8]
                            ctr["r"] += 1
                            P.op(ACT, lambda: nc.scalar.activation(out=rb.t[:, :n], in_=Lb.t[:, :n], func=AF.Relu), reads=[Lb.tr], writes=[rb.tr])
                            rbs.append((h, rb))
                        if pendD:
                            flush_diag()
                        pendD.append((c, grp, rbs, s0, n))
                while pendD:
                    flush_diag()
                nsc = len(chunks)
                P.op(POOL, lambda: nc.gpsimd.tensor_tensor(out=sc.t[:, N - 128:N], in0=sc.t[:, N - 128:N], in1=negmask.t[:], op=ALU.add),
                     reads=[sctr[nsc - 1], negmask.tr], writes=[sctr[nsc - 1]])

            def S1b(i):
                s = i % 2
                N = 128 * (i + 1)
                sc = score[s]
                sc_trs = score_tr[s][:(N + 511) // 512]
                if cstop < 2:
                    return
                if i < 2:
                    thr = thrneg
                else:
                    P.op(DVE, lambda: nc.vector.tensor_reduce(out=hi.t[:], in_=sc.t[:, :N], axis=AX.X, op=ALU.max), reads=sc_trs, writes=[hi.tr])
                    P.op(DVE, lambda: nc.vector.tensor_reduce(out=lo.t[:], in_=sc.t[:, :N - 128], axis=AX.X, op=ALU.min), reads=sc_trs, writes=[lo.tr])
                    P.op(DVE, lambda: nc.vector.tensor_tensor(out=w0.t[:], in0=hi.t[:], in1=lo.t[:], op=ALU.subtract), reads=[hi.tr, lo.tr], writes=[w0.tr])
                    P.op(DVE, lambda: nc.vector.tensor_scalar(out=Wk.t[:], in0=pw.t[:], scalar1=w0.t[:, 0:1], scalar2=None, op0=ALU.mult), reads=[pw.tr, w0.tr], writes=[Wk.tr])
                    P.op(DVE, lambda: nc.vector.tensor_tensor(out=mid[0].t[:], in0=lo.t[:], in1=Wk.t[:, 0:1], op=ALU.add), reads=[lo.tr, Wk.tr], writes=[mid[0].tr])
                    KB = max(12, KBIS - int(np.floor(np.log2(LP / N))))
                    for k in range(KB):
                        mc, mn = mid[k % 2], mid[(k + 1) % 2]
                        kb = k + 1 if k < KB - 1 else k
                        P.op(DVE, lambda: nc.vector.tensor_scalar(out=junk.t[:, :N], in0=sc.t[:, :N], scalar1=mc.t[:, 0:1], scalar2=None, op0=ALU.is_ge, op1=ALU.add, accum_out=cntb.t[:, 0:1]),
                             reads=sc_trs + [mc.tr], writes=[junk.tr, cntb.tr])
                        P.op(DVE, lambda: nc.vector.tensor_scalar(out=dd.t[:], in0=cntb.t[:], scalar1=TOPK - 0.5, scalar2=Wk.t[:, k:k + 1], op0=ALU.is_ge, op1=ALU.mult),
                             reads=[cntb.tr, Wk.tr], writes=[dd.tr])
                        P.op(DVE, lambda: nc.vector.scalar_tensor_tensor(out=mn.t[:], in0=dd.t[:], scalar=Wk.t[:, kb:kb + 1], in1=mc.t[:], op0=ALU.subtract, op1=ALU.add),
                             reads=[dd.tr, Wk.tr, mc.tr], writes=[mn.tr])
                    thr = mid[KB % 2]
                P.op(DVE, lambda: nc.vector.tensor_scalar(out=mask.t[:, :N], in0=sc.t[:, :N], scalar1=thr.t[:, 0:1], scalar2=None, op0=ALU.is_ge),
                     reads=sc_trs + [thr.tr], writes=[mask.tr])

            def S1c(i):
                if cstop < 3:
                    return
                s = i % 2
                nb = i + 1
                for g in range((nb + 7) // 8):
                    q = PQ[0]
                    m = min(8, nb - g * 8)
                    for bi in range(m):
                        b = g * 8 + bi
                        P.op(PE, lambda: nc.tensor.transpose(out=q.t[:, bi * 128:(bi + 1) * 128], in_=mask.t[:, b * 128:(b + 1) * 128], identity=identb.t[:]),
                             reads=[mask.tr, identb.tr], writes=[q.tr])
                    P.op(ACT, lambda: nc.scalar.copy(out=maskT[s].t[:, g * 1024:g * 1024 + m * 128], in_=q.t[:, :m * 128]), reads=[q.tr], writes=[maskT_tr[s][g]])

            def S2(i):
                if cstop < 4:
                    return
                s = i % 2
                nb = i + 1
                c0, c1 = i * 128, (i + 1) * 128
                mT, mTtr = maskT[s], maskT_tr[s]
                units = [(hp, b0, min(4, nb - b0)) for hp in range(4) for b0 in range(0, nb, 4)]
                pend = []

                def pv(hp, b0, m, ptbs):
                    for hi_, ptb in enumerate(ptbs):
                        h = 2 * hp + hi_
                        Ob = PS[4 + hi_]
                        hh = hp
                        for bi in range(m):
                            b = b0 + bi
                            P.op(PE, lambda: nc.tensor.matmul(Ob.t[:, hh * 65:(hh + 1) * 65], lhsT=ptb.t[:, bi * 128:(bi + 1) * 128], rhs=vaug.t[:, b, h * 65:(h + 1) * 65], start=(b == 0), stop=(b == nb - 1)),
                                 reads=[ptb.tr, vaug.tr], writes=[Ob.tr])

                for (hp, b0, m) in units:
                    Sbs = [LB[ctr["m"] % 4], LB[(ctr["m"] + 1) % 4]]
                    ctr["m"] += 2
                    for bi in range(m):
                        b = b0 + bi
                        for hi_ in range(2):
                            po = hi_ * 64
                            Sb = Sbs[hi_]
                            P.op(PE, lambda: nc.tensor.matmul(Sb.t[:, bi * 128:(bi + 1) * 128], lhsT=kT.t[po:po + 64, hp, b * 128:(b + 1) * 128], rhs=qt[s].t[po:po + 64, hp, :], start=True, stop=True),
                                 reads=[kT.tr, qt[s].tr], writes=[Sb.tr])
                    mtrs = list({id(mTtr[b // 8]): mTtr[b // 8] for b in range(b0, b0 + m)}.values())
                    ptbs = []
                    for hi_ in range(2):
                        Sb = Sbs[hi_]
                        eb = Eb[ctr["e"] % 4]
                        ptb = PTb[ctr["e"] % 6]
                        ctr["e"] += 1
                        P.op(ACT, lambda: nc.scalar.activation(out=eb.t[:, :m * 128], in_=Sb.t[:, :m * 128], func=AF.Exp, scale=0.125), reads=[Sb.tr], writes=[eb.tr])
                        P.op(POOL, lambda: nc.gpsimd.tensor_tensor(out=ptb.t[:, :m * 128], in0=eb.t[:, :m * 128], in1=mT.t[:, b0 * 128:(b0 + m) * 128], op=ALU.mult),
                             reads=[eb.tr] + mtrs, writes=[ptb.tr])
                        ptbs.append(ptb)
                    pend.append((hp, b0, m, ptbs))
                    if len(pend) > 1:
                        pv(*pend.pop(0))
                while pend:
                    pv(*pend.pop(0))
                if cstop < 5:
                    return
                for half in range(2):
                    Ob = PS[4 + half]
                    P.op(ACT, lambda: nc.scalar.copy(out=osb.t[:, half * 260:(half + 1) * 260], in_=Ob.t[:, :260]), reads=[Ob.tr], writes=[osb_tr[half]])
                    P.op(POOL, lambda: nc.gpsimd.tensor_copy(out=rsum.t[:, half * 4:(half + 1) * 4], in_=osb.t[:, half * 260:(half + 1) * 260].rearrange("p (h d) -> p h d", h=4)[:, :, 64]),
                         reads=[osb_tr[half]], writes=[rsum_tr[half]])
                    P.op(POOL, lambda: nc.gpsimd.tensor_tensor(out=rinv.t[:, half * 4:(half + 1) * 4], in0=rsum.t[:, half * 4:(half + 1) * 4], in1=mones.t[:, 0:4], op=ALU.pow),
                         reads=[rsum_tr[half], mones.tr], writes=[rinv_tr[half]])
                for h in range(8):
                    oc = (h % 2) * 260 + (h // 2) * 65
                    ri = (h % 2) * 4 + h // 2
                    P.op(ACT, lambda: nc.scalar.activation(out=ya.t[:, h * 64:(h + 1) * 64], in_=osb.t[:, oc:oc + 64], func=AF.Identity, scale=rinv.t[:, ri:ri + 1]),
                         reads=[osb_tr[h % 2], rinv_tr[h % 2]], writes=[ya_tr[h]])
                if cstop < 6:
                    return
                q = PQ[0]
                for j in range(4):
                    P.op(PE, lambda: nc.tensor.transpose(out=q.t[:, j * 128:(j + 1) * 128], in_=ya.t[:, j * 128:(j + 1) * 128], identity=identb.t[:]),
                         reads=[ya_tr[2 * j], ya_tr[2 * j + 1], identb.tr], writes=[q.tr])
                P.op(ACT, lambda: nc.scalar.copy(out=yaT.t[:], in_=q.t[:, :512].rearrange("p (j t) -> p j t", j=4)), reads=[q.tr], writes=[yaT.tr])
                P.op(POOL, lambda: nc.gpsimd.tensor_tensor(out=yat[s].t[:], in0=yaT.t[:], in1=szat[s].t[:], op=ALU.mult),
                     reads=[yaT.tr, szat[s].tr], writes=[yat[s].tr])
                if cstop < 7:
                    return
                P.dma(POOL, yat[s].sem, YC[4:8, :, c0:c1].rearrange("j p t -> p j t"), yat[s].t[:], reads=[yat[s].tr])

            if NTC > 0:
                load_idx(0)
                if NTC > 1:
                    load_idx(1)
                S1a(0)
            for i in range(NTC):
                if i + 2 < NTC:
                    load_idx(i + 2)
                load_att(i)
                if i + 1 < NTC:
                    S1a(i + 1)
                if i >= 1:
                    S2(i - 1)
                S1b(i)
                S1c(i)
            if NTC > 0:
                S2(NTC - 1)
            P.barrier()

        with ExitStack() as sD:
            yc = sb(sD, "yc", [128, 8, LP], BF16, True)
            wol = [sb(sD, f"wol{i}", [128, 1024], F32, True) for i in range(2)]
            wob = sb(sD, "wob", [128, 8, 1024], BF16)
            wob_tr = [Tr() for _ in range(8)]
            grep = sb(sD, "grep", [128, 1024], F32, True)
            brep = sb(sD, "brep", [128, 1024], F32, True)
            htl = [sb(sD, f"htl{i}", [128, 1024], F32, True) for i in range(3)]
            zt = [sb(sD, f"zt{i}", [128, 1024], F32) for i in range(3)]
            zn = [sb(sD, f"zn{i}", [128, 1024], F32) for i in range(3)]
            o1 = [sb(sD, f"o1{i}", [128, 1024], F32) for i in range(3)]
            ho = [sb(sD, f"ho{i}", [128, 1024], F32, True) for i in range(3)]
            st6 = [sb(sD, f"st6{i}", [128, 12], F32) for i in range(3)]
            mv = [sb(sD, f"mv{i}", [128, 2], F32) for i in range(3)]
            sdv = [sb(sD, f"sdv{i}", [128, 1], F32) for i in range(3)]
            rs = [sb(sD, f"rs{i}", [128, 1], F32) for i in range(3)]
            nmr = [sb(sD, f"nmr{i}", [128, 1], F32) for i in range(3)]

            for ec in range(8):
                P.dma(SP, yc.sem, yc.t[:, ec, :], YC[ec], writes=[yc.tr])
            P.dma(SP, grep.sem, grep.t[:], GB[l, 0], writes=[grep.tr])
            P.dma(SP, brep.sem, brep.t[:], GB[l, 1], writes=[brep.tr])
            for ec in range(8):
                s = ec % 2
                P.dma(SP, wol[s].sem, wol[s].t[:], WO[l, ec], writes=[wol[s].tr])
                P.op(POOL, lambda s=s, ec=ec: nc.gpsimd.tensor_copy(out=wob.t[:, ec, :], in_=wol[s].t[:]), reads=[wol[s].tr], writes=[wob_tr[ec]])
            for i in range(NT if "D" in phases else 0):
                s = i % 3
                P.dma(SP, htl[s].sem, htl[s].t[:], h_in[i * 128:(i + 1) * 128, :], writes=[htl[s].tr])
                for half in range(2):
                    b = PS[(i % 3) * 2 + half]
                    for ec in range(8):
                        P.op(PE, lambda b=b, ec=ec, half=half, i=i: nc.tensor.matmul(b.t[:, :], lhsT=yc.t[:, ec, i * 128:(i + 1) * 128], rhs=wob.t[:, ec, half * 512:(half + 1) * 512], start=(ec == 0), stop=(ec == 7)),
                             reads=[yc.tr, wob_tr[ec]], writes=[b.tr])
                    P.op(DVE, lambda b=b, half=half, s=s: nc.vector.scalar_tensor_tensor(out=zt[s].t[:, half * 512:(half + 1) * 512], in0=htl[s].t[:, half * 512:(half + 1) * 512], scalar=ALPHA, in1=b.t[:, :], op0=ALU.mult, op1=ALU.add),
                         reads=[htl[s].tr, b.tr], writes=[zt[s].tr])
                for half in range(2):
                    P.op(DVE, lambda half=half, s=s: nc.vector.bn_stats(out=st6[s].t[:, half * 6:(half + 1) * 6], in_=zt[s].t[:, half * 512:(half + 1) * 512]), reads=[zt[s].tr], writes=[st6[s].tr])
                P.op(DVE, lambda s=s: nc.vector.bn_aggr(out=mv[s].t[:], in_=st6[s].t[:]), reads=[st6[s].tr], writes=[mv[s].tr])
                P.op(ACT, lambda s=s: nc.scalar.activation(out=sdv[s].t[:], in_=mv[s].t[:, 1:2], func=AF.Sqrt, bias=epst.t[:, 0:1]), reads=[mv[s].tr, epst.tr], writes=[sdv[s].tr])
                P.op(DVE, lambda s=s: nc.vector.reciprocal(out=rs[s].t[:], in_=sdv[s].t[:]), reads=[sdv[s].tr], writes=[rs[s].tr])
                P.op(DVE, lambda s=s: nc.vector.scalar_tensor_tensor(out=nmr[s].t[:], in0=mv[s].t[:, 0:1], scalar=-1.0, in1=rs[s].t[:], op0=ALU.mult, op1=ALU.mult), reads=[mv[s].tr, rs[s].tr], writes=[nmr[s].tr])
                P.op(ACT, lambda s=s: nc.scalar.activation(out=zn[s].t[:], in_=zt[s].t[:], func=AF.Identity, scale=rs[s].t[:, 0:1], bias=nmr[s].t[:, 0:1]), reads=[zt[s].tr, rs[s].tr, nmr[s].tr], writes=[zn[s].tr])
                P.op(POOL, lambda s=s: nc.gpsimd.tensor_tensor(out=o1[s].t[:], in0=zn[s].t[:], in1=grep.t[:], op=ALU.mult), reads=[zn[s].tr, grep.tr], writes=[o1[s].tr])
                P.op(POOL, lambda s=s: nc.gpsimd.tensor_tensor(out=ho[s].t[:], in0=o1[s].t[:], in1=brep.t[:], op=ALU.add), reads=[o1[s].tr, brep.tr], writes=[ho[s].tr])
                P.dma(POOL, ho[s].sem, h_out[i * 128:(i + 1) * 128, :], ho[s].t[:], reads=[ho[s].tr])
            P.barrier()

    es.close()
    return nc, P.nins


def _rope_tables():
    inv_freq = (10000.0 ** (-np.arange(0, 64, 2, dtype=np.float32) / np.float32(64))).astype(np.float32)
    ang = np.arange(LP, dtype=np.float32)[:, None] * inv_freq[None, :]
    cos = np.cos(ang).astype(np.float32).T
    sin = np.sin(ang).astype(np.float32).T
    p = np.arange(128)
    d = p % 64
    cosT = cos[d % 32]
    sinT = np.where((d < 32)[:, None], -sin[d % 32], sin[d % 32])
    return np.ascontiguousarray(cosT, dtype=np.float32), np.ascontiguousarray(sinT, dtype=np.float32)


def _prep_weights(w_in, conv_w, conv_b, conv_ln_g, conv_ln_b, w_out, post_ln_g, post_ln_b):
    Ld = w_in.shape[0]
    cols = _weight_cols()
    WA = np.empty((Ld, NWCH, 128, 1024), np.float32)
    for l in range(Ld):
        for c, cc in enumerate(cols):
            blk = w_in[l][:, cc]
            WA[l, c] = blk.reshape(8, 128, 128).transpose(1, 0, 2).reshape(128, 1024)
        for hv in range(2):
            blk = w_in[l][:, O_V + hv * 256:O_V + (hv + 1) * 256]
            arr = blk.reshape(8, 128, 256).transpose(1, 0, 2).reshape(128, 2048)
            WA[l, len(cols) + 2 * hv] = arr[:, :1024]
            WA[l, len(cols) + 2 * hv + 1] = arr[:, 1024:]
    WWI = np.ascontiguousarray(w_in[:, :, O_WI:O_WI + 8].reshape(Ld, 8, 128, 8).transpose(0, 2, 1, 3).reshape(Ld, 128, 64))
    CPa = np.empty((Ld, 128, 136), np.float32)
    CPa[:, :, 0:124] = conv_w.reshape(Ld, 31, 4, 128).transpose(0, 3, 2, 1).reshape(Ld, 128, 124)
    CPa[:, :, 124:128] = conv_b.reshape(Ld, 4, 128).transpose(0, 2, 1)
    CPa[:, :, 128:132] = conv_ln_g.reshape(Ld, 4, 128).transpose(0, 2, 1)
    CPa[:, :, 132:136] = conv_ln_b.reshape(Ld, 4, 128).transpose(0, 2, 1)
    WOa = np.ascontiguousarray(w_out.reshape(Ld, 8, 128, 1024))
    GBa = np.empty((Ld, 2, 128, 1024), np.float32)
    GBa[:, 0] = post_ln_g[:, None, :]
    GBa[:, 1] = post_ln_b[:, None, :]
    return WA, WWI, CPa, WOa, GBa


def _consts():
    ident = np.eye(128, dtype=np.float32)
    t = np.arange(128)
    negmask = np.where(t[None, :] <= t[:, None], 0.0, NEG).astype(np.float32)
    cosT, sinT = _rope_tables()
    pwr = np.broadcast_to((0.5 ** np.arange(1, KBIS + 1)).astype(np.float32)[None, :], (128, KBIS)).copy()
    return {"c_ident": ident, "c_negmask": negmask, "c_cos": cosT, "c_sin": sinT, "c_pw": pwr}


_CACHE = {}
FUSED = True


def _get_prog(n_layers):
    if n_layers not in _CACHE:
        _CACHE[n_layers] = build_program(n_layers)[0]
    return _CACHE[n_layers]


def kernel(x, meta_tokens, w_in, conv_w, conv_b, conv_ln_g, conv_ln_b, w_out, post_ln_g, post_ln_b):
    x = np.asarray(x, np.float32)
    B = x.shape[0]
    f = lambda a: np.asarray(a, np.float32)
    WA, WWI, CPa, WOa, GBa = _prep_weights(f(w_in), f(conv_w), f(conv_b), f(conv_ln_g), f(conv_ln_b), f(w_out), f(post_ln_g), f(post_ln_b))
    consts = _consts()
    hs = []
    for b in range(B):
        h = np.zeros((LP, D_MODEL), np.float32)
        h[:N_META] = f(meta_tokens)
        h[N_META:N_META + SEQ] = x[b]
        hs.append(h)
    if FUSED:
        nc = _get_prog(DEPTH)
        in_maps = [dict(h0=hs[b], WA=WA, WWI=WWI, CP=CPa, WO=WOa, GB=GBa, **consts) for b in range(B)]
        res = run_bass_kernel_spmd(nc, in_maps, core_ids=list(range(B)))
        hs = [res.results[b]["hout"] for b in range(B)]
    else:
        nc = _get_prog(1)
        for l in range(DEPTH):
            in_maps = [dict(h0=hs[b], WA=WA[l:l + 1], WWI=WWI[l:l + 1], CP=CPa[l:l + 1], WO=WOa[l:l + 1], GB=GBa[l:l + 1], **consts) for b in range(B)]
            res = run_bass_kernel_spmd(nc, in_maps, core_ids=list(range(B)))
            hs = [res.results[b]["hout"] for b in range(B)]
    out = np.stack([hs[b][N_META:N_META + SEQ] for b in range(B)], axis=0)
    return np.ascontiguousarray(out, dtype=np.float32)
```

```python
import numpy as np
from contextlib import ExitStack
import concourse.bass as bass
import concourse.mybir as mybir
from concourse.bass_utils import run_bass_kernel_spmd

F32 = mybir.dt.float32
BF16 = mybir.dt.bfloat16
AF = mybir.ActivationFunctionType
ALU = mybir.AluOpType
AX = mybir.AxisListType

D_MODEL = 1024
SEQ = 4096
N_META = 16
LP = 4224
NT = LP // 128
DEPTH = 4
TOPK = 256
KBIS = 17
LN_EPS = 1e-5
ALPHA = (2.0 * DEPTH) ** 0.25
WI_SCALE = (64 ** -0.5) * (8 ** -0.5)
NEG = -1.0e30
TCH = [(i * 512, 512) for i in range(8)] + [(4096, 128)]

PJ_U, PJ_SZC, PJ_Q, PJ_K, PJ_SZA, PJ_QI, PJ_KI = 0, 4, 8, 12, 16, 20, 24
NPJ = 25
O_A, O_G, O_ZC, O_Q, O_K, O_V, O_ZA, O_QI, O_KI, O_WI = 0, 512, 1024, 1536, 2048, 2560, 3072, 3584, 4096, 4160


def _items():
    items = []
    c = 0
    for j in range(4):
        items.append(("conv", c, 2, PJ_U + j)); c += 2
    for j in range(4):
        items.append(("silu", c, 1, PJ_SZC + j)); c += 1
    for j in range(4):
        items.append(("rope", c, 2, PJ_Q + j)); c += 2
    for j in range(4):
        items.append(("rope", c, 2, PJ_K + j)); c += 2
    for j in range(4):
        items.append(("silu", c, 1, PJ_SZA + j)); c += 1
    for j in range(4):
        items.append(("rope", c, 2, PJ_QI + j)); c += 2
    items.append(("rope", c, 2, PJ_KI)); c += 2
    for hv in range(2):
        items.append(("v", c, 2, hv)); c += 2
    return items, c


ITEMS, NWCH = _items()


def _rot_cols(base, width):
    idx = np.arange(width)
    return base + (idx // 64) * 64 + ((idx % 64) + 32) % 64


def _weight_cols():
    cols = []
    for j in range(4):
        cols.append(O_A + j * 128 + np.arange(128)); cols.append(O_G + j * 128 + np.arange(128))
    for j in range(4):
        cols.append(O_ZC + j * 128 + np.arange(128))
    for base in (O_Q, O_K):
        for j in range(4):
            cols.append(base + j * 128 + np.arange(128))
            cols.append(_rot_cols(base, 512)[j * 128:(j + 1) * 128])
    za = [O_ZA + j * 128 + np.arange(128) for j in range(4)]
    qi = []
    for j in range(4):
        qi.append(O_QI + j * 128 + np.arange(128))
        qi.append(_rot_cols(O_QI, 512)[j * 128:(j + 1) * 128])
    cols = cols + za + qi
    kic = O_KI + np.arange(64)
    kir = _rot_cols(O_KI, 64)
    cols.append(np.concatenate([kic, kic])); cols.append(np.concatenate([kir, kir]))
    return cols


class Sem:
    __slots__ = ("h", "val")

    def __init__(self, h):
        self.h = h
        self.val = 0


class Tr:
    __slots__ = ("w", "r")

    def __init__(self):
        self.w = {}
        self.r = {}


class Eng:
    def __init__(self, name, e, sem, sync_self):
        self.name = name
        self.e = e
        self.sem = sem
        self.sync_self = sync_self
        self.waited = {}


class Prog:
    def __init__(self, nc, es):
        self.nc = nc
        self.es = es
        self.sems = []
        self.PE = Eng("pe", nc.tensor, self.new_sem("s_pe"), False)
        self.ACT = Eng("act", nc.scalar, self.new_sem("s_act"), True)
        self.DVE = Eng("dve", nc.vector, self.new_sem("s_dve"), True)
        self.POOL = Eng("pool", nc.gpsimd, self.new_sem("s_pool"), True)
        self.SP = Eng("sp", nc.sync, self.new_sem("s_sp"), False)
        self.engs = [self.PE, self.ACT, self.DVE, self.POOL, self.SP]
        self.nins = 0

    def new_sem(self, name):
        s = Sem(self.es.enter_context(self.nc.semaphore(name)))
        self.sems.append(s)
        return s

    def _wait(self, E, deps):
        for s, v in deps.items():
            if s is E.sem and not E.sync_self:
                continue
            if E.waited.get(s, 0) < v:
                E.e.wait_ge(s.h, v)
                E.waited[s] = v

    @staticmethod
    def _deps(reads, writes):
        deps = {}
        for t in reads:
            for s, v in t.w.items():
                if deps.get(s, 0) < v:
                    deps[s] = v
        for t in writes:
            for dd in (t.w, t.r):
                for s, v in dd.items():
                    if deps.get(s, 0) < v:
                        deps[s] = v
        return deps

    def op(self, E, ins_fn, reads=(), writes=()):
        self._wait(E, self._deps(reads, writes))
        ins = ins_fn()
        E.sem.val += 1
        ins.then_inc(E.sem.h, 1)
        v = E.sem.val
        for t in writes:
            t.w = {E.sem: v}
            t.r = {}
        for t in reads:
            t.r[E.sem] = v
        self.nins += 1

    def dma(self, Q, sem, out, in_, reads=(), writes=()):
        deps = self._deps(reads, writes)
        deps.pop(sem, None)
        self._wait(Q, deps)
        ins = Q.e.dma_start(out=out, in_=in_)
        sem.val += 16
        ins.then_inc(sem.h, 16)
        for t in writes:
            t.w = {sem: sem.val}
            t.r = {}
        for t in reads:
            t.r[sem] = sem.val
        self.nins += 1

    def barrier(self):
        allv = {s: s.val for s in self.sems if s.val > 0}
        for E in self.engs:
            for s, v in allv.items():
                if s is E.sem:
                    continue
                if E.waited.get(s, 0) < v:
                    E.e.wait_ge(s.h, v)
                    E.waited[s] = v


class Buf:
    def __init__(self, t, tr=None, sem=None):
        self.t = t
        self.tr = tr if tr is not None else Tr()
        self.sem = sem


def build_program(n_layers, dbg=False, phases="ABCD", nitems=None, ntiles=None, cstop=9):
    nc = bass.Bass("TRN2", target_bir_lowering=False)
    es = ExitStack()
    P = Prog(nc, es)
    PE, ACT, DVE, POOL, SP = P.PE, P.ACT, P.DVE, P.POOL, P.SP
    L = n_layers

    def din(name, shape, dt=F32):
        return nc.dram_tensor(name, shape, dt, kind="ExternalInput").ap()

    skind = "ExternalOutput" if dbg else "Internal"
    h0 = din("h0", [LP, D_MODEL])
    WA = din("WA", [L, NWCH, 128, 1024])
    WWI = din("WWI", [L, 128, 64])
    CP = din("CP", [L, 128, 136])
    WO = din("WO", [L, 8, 128, 1024])
    GB = din("GB", [L, 2, 128, 1024])
    c_ident = din("c_ident", [128, 128])
    c_negmask = din("c_negmask", [128, 128])
    c_cos = din("c_cos", [128, LP])
    c_sin = din("c_sin", [128, LP])
    c_pw = din("c_pw", [128, KBIS])
    hout = nc.dram_tensor("hout", [LP, D_MODEL], F32, kind="ExternalOutput").ap()
    hbufs = [nc.dram_tensor(f"hbuf{i}", [LP, D_MODEL], F32, kind="Internal").ap() for i in range(2)] if L > 1 else []
    PJ = nc.dram_tensor("PJ", [NPJ, 128, LP], BF16, kind=skind).ap()
    VA = nc.dram_tensor("VA", [NT, 128, 520], BF16, kind=skind).ap()
    WI = nc.dram_tensor("WI", [128, NT * 8], F32, kind=skind).ap()
    YC = nc.dram_tensor("YC", [8, 128, LP], BF16, kind=skind).ap()

    semcache = {}
    cur = {"l": "g"}

    def sb(stack, name, shape, dt, dma_sem=False):
        t = stack.enter_context(nc.sbuf_tensor(f"{name}_{cur['l']}", shape, dt))
        sem = None
        if dma_sem:
            if name not in semcache:
                semcache[name] = P.new_sem("d_" + name)
            sem = semcache[name]
        return Buf(t, sem=sem)

    PS = [Buf(es.enter_context(nc.psum_tensor(f"ps{i}", [128, 512], F32))) for i in range(6)]
    PQ = [Buf(es.enter_context(nc.psum_tensor(f"pq{i}", [128, 1024], BF16))) for i in range(2)]
    identf = sb(es, "identf", [128, 128], F32, True)
    identb = sb(es, "identb", [128, 128], BF16)
    negmask = sb(es, "negmask", [128, 128], F32, True)
    pw = sb(es, "pw", [128, KBIS], F32, True)
    epst = sb(es, "epst", [128, 1], F32)
    thrneg = sb(es, "thrneg", [128, 1], F32)
    onesm = sb(es, "onesm", [128, 128], F32)
    mones = sb(es, "mones", [128, 8], F32)

    P.dma(SP, identf.sem, identf.t[:], c_ident[:, :], writes=[identf.tr])
    P.dma(SP, negmask.sem, negmask.t[:], c_negmask[:, :], writes=[negmask.tr])
    P.dma(SP, pw.sem, pw.t[:], c_pw[:, :], writes=[pw.tr])
    P.op(DVE, lambda: nc.vector.tensor_copy(out=identb.t[:], in_=identf.t[:]), reads=[identf.tr], writes=[identb.tr])
    P.op(DVE, lambda: nc.vector.memset(epst.t[:], LN_EPS), writes=[epst.tr])
    P.op(DVE, lambda: nc.vector.memset(thrneg.t[:], -1.0e29), writes=[thrneg.tr])
    P.op(DVE, lambda: nc.vector.memset(onesm.t[:], 1.0 / 512.0), writes=[onesm.tr])
    P.op(DVE, lambda: nc.vector.memset(mones.t[:], -1.0), writes=[mones.tr])

    cnt = {"ps": 0}
    LB = [PS[0], PS[1], PS[2], Buf(PQ[1].t[:, :].bitcast(F32), tr=PQ[1].tr)]

    def next_ps3():
        b = PS[cnt["ps"] % 3]
        cnt["ps"] += 1
        return b

    for l in range(L):
        cur["l"] = f"L{l}"
        h_in = h0 if l == 0 else hbufs[(l - 1) % 2]
        h_out = hout if l == L - 1 else hbufs[l % 2]

        with ExitStack() as sA:
            hT = sb(sA, "hT", [128, 8, LP], BF16)
            hT_tr = [Tr() for _ in range(NT)]
            with ExitStack() as s0:
                hld = [sb(s0, f"hld{i}", [128, 1024], F32, True) for i in range(2)]
                hb = [sb(s0, f"hb{i}", [128, 1024], BF16) for i in range(2)]
                for i in range(NT):
                    s = i % 2
                    P.dma(SP, hld[s].sem, hld[s].t[:], h_in[i * 128:(i + 1) * 128, :], writes=[hld[s].tr])
                    if i % 2 == 0:
                        P.op(ACT, lambda s=s: nc.scalar.copy(out=hb[s].t[:], in_=hld[s].t[:]), reads=[hld[s].tr], writes=[hb[s].tr])
                    else:
                        P.op(DVE, lambda s=s: nc.vector.tensor_copy(out=hb[s].t[:], in_=hld[s].t[:]), reads=[hld[s].tr], writes=[hb[s].tr])
                    q = PQ[i % 2]
                    for kc in range(8):
                        P.op(PE, lambda s=s, kc=kc, q=q: nc.tensor.transpose(out=q.t[:, kc * 128:(kc + 1) * 128], in_=hb[s].t[:, kc * 128:(kc + 1) * 128], identity=identb.t[:]),
                             reads=[hb[s].tr, identb.tr], writes=[q.tr])
                    src = q.t[:, :].rearrange("p (k t) -> p k t", k=8)
                    if i % 2 == 0:
                        P.op(DVE, lambda i=i, src=src: nc.vector.tensor_copy(out=hT.t[:, :, i * 128:(i + 1) * 128], in_=src), reads=[q.tr], writes=[hT_tr[i]])
                    else:
                        P.op(ACT, lambda i=i, src=src: nc.scalar.copy(out=hT.t[:, :, i * 128:(i + 1) * 128], in_=src), reads=[q.tr], writes=[hT_tr[i]])
                P.barrier()

            with ExitStack() as s1:
                if "A" not in phases:
                    ITEMS_ = []
                else:
                    ITEMS_ = ITEMS if nitems is None else ITEMS[:nitems]
                cosT = sb(s1, "cosT", [128, LP], F32, True)
                sinT = sb(s1, "sinT", [128, LP], F32, True)
                wld = [sb(s1, f"wld{i}", [128, 2048], F32, True) for i in range(2)]
                wbf = [sb(s1, f"wbf{i}", [128, 2048], BF16) for i in range(2)]
                stage = [sb(s1, f"stage{i}", [128, LP], BF16, True) for i in range(2)]
                sig = [sb(s1, f"sig{i}", [128, 512], F32) for i in range(2)]
                tm1 = [sb(s1, f"tm1{i}", [128, 512], F32) for i in range(2)]
                tm2 = [sb(s1, f"tm2{i}", [128, 512], F32) for i in range(2)]
                vst = [sb(s1, f"vst{i}", [128, 8, 4, 65], BF16, True) for i in range(2)]
                wwif = sb(s1, "wwif", [128, 64], F32, True)
                wwib = sb(s1, "wwib", [128, 64], BF16)
                wisb = sb(s1, "wisb", [128, NT * 8], F32, True)

                P.dma(SP, cosT.sem, cosT.t[:], c_cos[:, :], writes=[cosT.tr])
                P.dma(SP, sinT.sem, sinT.t[:], c_sin[:, :], writes=[sinT.tr])
                P.dma(SP, wwif.sem, wwif.t[:], WWI[l], writes=[wwif.tr])
                P.op(POOL, lambda: nc.gpsimd.tensor_copy(out=wwib.t[:], in_=wwif.t[:]), reads=[wwif.tr], writes=[wwib.tr])
                for s in range(2):
                    P.op(POOL, lambda s=s: nc.gpsimd.memset(vst[s].t[:], 1.0), writes=[vst[s].tr])

                def load_w(it_idx):
                    kind, c0, nch, _ = ITEMS[it_idx]
                    s = it_idx % 2
                    P.dma(SP, wld[s].sem, wld[s].t[:, :nch * 1024].rearrange("p (c f) -> p c f", c=nch), WA[l, c0:c0 + nch].rearrange("c p f -> p c f"), writes=[wld[s].tr])
                    P.op(POOL, lambda s=s, nch=nch: nc.gpsimd.tensor_copy(out=wbf[s].t[:, :nch * 1024], in_=wld[s].t[:, :nch * 1024]),
                         reads=[wld[s].tr], writes=[wbf[s].tr])

                if ITEMS_:
                    load_w(0)
                gctr = 0
                st_ctr = 0
                for it_idx, (kind, c0, nch, pj) in enumerate(ITEMS_):
                    if it_idx + 1 < len(ITEMS_):
                        load_w(it_idx + 1)
                    ws = it_idx % 2
                    wv_ = wbf[ws]
                    if kind != "v":
                        stg = stage[st_ctr % 2]
                        st_ctr += 1
                        for (t0, n) in TCH:
                            bA = PS[(gctr % 2) * 2]
                            bB = PS[(gctr % 2) * 2 + 1]
                            g2 = gctr % 2
                            gctr += 1
                            hts = hT_tr[t0 // 128:(t0 + n) // 128]
                            for kc in range(8):
                                P.op(PE, lambda kc=kc, bA=bA, t0=t0, n=n: nc.tensor.matmul(bA.t[:, :n], lhsT=wv_.t[:, kc * 128:(kc + 1) * 128], rhs=hT.t[:, kc, t0:t0 + n], start=(kc == 0), stop=(kc == 7)),
                                     reads=[wv_.tr] + hts, writes=[bA.tr])
                            if nch == 2:
                                for kc in range(8):
                                    P.op(PE, lambda kc=kc, bB=bB, t0=t0, n=n: nc.tensor.matmul(bB.t[:, :n], lhsT=wv_.t[:, 1024 + kc * 128:1024 + (kc + 1) * 128], rhs=hT.t[:, kc, t0:t0 + n], start=(kc == 0), stop=(kc == 7)),
                                         reads=[wv_.tr] + hts, writes=[bB.tr])
                            if kind == "conv":
                                P.op(ACT, lambda bB=bB, g2=g2, n=n: nc.scalar.activation(out=sig[g2].t[:, :n], in_=bB.t[:, :n], func=AF.Sigmoid), reads=[bB.tr], writes=[sig[g2].tr])
                                P.op(DVE, lambda bA=bA, g2=g2, t0=t0, n=n, stg=stg: nc.vector.tensor_tensor(out=stg.t[:, t0:t0 + n], in0=bA.t[:, :n], in1=sig[g2].t[:, :n], op=ALU.mult),
                                     reads=[bA.tr, sig[g2].tr], writes=[stg.tr])
                            elif kind == "silu":
                                P.op(ACT, lambda bA=bA, t0=t0, n=n, stg=stg: nc.scalar.activation(out=stg.t[:, t0:t0 + n], in_=bA.t[:, :n], func=AF.Silu), reads=[bA.tr], writes=[stg.tr])
                            else:
                                P.op(DVE, lambda bA=bA, g2=g2, t0=t0, n=n: nc.vector.tensor_tensor(out=tm1[g2].t[:, :n], in0=bA.t[:, :n], in1=cosT.t[:, t0:t0 + n], op=ALU.mult),
                                     reads=[bA.tr, cosT.tr], writes=[tm1[g2].tr])
                                P.op(DVE, lambda bB=bB, g2=g2, t0=t0, n=n: nc.vector.tensor_tensor(out=tm2[g2].t[:, :n], in0=bB.t[:, :n], in1=sinT.t[:, t0:t0 + n], op=ALU.mult),
                                     reads=[bB.tr, sinT.tr], writes=[tm2[g2].tr])
                                P.op(POOL, lambda g2=g2, t0=t0, n=n, stg=stg: nc.gpsimd.tensor_tensor(out=stg.t[:, t0:t0 + n], in0=tm1[g2].t[:, :n], in1=tm2[g2].t[:, :n], op=ALU.add),
                                     reads=[tm1[g2].tr, tm2[g2].tr], writes=[stg.tr])
                        P.dma(POOL, stg.sem, PJ[pj], stg.t[:], reads=[stg.tr])
                    else:
                        hv = pj
                        wv3 = wv_.t[:, :].rearrange("p (k e) -> p k e", k=8)
                        for i in range(NT):
                            b = PS[(gctr % 2) * 2]
                            gctr += 1
                            gi, gs = i // 8, (i // 8) % 2
                            for kc in range(8):
                                P.op(PE, lambda kc=kc, b=b, i=i: nc.tensor.matmul(b.t[:, :256], lhsT=hT.t[:, kc, i * 128:(i + 1) * 128], rhs=wv3[:, kc, :], start=(kc == 0), stop=(kc == 7)),
                                     reads=[wv_.tr, hT_tr[i]], writes=[b.tr])
                            P.op(ACT, lambda b=b, i=i, gs=gs: nc.scalar.copy(out=vst[gs].t[:, i % 8, :, 0:64], in_=b.t[:, :256].rearrange("p (h d) -> p h d", h=4)),
                                 reads=[b.tr], writes=[vst[gs].tr])
                            if i % 8 == 7 or i == NT - 1:
                                ng = i % 8 + 1
                                i0 = gi * 8
                                dst = VA[i0:i0 + ng].rearrange("i p (v f) -> p i v f", v=2)[:, :, hv, :]
                                P.dma(POOL, vst[gs].sem, dst, vst[gs].t[:, :ng].rearrange("p i h d -> p i (h d)"), reads=[vst[gs].tr])
                            if hv == 0:
                                b5 = PS[4 + (i % 2)]
                                for kc in range(8):
                                    P.op(PE, lambda kc=kc, b5=b5, i=i: nc.tensor.matmul(b5.t[:, :8], lhsT=hT.t[:, kc, i * 128:(i + 1) * 128], rhs=wwib.t[:, kc * 8:(kc + 1) * 8], start=(kc == 0), stop=(kc == 7)),
                                         reads=[wwib.tr, hT_tr[i]], writes=[b5.tr])
                                P.op(DVE, lambda b5=b5, i=i: nc.vector.tensor_scalar(out=wisb.t[:, i * 8:(i + 1) * 8], in0=b5.t[:, :8], scalar1=WI_SCALE, scalar2=None, op0=ALU.mult),
                                     reads=[b5.tr], writes=[wisb.tr])
                P.dma(POOL, wisb.sem, WI[:, :], wisb.t[:], reads=[wisb.tr])
                P.barrier()

        with ExitStack() as sB:
            upad = [sb(sB, f"upad{j}", [128, 30 + LP], BF16, True) for j in range(4)]
            cp = sb(sB, "cp", [128, 136], F32, True)
            dg = sb(sB, "dg", [128, 4 * 31, 128], BF16)
            szc = [sb(sB, f"szc{i}", [128, 4, 512], BF16, True) for i in range(2)]
            yst = [sb(sB, f"yst{i}", [128, 4, 512], BF16, True) for i in range(2)]
            cf = [sb(sB, f"cf{j}", [128, 512], F32) for j in range(4)]
            sq = [sb(sB, f"sq{j}", [128, 512], F32) for j in range(4)]
            mean_sb = sb(sB, "mean_sb", [128, 512], F32)
            msq = sb(sB, "msq", [128, 512], F32)
            var = sb(sB, "var", [128, 512], F32)
            sd = sb(sB, "sd", [128, 512], F32)
            rstd = sb(sB, "rstd", [128, 512], F32)
            y1 = [sb(sB, f"y1{i}", [128, 512], F32) for i in range(2)]
            y2 = [sb(sB, f"y2{i}", [128, 512], F32) for i in range(2)]
            zz = [sb(sB, f"zz{i}", [128, 512], F32) for i in range(2)]

            P.dma(SP, cp.sem, cp.t[:], CP[l], writes=[cp.tr])
            for j in range(4):
                P.op(POOL, lambda j=j: nc.gpsimd.memset(upad[j].t[:, 0:30], 0.0), writes=[upad[j].tr])
                P.dma(SP, upad[j].sem, upad[j].t[:, 30:], PJ[PJ_U + j], reads=[upad[j].tr], writes=[upad[j].tr])
            dg_tr = [Tr() for _ in range(4 * 31)]
            for j in range(4):
                for tap in range(31):
                    di = j * 31 + tap
                    if di % 2 == 0:
                        P.op(DVE, lambda di=di: nc.vector.tensor_scalar(out=dg.t[:, di, :], in0=identb.t[:], scalar1=cp.t[:, di:di + 1], scalar2=None, op0=ALU.mult),
                             reads=[identb.tr, cp.tr], writes=[dg_tr[di]])
                    else:
                        P.op(POOL, lambda di=di: nc.gpsimd.tensor_scalar(out=dg.t[:, di, :], in0=identb.t[:], scalar1=cp.t[:, di:di + 1], scalar2=1.0, op0=ALU.mult, op1=ALU.mult),
                             reads=[identb.tr, cp.tr], writes=[dg_tr[di]])
            for ci, (t0, n) in enumerate(TCH if "B" in phases else []):
                s = ci % 2
                P.dma(SP, szc[s].sem, szc[s].t[:, :, :n], PJ[PJ_SZC:PJ_SZC + 4, :, t0:t0 + n].rearrange("j p t -> p j t"), writes=[szc[s].tr])
                for j in range(4):
                    for tap in range(31):
                        P.op(PE, lambda j=j, tap=tap, t0=t0, n=n: nc.tensor.matmul(PS[j].t[:, :n], lhsT=dg.t[:, j * 31 + tap, :], rhs=upad[j].t[:, t0 + tap:t0 + tap + n], start=(tap == 0), stop=(tap == 30)),
                             reads=[dg_tr[j * 31 + tap], upad[j].tr], writes=[PS[j].tr])
                for j in range(4):
                    P.op(ACT, lambda j=j, n=n: nc.scalar.activation(out=cf[j].t[:, :n], in_=PS[j].t[:, :n], func=AF.Identity, bias=cp.t[:, 124 + j:125 + j]), reads=[PS[j].tr, cp.tr], writes=[cf[j].tr])
                    P.op(ACT, lambda j=j, n=n: nc.scalar.activation(out=sq[j].t[:, :n], in_=PS[j].t[:, :n], func=AF.Square, bias=cp.t[:, 124 + j:125 + j]), reads=[PS[j].tr, cp.tr], writes=[sq[j].tr])
                for j in range(4):
                    P.op(PE, lambda j=j, n=n: nc.tensor.matmul(PS[4].t[:, :n], lhsT=onesm.t[:], rhs=cf[j].t[:, :n], start=(j == 0), stop=(j == 3)), reads=[onesm.tr, cf[j].tr], writes=[PS[4].tr])
                for j in range(4):
                    P.op(PE, lambda j=j, n=n: nc.tensor.matmul(PS[5].t[:, :n], lhsT=onesm.t[:], rhs=sq[j].t[:, :n], start=(j == 0), stop=(j == 3)), reads=[onesm.tr, sq[j].tr], writes=[PS[5].tr])
                P.op(ACT, lambda n=n: nc.scalar.copy(out=mean_sb.t[:, :n], in_=PS[4].t[:, :n]), reads=[PS[4].tr], writes=[mean_sb.tr])
                P.op(ACT, lambda n=n: nc.scalar.activation(out=msq.t[:, :n], in_=PS[4].t[:, :n], func=AF.Square), reads=[PS[4].tr], writes=[msq.tr])
                P.op(DVE, lambda n=n: nc.vector.tensor_tensor(out=var.t[:, :n], in0=PS[5].t[:, :n], in1=msq.t[:, :n], op=ALU.subtract), reads=[PS[5].tr, msq.tr], writes=[var.tr])
                P.op(ACT, lambda n=n: nc.scalar.activation(out=sd.t[:, :n], in_=var.t[:, :n], func=AF.Sqrt, bias=epst.t[:, 0:1]), reads=[var.tr, epst.tr], writes=[sd.tr])
                P.op(DVE, lambda n=n: nc.vector.reciprocal(out=rstd.t[:, :n], in_=sd.t[:, :n]), reads=[sd.tr], writes=[rstd.tr])
                for j in range(4):
                    k2 = j % 2
                    P.op(DVE, lambda j=j, n=n, k2=k2: nc.vector.tensor_tensor(out=y1[k2].t[:, :n], in0=cf[j].t[:, :n], in1=mean_sb.t[:, :n], op=ALU.subtract), reads=[cf[j].tr, mean_sb.tr], writes=[y1[k2].tr])
                    P.op(POOL, lambda n=n, k2=k2: nc.gpsimd.tensor_tensor(out=y2[k2].t[:, :n], in0=y1[k2].t[:, :n], in1=rstd.t[:, :n], op=ALU.mult), reads=[y1[k2].tr, rstd.tr], writes=[y2[k2].tr])
                    P.op(ACT, lambda j=j, n=n, k2=k2: nc.scalar.activation(out=zz[k2].t[:, :n], in_=y2[k2].t[:, :n], func=AF.Silu, scale=cp.t[:, 128 + j:129 + j], bias=cp.t[:, 132 + j:133 + j]),
                         reads=[y2[k2].tr, cp.tr], writes=[zz[k2].tr])
                    P.op(DVE, lambda j=j, n=n, k2=k2, s=s: nc.vector.tensor_tensor(out=yst[s].t[:, j, :n], in0=zz[k2].t[:, :n], in1=szc[s].t[:, j, :n], op=ALU.mult), reads=[zz[k2].tr, szc[s].tr], writes=[yst[s].tr])
                P.dma(POOL, yst[s].sem, YC[0:4, :, t0:t0 + n].rearrange("j p t -> p j t"), yst[s].t[:, :, :n], reads=[yst[s].tr])
            P.barrier()

        with ExitStack() as sC:
            kT = sb(sC, "kT", [128, 4, LP], BF16, True)
            kiT = sb(sC, "kiT", [128, LP], BF16, True)
            vaug = sb(sC, "vaug", [128, NT, 520], BF16, True)
            wi_all = sb(sC, "wi_all", [128, NT * 8], F32, True)
            qt = [sb(sC, f"qt{i}", [128, 4, 128], BF16, True) for i in range(2)]
            qit = [sb(sC, f"qit{i}", [128, 4, 128], BF16, True) for i in range(2)]
            szat = [sb(sC, f"szat{i}", [128, 4, 128], BF16, True) for i in range(2)]
            dgw = [sb(sC, f"dgw{i}", [128, 8, 128], BF16) for i in range(2)]
            score = [sb(sC, f"score{i}", [128, LP], F32) for i in range(2)]
            score_tr = [[Tr() for _ in range(9)] for _ in range(2)]
            junk = sb(sC, "junk", [128, LP], mybir.dt.uint8)
            mask = sb(sC, "mask", [128, LP], BF16)
            maskT = [sb(sC, f"maskT{i}", [128, LP], BF16) for i in range(2)]
            maskT_tr = [[Tr() for _ in range(5)] for _ in range(2)]
            Rr = [sb(sC, f"Rr{i}", [128, 512], BF16) for i in range(8)]
            Eb = [sb(sC, f"Eb{i}", [128, 512], BF16) for i in range(4)]
            PTb = [sb(sC, f"PTb{i}", [128, 512], BF16) for i in range(6)]
            hi = sb(sC, "hi", [128, 1], F32)
            lo = sb(sC, "lo", [128, 1], F32)
            w0 = sb(sC, "w0", [128, 1], F32)
            Wk = sb(sC, "Wk", [128, KBIS], F32)
            mid = [sb(sC, f"mid{i}", [128, 1], F32) for i in range(2)]
            cntb = sb(sC, "cntb", [128, 1], F32)
            dd = sb(sC, "dd", [128, 1], F32)
            rinv = sb(sC, "rinv", [128, 8], F32)
            rinv_tr = [Tr(), Tr()]
            rsum = sb(sC, "rsum", [128, 8], F32)
            rsum_tr = [Tr(), Tr()]
            osb = sb(sC, "osb", [128, 520], F32)
            osb_tr = [Tr(), Tr()]
            ya = sb(sC, "ya", [128, 512], BF16)
            ya_tr = [Tr() for _ in range(8)]
            yaT = sb(sC, "yaT", [128, 4, 128], BF16)
            yat = [sb(sC, f"yat{i}", [128, 4, 128], BF16, True) for i in range(2)]

            P.dma(SP, kiT.sem, kiT.t[:], PJ[PJ_KI], writes=[kiT.tr])
            P.dma(SP, wi_all.sem, wi_all.t[:], WI[:, :], writes=[wi_all.tr])

            def load_kv():
                for j in range(4):
                    P.dma(SP, kT.sem, kT.t[:, j, :], PJ[PJ_K + j], writes=[kT.tr])
                for i0 in range(0, NT, 11):
                    P.dma(SP, vaug.sem, vaug.t[:, i0:i0 + 11, :], VA[i0:i0 + 11].rearrange("i p f -> p i f"), writes=[vaug.tr])

            ctr = {"r": 0, "e": 0, "m": 0}
            NTC = (NT if ntiles is None else ntiles) if "C" in phases else 0

            def load_idx(i):
                s = i % 2
                c0, c1 = i * 128, (i + 1) * 128
                P.dma(SP, qit[s].sem, qit[s].t[:], PJ[PJ_QI:PJ_QI + 4, :, c0:c1].rearrange("j p t -> p j t"), writes=[qit[s].tr])

            def load_att(i):
                s = i % 2
                c0, c1 = i * 128, (i + 1) * 128
                P.dma(SP, qt[s].sem, qt[s].t[:], PJ[PJ_Q:PJ_Q + 4, :, c0:c1].rearrange("j p t -> p j t"), writes=[qt[s].tr])
                P.dma(SP, szat[s].sem, szat[s].t[:], PJ[PJ_SZA:PJ_SZA + 4, :, c0:c1].rearrange("j p t -> p j t"), writes=[szat[s].tr])

            def S1a(i):
                s = i % 2
                N = 128 * (i + 1)
                sc, sctr = score[s], score_tr[s]
                for h in range(8):
                    P.op(POOL, lambda h=h: nc.gpsimd.tensor_scalar(out=dgw[s].t[:, h, :], in0=identb.t[:], scalar1=wi_all.t[:, i * 8 + h:i * 8 + h + 1], scalar2=1.0, op0=ALU.mult, op1=ALU.mult),
                         reads=[identb.tr, wi_all.tr], writes=[dgw[s].tr])
                chunks = [(s0, min(512, N - s0)) for s0 in range(0, N, 512)]
                pendD = []

                def flush_diag():
                    c_, grp_, rbs_, s0_, n_ = pendD.pop(0)
                    for (h, rb) in rbs_:
                        P.op(PE, lambda: nc.tensor.matmul(PS[3].t[:, :n_], lhsT=dgw[s].t[:, h, :], rhs=rb.t[:, :n_], start=(h == 0), stop=(h == 7)),
                             reads=[dgw[s].tr, rb.tr], writes=[PS[3].tr])
                    if grp_ == 1:
                        P.op(ACT, lambda: nc.scalar.copy(out=sc.t[:, s0_:s0_ + n_], in_=PS[3].t[:, :n_]), reads=[PS[3].tr], writes=[sctr[c_]])

                for c, (s0, n) in enumerate(chunks):
                    for grp in range(2):
                        rbs = []
                        for hh in range(4):
                            h = grp * 4 + hh
                            Lb = LB[hh]
                            po = (h % 2) * 64
                            P.op(PE, lambda: nc.tensor.matmul(Lb.t[:, :n], lhsT=qit[s].t[po:po + 64, h // 2, :], rhs=kiT.t[po:po + 64, s0:s0 + n], start=True, stop=True),
                                 reads=[qit[s].tr, kiT.tr], writes=[Lb.tr])
                            rb = Rr[ctr["r"] % 8]
                            ctr["r"] += 1
                            P.op(ACT, lambda: nc.scalar.activation(out=rb.t[:, :n], in_=Lb.t[:, :n], func=AF.Relu), reads=[Lb.tr], writes=[rb.tr])
                            rbs.append((h, rb))
                        if pendD:
                            flush_diag()
                        pendD.append((c, grp, rbs, s0, n))
                while pendD:
                    flush_diag()
                nsc = len(chunks)
                P.op(POOL, lambda: nc.gpsimd.tensor_tensor(out=sc.t[:, N - 128:N], in0=sc.t[:, N - 128:N], in1=negmask.t[:], op=ALU.add),
                     reads=[sctr[nsc - 1], negmask.tr], writes=[sctr[nsc - 1]])

            def S1b(i):
                s = i % 2
                N = 128 * (i + 1)
                sc = score[s]
                sc_trs = score_tr[s][:(N + 511) // 512]
                if cstop < 2:
                    return
                if i < 2:
                    thr = thrneg
                else:
                    P.op(DVE, lambda: nc.vector.tensor_reduce(out=hi.t[:], in_=sc.t[:, :N], axis=AX.X, op=ALU.max), reads=sc_trs, writes=[hi.tr])
                    P.op(DVE, lambda: nc.vector.tensor_reduce(out=lo.t[:], in_=sc.t[:, :N - 128], axis=AX.X, op=ALU.min), reads=sc_trs, writes=[lo.tr])
                    P.op(DVE, lambda: nc.vector.tensor_tensor(out=w0.t[:], in0=hi.t[:], in1=lo.t[:], op=ALU.subtract), reads=[hi.tr, lo.tr], writes=[w0.tr])
                    P.op(DVE, lambda: nc.vector.tensor_scalar(out=Wk.t[:], in0=pw.t[:], scalar1=w0.t[:, 0:1], scalar2=None, op0=ALU.mult), reads=[pw.tr, w0.tr], writes=[Wk.tr])
                    P.op(DVE, lambda: nc.vector.tensor_tensor(out=mid[0].t[:], in0=lo.t[:], in1=Wk.t[:, 0:1], op=ALU.add), reads=[lo.tr, Wk.tr], writes=[mid[0].tr])
                    for k in range(KBIS):
                        mc, mn = mid[k % 2], mid[(k + 1) % 2]
                        kb = k + 1 if k < KBIS - 1 else k
                        P.op(DVE, lambda: nc.vector.tensor_scalar(out=junk.t[:, :N], in0=sc.t[:, :N], scalar1=mc.t[:, 0:1], scalar2=None, op0=ALU.is_ge, op1=ALU.add, accum_out=cntb.t[:, 0:1]),
                             reads=sc_trs + [mc.tr], writes=[junk.tr, cntb.tr])
                        P.op(DVE, lambda: nc.vector.tensor_scalar(out=dd.t[:], in0=cntb.t[:], scalar1=TOPK - 0.5, scalar2=Wk.t[:, k:k + 1], op0=ALU.is_ge, op1=ALU.mult),
                             reads=[cntb.tr, Wk.tr], writes=[dd.tr])
                        P.op(DVE, lambda: nc.vector.scalar_tensor_tensor(out=mn.t[:], in0=dd.t[:], scalar=Wk.t[:, kb:kb + 1], in1=mc.t[:], op0=ALU.subtract, op1=ALU.add),
                             reads=[dd.tr, Wk.tr, mc.tr], writes=[mn.tr])
                    thr = mid[KBIS % 2]
                P.op(DVE, lambda: nc.vector.tensor_scalar(out=mask.t[:, :N], in0=sc.t[:, :N], scalar1=thr.t[:, 0:1], scalar2=None, op0=ALU.is_ge),
                     reads=sc_trs + [thr.tr], writes=[mask.tr])

            def S1c(i):
                if cstop < 3:
                    return
                s = i % 2
                nb = i + 1
                for g in range((nb + 7) // 8):
                    q = PQ[0]
                    m = min(8, nb - g * 8)
                    for bi in range(m):
                        b = g * 8 + bi
                        P.op(PE, lambda: nc.tensor.transpose(out=q.t[:, bi * 128:(bi + 1) * 128], in_=mask.t[:, b * 128:(b + 1) * 128], identity=identb.t[:]),
                             reads=[mask.tr, identb.tr], writes=[q.tr])
                    P.op(ACT, lambda: nc.scalar.copy(out=maskT[s].t[:, g * 1024:g * 1024 + m * 128], in_=q.t[:, :m * 128]), reads=[q.tr], writes=[maskT_tr[s][g]])

            def S2(i):
                if cstop < 4:
                    return
                s = i % 2
                nb = i + 1
                c0, c1 = i * 128, (i + 1) * 128
                mT, mTtr = maskT[s], maskT_tr[s]
                units = [(hp, b0, min(4, nb - b0)) for hp in range(4) for b0 in range(0, nb, 4)]
                pend = []

                def pv(hp, b0, m, ptbs):
                    for hi_, ptb in enumerate(ptbs):
                        h = 2 * hp + hi_
                        Ob = PS[4 + hi_]
                        hh = hp
                        for bi in range(m):
                            b = b0 + bi
                            P.op(PE, lambda: nc.tensor.matmul(Ob.t[:, hh * 65:(hh + 1) * 65], lhsT=ptb.t[:, bi * 128:(bi + 1) * 128], rhs=vaug.t[:, b, h * 65:(h + 1) * 65], start=(b == 0), stop=(b == nb - 1)),
                                 reads=[ptb.tr, vaug.tr], writes=[Ob.tr])

                for (hp, b0, m) in units:
                    Sbs = [LB[ctr["m"] % 4], LB[(ctr["m"] + 1) % 4]]
                    ctr["m"] += 2
                    for bi in range(m):
                        b = b0 + bi
                        for hi_ in range(2):
                            po = hi_ * 64
                            Sb = Sbs[hi_]
                            P.op(PE, lambda: nc.tensor.matmul(Sb.t[:, bi * 128:(bi + 1) * 128], lhsT=kT.t[po:po + 64, hp, b * 128:(b + 1) * 128], rhs=qt[s].t[po:po + 64, hp, :], start=True, stop=True),
                                 reads=[kT.tr, qt[s].tr], writes=[Sb.tr])
                    mtrs = list({id(mTtr[b // 8]): mTtr[b // 8] for b in range(b0, b0 + m)}.values())
                    ptbs = []
                    for hi_ in range(2):
                        Sb = Sbs[hi_]
                        eb = Eb[ctr["e"] % 4]
                        ptb = PTb[ctr["e"] % 6]
                        ctr["e"] += 1
                        P.op(ACT, lambda: nc.scalar.activation(out=eb.t[:, :m * 128], in_=Sb.t[:, :m * 128], func=AF.Exp, scale=0.125), reads=[Sb.tr], writes=[eb.tr])
                        P.op(POOL, lambda: nc.gpsimd.tensor_tensor(out=ptb.t[:, :m * 128], in0=eb.t[:, :m * 128], in1=mT.t[:, b0 * 128:(b0 + m) * 128], op=ALU.mult),
                             reads=[eb.tr] + mtrs, writes=[ptb.tr])
                        ptbs.append(ptb)
                    pend.append((hp, b0, m, ptbs))
                    if len(pend) > 1:
                        pv(*pend.pop(0))
                while pend:
                    pv(*pend.pop(0))
                if cstop < 5:
                    return
                for half in range(2):
                    Ob = PS[4 + half]
                    P.op(ACT, lambda: nc.scalar.copy(out=osb.t[:, half * 260:(half + 1) * 260], in_=Ob.t[:, :260]), reads=[Ob.tr], writes=[osb_tr[half]])
                    P.op(POOL, lambda: nc.gpsimd.tensor_copy(out=rsum.t[:, half * 4:(half + 1) * 4], in_=osb.t[:, half * 260:(half + 1) * 260].rearrange("p (h d) -> p h d", h=4)[:, :, 64]),
                         reads=[osb_tr[half]], writes=[rsum_tr[half]])
                    P.op(POOL, lambda: nc.gpsimd.tensor_tensor(out=rinv.t[:, half * 4:(half + 1) * 4], in0=rsum.t[:, half * 4:(half + 1) * 4], in1=mones.t[:, 0:4], op=ALU.pow),
                         reads=[rsum_tr[half], mones.tr], writes=[rinv_tr[half]])
                for h in range(8):
                    oc = (h % 2) * 260 + (h // 2) * 65
                    ri = (h % 2) * 4 + h // 2
                    P.op(ACT, lambda: nc.scalar.activation(out=ya.t[:, h * 64:(h + 1) * 64], in_=osb.t[:, oc:oc + 64], func=AF.Identity, scale=rinv.t[:, ri:ri + 1]),
                         reads=[osb_tr[h % 2], rinv_tr[h % 2]], writes=[ya_tr[h]])
                if cstop < 6:
                    return
                q = PQ[0]
                for j in range(4):
                    P.op(PE, lambda: nc.tensor.transpose(out=q.t[:, j * 128:(j + 1) * 128], in_=ya.t[:, j * 128:(j + 1) * 128], identity=identb.t[:]),
                         reads=[ya_tr[2 * j], ya_tr[2 * j + 1], identb.tr], writes=[q.tr])
                P.op(ACT, lambda: nc.scalar.copy(out=yaT.t[:], in_=q.t[:, :512].rearrange("p (j t) -> p j t", j=4)), reads=[q.tr], writes=[yaT.tr])
                P.op(POOL, lambda: nc.gpsimd.tensor_tensor(out=yat[s].t[:], in0=yaT.t[:], in1=szat[s].t[:], op=ALU.mult),
                     reads=[yaT.tr, szat[s].tr], writes=[yat[s].tr])
                if cstop < 7:
                    return
                P.dma(POOL, yat[s].sem, YC[4:8, :, c0:c1].rearrange("j p t -> p j t"), yat[s].t[:], reads=[yat[s].tr])

            if NTC > 0:
                load_idx(0)
                if NTC > 1:
                    load_idx(1)
            load_kv()
            if NTC > 0:
                S1a(0)
            for i in range(NTC):
                if i + 2 < NTC:
                    load_idx(i + 2)
                load_att(i)
                if i + 1 < NTC:
                    S1a(i + 1)
                if i >= 1:
                    S2(i - 1)
                S1b(i)
                S1c(i)
            if NTC > 0:
                S2(NTC - 1)
            P.barrier()

        with ExitStack() as sD:
            yc = sb(sD, "yc", [128, 8, LP], BF16, True)
            wol = [sb(sD, f"wol{i}", [128, 1024], F32, True) for i in range(2)]
            wob = sb(sD, "wob", [128, 8, 1024], BF16)
            wob_tr = [Tr() for _ in range(8)]
            grep = sb(sD, "grep", [128, 1024], F32, True)
            brep = sb(sD, "brep", [128, 1024], F32, True)
            htl = [sb(sD, f"htl{i}", [128, 1024], F32, True) for i in range(3)]
            zt = [sb(sD, f"zt{i}", [128, 1024], F32) for i in range(3)]
            zn = [sb(sD, f"zn{i}", [128, 1024], F32) for i in range(3)]
            o1 = [sb(sD, f"o1{i}", [128, 1024], F32) for i in range(3)]
            ho = [sb(sD, f"ho{i}", [128, 1024], F32, True) for i in range(3)]
            st6 = [sb(sD, f"st6{i}", [128, 12], F32) for i in range(3)]
            mv = [sb(sD, f"mv{i}", [128, 2], F32) for i in range(3)]
            sdv = [sb(sD, f"sdv{i}", [128, 1], F32) for i in range(3)]
            rs = [sb(sD, f"rs{i}", [128, 1], F32) for i in range(3)]
            nmr = [sb(sD, f"nmr{i}", [128, 1], F32) for i in range(3)]

            for ec in range(8):
                P.dma(SP, yc.sem, yc.t[:, ec, :], YC[ec], writes=[yc.tr])
            P.dma(SP, grep.sem, grep.t[:], GB[l, 0], writes=[grep.tr])
            P.dma(SP, brep.sem, brep.t[:], GB[l, 1], writes=[brep.tr])
            for ec in range(8):
                s = ec % 2
                P.dma(SP, wol[s].sem, wol[s].t[:], WO[l, ec], writes=[wol[s].tr])
                P.op(POOL, lambda s=s, ec=ec: nc.gpsimd.tensor_copy(out=wob.t[:, ec, :], in_=wol[s].t[:]), reads=[wol[s].tr], writes=[wob_tr[ec]])
            for i in range(NT if "D" in phases else 0):
                s = i % 3
                P.dma(SP, htl[s].sem, htl[s].t[:], h_in[i * 128:(i + 1) * 128, :], writes=[htl[s].tr])
                for half in range(2):
                    b = PS[(i % 3) * 2 + half]
                    for ec in range(8):
                        P.op(PE, lambda b=b, ec=ec, half=half, i=i: nc.tensor.matmul(b.t[:, :], lhsT=yc.t[:, ec, i * 128:(i + 1) * 128], rhs=wob.t[:, ec, half * 512:(half + 1) * 512], start=(ec == 0), stop=(ec == 7)),
                             reads=[yc.tr, wob_tr[ec]], writes=[b.tr])
                    P.op(DVE, lambda b=b, half=half, s=s: nc.vector.scalar_tensor_tensor(out=zt[s].t[:, half * 512:(half + 1) * 512], in0=htl[s].t[:, half * 512:(half + 1) * 512], scalar=ALPHA, in1=b.t[:, :], op0=ALU.mult, op1=ALU.add),
                         reads=[htl[s].tr, b.tr], writes=[zt[s].tr])
                for half in range(2):
                    P.op(DVE, lambda half=half, s=s: nc.vector.bn_stats(out=st6[s].t[:, half * 6:(half + 1) * 6], in_=zt[s].t[:, half * 512:(half + 1) * 512]), reads=[zt[s].tr], writes=[st6[s].tr])
                P.op(DVE, lambda s=s: nc.vector.bn_aggr(out=mv[s].t[:], in_=st6[s].t[:]), reads=[st6[s].tr], writes=[mv[s].tr])
                P.op(ACT, lambda s=s: nc.scalar.activation(out=sdv[s].t[:], in_=mv[s].t[:, 1:2], func=AF.Sqrt, bias=epst.t[:, 0:1]), reads=[mv[s].tr, epst.tr], writes=[sdv[s].tr])
                P.op(DVE, lambda s=s: nc.vector.reciprocal(out=rs[s].t[:], in_=sdv[s].t[:]), reads=[sdv[s].tr], writes=[rs[s].tr])
                P.op(DVE, lambda s=s: nc.vector.scalar_tensor_tensor(out=nmr[s].t[:], in0=mv[s].t[:, 0:1], scalar=-1.0, in1=rs[s].t[:], op0=ALU.mult, op1=ALU.mult), reads=[mv[s].tr, rs[s].tr], writes=[nmr[s].tr])
                P.op(ACT, lambda s=s: nc.scalar.activation(out=zn[s].t[:], in_=zt[s].t[:], func=AF.Identity, scale=rs[s].t[:, 0:1], bias=nmr[s].t[:, 0:1]), reads=[zt[s].tr, rs[s].tr, nmr[s].tr], writes=[zn[s].tr])
                P.op(POOL, lambda s=s: nc.gpsimd.tensor_tensor(out=o1[s].t[:], in0=zn[s].t[:], in1=grep.t[:], op=ALU.mult), reads=[zn[s].tr, grep.tr], writes=[o1[s].tr])
                P.op(POOL, lambda s=s: nc.gpsimd.tensor_tensor(out=ho[s].t[:], in0=o1[s].t[:], in1=brep.t[:], op=ALU.add), reads=[o1[s].tr, brep.tr], writes=[ho[s].tr])
                P.dma(POOL, ho[s].sem, h_out[i * 128:(i + 1) * 128, :], ho[s].t[:], reads=[ho[s].tr])
            P.barrier()

    es.close()
    return nc, P.nins


def _rope_tables():
    inv_freq = (10000.0 ** (-np.arange(0, 64, 2, dtype=np.float32) / np.float32(64))).astype(np.float32)
    ang = np.arange(LP, dtype=np.float32)[:, None] * inv_freq[None, :]
    cos = np.cos(ang).astype(np.float32).T
    sin = np.sin(ang).astype(np.float32).T
    p = np.arange(128)
    d = p % 64
    cosT = cos[d % 32]
    sinT = np.where((d < 32)[:, None], -sin[d % 32], sin[d % 32])
    return np.ascontiguousarray(cosT, dtype=np.float32), np.ascontiguousarray(sinT, dtype=np.float32)


def _prep_weights(w_in, conv_w, conv_b, conv_ln_g, conv_ln_b, w_out, post_ln_g, post_ln_b):
    Ld = w_in.shape[0]
    cols = _weight_cols()
    WA = np.empty((Ld, NWCH, 128, 1024), np.float32)
    for l in range(Ld):
        for c, cc in enumerate(cols):
            blk = w_in[l][:, cc]
            WA[l, c] = blk.reshape(8, 128, 128).transpose(1, 0, 2).reshape(128, 1024)
        for hv in range(2):
            blk = w_in[l][:, O_V + hv * 256:O_V + (hv + 1) * 256]
            arr = blk.reshape(8, 128, 256).transpose(1, 0, 2).reshape(128, 2048)
            WA[l, len(cols) + 2 * hv] = arr[:, :1024]
            WA[l, len(cols) + 2 * hv + 1] = arr[:, 1024:]
    WWI = np.ascontiguousarray(w_in[:, :, O_WI:O_WI + 8].reshape(Ld, 8, 128, 8).transpose(0, 2, 1, 3).reshape(Ld, 128, 64))
    CPa = np.empty((Ld, 128, 136), np.float32)
    CPa[:, :, 0:124] = conv_w.reshape(Ld, 31, 4, 128).transpose(0, 3, 2, 1).reshape(Ld, 128, 124)
    CPa[:, :, 124:128] = conv_b.reshape(Ld, 4, 128).transpose(0, 2, 1)
    CPa[:, :, 128:132] = conv_ln_g.reshape(Ld, 4, 128).transpose(0, 2, 1)
    CPa[:, :, 132:136] = conv_ln_b.reshape(Ld, 4, 128).transpose(0, 2, 1)
    WOa = np.ascontiguousarray(w_out.reshape(Ld, 8, 128, 1024))
    GBa = np.empty((Ld, 2, 128, 1024), np.float32)
    GBa[:, 0] = post_ln_g[:, None, :]
    GBa[:, 1] = post_ln_b[:, None, :]
    return WA, WWI, CPa, WOa, GBa


def _consts():
    ident = np.eye(128, dtype=np.float32)
    t = np.arange(128)
    negmask = np.where(t[None, :] <= t[:, None], 0.0, NEG).astype(np.float32)
    cosT, sinT = _rope_tables()
    pwr = np.broadcast_to((0.5 ** np.arange(1, KBIS + 1)).astype(np.float32)[None, :], (128, KBIS)).copy()
    return {"c_ident": ident, "c_negmask": negmask, "c_cos": cosT, "c_sin": sinT, "c_pw": pwr}


_CACHE = {}
FUSED = True


def _get_prog(n_layers):
    if n_layers not in _CACHE:
        _CACHE[n_layers] = build_program(n_layers)[0]
    return _CACHE[n_layers]


def kernel(x, meta_tokens, w_in, conv_w, conv_b, conv_ln_g, conv_ln_b, w_out, post_ln_g, post_ln_b):
    x = np.asarray(x, np.float32)
    B = x.shape[0]
    f = lambda a: np.asarray(a, np.float32)
    WA, WWI, CPa, WOa, GBa = _prep_weights(f(w_in), f(conv_w), f(conv_b), f(conv_ln_g), f(conv_ln_b), f(w_out), f(post_ln_g), f(post_ln_b))
    consts = _consts()
    hs = []
    for b in range(B):
        h = np.zeros((LP, D_MODEL), np.float32)
        h[:N_META] = f(meta_tokens)
        h[N_META:N_META + SEQ] = x[b]
        hs.append(h)
    if FUSED:
        nc = _get_prog(DEPTH)
        in_maps = [dict(h0=hs[b], WA=WA, WWI=WWI, CP=CPa, WO=WOa, GB=GBa, **consts) for b in range(B)]
        res = run_bass_kernel_spmd(nc, in_maps, core_ids=list(range(B)))
        hs = [res.results[b]["hout"] for b in range(B)]
    else:
        nc = _get_prog(1)
        for l in range(DEPTH):
            in_maps = [dict(h0=hs[b], WA=WA[l:l + 1], WWI=WWI[l:l + 1], CP=CPa[l:l + 1], WO=WOa[l:l + 1], GB=GBa[l:l + 1], **consts) for b in range(B)]
            res = run_bass_kernel_spmd(nc, in_maps, core_ids=list(range(B)))
            hs = [res.results[b]["hout"] for b in range(B)]
    out = np.stack([hs[b][N_META:N_META + SEQ] for b in range(B)], axis=0)
    return np.ascontiguousarray(out, dtype=np.float32)
```

```python
import numpy as np
from contextlib import ExitStack
import concourse.bass as bass
import concourse.mybir as mybir
from concourse.bass_utils import run_bass_kernel_spmd

F32 = mybir.dt.float32
BF16 = mybir.dt.bfloat16
AF = mybir.ActivationFunctionType
ALU = mybir.AluOpType
AX = mybir.AxisListType

D_MODEL = 1024
SEQ = 4096
N_META = 16
LP = 4224
NT = LP // 128
DEPTH = 4
TOPK = 256
KBIS = 17
LN_EPS = 1e-5
ALPHA = (2.0 * DEPTH) ** 0.25
WI_SCALE = (64 ** -0.5) * (8 ** -0.5)
NEG = -1.0e30
TCH = [(i * 512, 512) for i in range(8)] + [(4096, 128)]

PJ_U, PJ_SZC, PJ_Q, PJ_K, PJ_SZA, PJ_QI, PJ_KI = 0, 4, 8, 12, 16, 20, 24
NPJ = 25
O_A, O_G, O_ZC, O_Q, O_K, O_V, O_ZA, O_QI, O_KI, O_WI = 0, 512, 1024, 1536, 2048, 2560, 3072, 3584, 4096, 4160


def _items():
    items = []
    c = 0
    for j in range(4):
        items.append(("conv", c, 2, PJ_U + j)); c += 2
    for j in range(4):
        items.append(("silu", c, 1, PJ_SZC + j)); c += 1
    for j in range(4):
        items.append(("rope", c, 2, PJ_Q + j)); c += 2
    for j in range(4):
        items.append(("rope", c, 2, PJ_K + j)); c += 2
    for j in range(4):
        items.append(("silu", c, 1, PJ_SZA + j)); c += 1
    for j in range(4):
        items.append(("rope", c, 2, PJ_QI + j)); c += 2
    items.append(("rope", c, 2, PJ_KI)); c += 2
    for hv in range(2):
        items.append(("v", c, 2, hv)); c += 2
    return items, c


ITEMS, NWCH = _items()


def _rot_cols(base, width):
    idx = np.arange(width)
    return base + (idx // 64) * 64 + ((idx % 64) + 32) % 64


def _weight_cols():
    cols = []
    for j in range(4):
        cols.append(O_A + j * 128 + np.arange(128)); cols.append(O_G + j * 128 + np.arange(128))
    for j in range(4):
        cols.append(O_ZC + j * 128 + np.arange(128))
    for base in (O_Q, O_K):
        for j in range(4):
            cols.append(base + j * 128 + np.arange(128))
            cols.append(_rot_cols(base, 512)[j * 128:(j + 1) * 128])
    za = [O_ZA + j * 128 + np.arange(128) for j in range(4)]
    qi = []
    for j in range(4):
        qi.append(O_QI + j * 128 + np.arange(128))
        qi.append(_rot_cols(O_QI, 512)[j * 128:(j + 1) * 128])
    cols = cols + za + qi
    kic = O_KI + np.arange(64)
    kir = _rot_cols(O_KI, 64)
    cols.append(np.concatenate([kic, kic])); cols.append(np.concatenate([kir, kir]))
    return cols


class Sem:
    __slots__ = ("h", "val")

    def __init__(self, h):
        self.h = h
        self.val = 0


class Tr:
    __slots__ = ("w", "r")

    def __init__(self):
        self.w = {}
        self.r = {}


class Eng:
    def __init__(self, name, e, sem, sync_self):
        self.name = name
        self.e = e
        self.sem = sem
        self.sync_self = sync_self
        self.waited = {}


class Prog:
    def __init__(self, nc, es):
        self.nc = nc
        self.es = es
        self.sems = []
        self.PE = Eng("pe", nc.tensor, self.new_sem("s_pe"), False)
        self.ACT = Eng("act", nc.scalar, self.new_sem("s_act"), True)
        self.DVE = Eng("dve", nc.vector, self.new_sem("s_dve"), True)
        self.POOL = Eng("pool", nc.gpsimd, self.new_sem("s_pool"), True)
        self.SP = Eng("sp", nc.sync, self.new_sem("s_sp"), False)
        self.engs = [self.PE, self.ACT, self.DVE, self.POOL, self.SP]
        self.nins = 0

    def new_sem(self, name):
        s = Sem(self.es.enter_context(self.nc.semaphore(name)))
        self.sems.append(s)
        return s

    def _wait(self, E, deps):
        for s, v in deps.items():
            if s is E.sem and not E.sync_self:
                continue
            if E.waited.get(s, 0) < v:
                E.e.wait_ge(s.h, v)
                E.waited[s] = v

    @staticmethod
    def _deps(reads, writes):
        deps = {}
        for t in reads:
            for s, v in t.w.items():
                if deps.get(s, 0) < v:
                    deps[s] = v
        for t in writes:
            for dd in (t.w, t.r):
                for s, v in dd.items():
                    if deps.get(s, 0) < v:
                        deps[s] = v
        return deps

    def op(self, E, ins_fn, reads=(), writes=()):
        self._wait(E, self._deps(reads, writes))
        ins = ins_fn()
        E.sem.val += 1
        ins.then_inc(E.sem.h, 1)
        v = E.sem.val
        for t in writes:
            t.w = {E.sem: v}
            t.r = {}
        for t in reads:
            t.r[E.sem] = v
        self.nins += 1

    def dma(self, Q, sem, out, in_, reads=(), writes=()):
        deps = self._deps(reads, writes)
        deps.pop(sem, None)
        self._wait(Q, deps)
        ins = Q.e.dma_start(out=out, in_=in_)
        sem.val += 16
        ins.then_inc(sem.h, 16)
        for t in writes:
            t.w = {sem: sem.val}
            t.r = {}
        for t in reads:
            t.r[sem] = sem.val
        self.nins += 1

    def barrier(self):
        allv = {s: s.val for s in self.sems if s.val > 0}
        for E in self.engs:
            for s, v in allv.items():
                if s is E.sem:
                    continue
                if E.waited.get(s, 0) < v:
                    E.e.wait_ge(s.h, v)
                    E.waited[s] = v


class Buf:
    def __init__(self, t, tr=None, sem=None):
        self.t = t
        self.tr = tr if tr is not None else Tr()
        self.sem = sem


def build_program(n_layers, dbg=False, phases="ABCD", nitems=None, ntiles=None, cstop=9):
    nc = bass.Bass("TRN2", target_bir_lowering=False)
    es = ExitStack()
    P = Prog(nc, es)
    PE, ACT, DVE, POOL, SP = P.PE, P.ACT, P.DVE, P.POOL, P.SP
    L = n_layers

    def din(name, shape, dt=F32):
        return nc.dram_tensor(name, shape, dt, kind="ExternalInput").ap()

    skind = "ExternalOutput" if dbg else "Internal"
    h0 = din("h0", [LP, D_MODEL])
    WA = din("WA", [L, NWCH, 128, 1024])
    WWI = din("WWI", [L, 128, 64])
    CP = din("CP", [L, 128, 136])
    WO = din("WO", [L, 8, 128, 1024])
    GB = din("GB", [L, 2, 128, 1024])
    c_ident = din("c_ident", [128, 128])
    c_negmask = din("c_negmask", [128, 128])
    c_cos = din("c_cos", [128, LP])
    c_sin = din("c_sin", [128, LP])
    c_pw = din("c_pw", [128, KBIS])
    hout = nc.dram_tensor("hout", [LP, D_MODEL], F32, kind="ExternalOutput").ap()
    hbufs = [nc.dram_tensor(f"hbuf{i}", [LP, D_MODEL], F32, kind="Internal").ap() for i in range(2)] if L > 1 else []
    PJ = nc.dram_tensor("PJ", [NPJ, 128, LP], BF16, kind=skind).ap()
    VA = nc.dram_tensor("VA", [NT, 128, 520], BF16, kind=skind).ap()
    WI = nc.dram_tensor("WI", [128, NT * 8], F32, kind=skind).ap()
    YC = nc.dram_tensor("YC", [8, 128, LP], BF16, kind=skind).ap()

    semcache = {}
    cur = {"l": "g"}

    def sb(stack, name, shape, dt, dma_sem=False):
        t = stack.enter_context(nc.sbuf_tensor(f"{name}_{cur['l']}", shape, dt))
        sem = None
        if dma_sem:
            if name not in semcache:
                semcache[name] = P.new_sem("d_" + name)
            sem = semcache[name]
        return Buf(t, sem=sem)

    PS = [Buf(es.enter_context(nc.psum_tensor(f"ps{i}", [128, 512], F32))) for i in range(6)]
    PQ = [Buf(es.enter_context(nc.psum_tensor(f"pq{i}", [128, 1024], BF16))) for i in range(2)]
    identf = sb(es, "identf", [128, 128], F32, True)
    identb = sb(es, "identb", [128, 128], BF16)
    negmask = sb(es, "negmask", [128, 128], F32, True)
    pw = sb(es, "pw", [128, KBIS], F32, True)
    epst = sb(es, "epst", [128, 1], F32)
    thrneg = sb(es, "thrneg", [128, 1], F32)
    onesm = sb(es, "onesm", [128, 128], F32)
    mones = sb(es, "mones", [128, 8], F32)

    P.dma(SP, identf.sem, identf.t[:], c_ident[:, :], writes=[identf.tr])
    P.dma(SP, negmask.sem, negmask.t[:], c_negmask[:, :], writes=[negmask.tr])
    P.dma(SP, pw.sem, pw.t[:], c_pw[:, :], writes=[pw.tr])
    P.op(DVE, lambda: nc.vector.tensor_copy(out=identb.t[:], in_=identf.t[:]), reads=[identf.tr], writes=[identb.tr])
    P.op(DVE, lambda: nc.vector.memset(epst.t[:], LN_EPS), writes=[epst.tr])
    P.op(DVE, lambda: nc.vector.memset(thrneg.t[:], -1.0e29), writes=[thrneg.tr])
    P.op(DVE, lambda: nc.vector.memset(onesm.t[:], 1.0 / 512.0), writes=[onesm.tr])
    P.op(DVE, lambda: nc.vector.memset(mones.t[:], -1.0), writes=[mones.tr])

    cnt = {"ps": 0}
    LB = [PS[0], PS[1], PS[2], Buf(PQ[1].t[:, :].bitcast(F32), tr=PQ[1].tr)]

    def next_ps3():
        b = PS[cnt["ps"] % 3]
        cnt["ps"] += 1
        return b

    for l in range(L):
        cur["l"] = f"L{l}"
        h_in = h0 if l == 0 else hbufs[(l - 1) % 2]
        h_out = hout if l == L - 1 else hbufs[l % 2]

        with ExitStack() as sA:
            hT = sb(sA, "hT", [128, 8, LP], BF16)
            hT_tr = [Tr() for _ in range(NT)]
            with ExitStack() as s0:
                hld = [sb(s0, f"hld{i}", [128, 1024], F32, True) for i in range(2)]
                hb = [sb(s0, f"hb{i}", [128, 1024], BF16) for i in range(2)]
                for i in range(NT):
                    s = i % 2
                    P.dma(SP, hld[s].sem, hld[s].t[:], h_in[i * 128:(i + 1) * 128, :], writes=[hld[s].tr])
                    if i % 2 == 0:
                        P.op(ACT, lambda s=s: nc.scalar.copy(out=hb[s].t[:], in_=hld[s].t[:]), reads=[hld[s].tr], writes=[hb[s].tr])
                    else:
                        P.op(DVE, lambda s=s: nc.vector.tensor_copy(out=hb[s].t[:], in_=hld[s].t[:]), reads=[hld[s].tr], writes=[hb[s].tr])
                    q = PQ[i % 2]
                    for kc in range(8):
                        P.op(PE, lambda s=s, kc=kc, q=q: nc.tensor.transpose(out=q.t[:, kc * 128:(kc + 1) * 128], in_=hb[s].t[:, kc * 128:(kc + 1) * 128], identity=identb.t[:]),
                             reads=[hb[s].tr, identb.tr], writes=[q.tr])
                    src = q.t[:, :].rearrange("p (k t) -> p k t", k=8)
                    if i % 2 == 0:
                        P.op(DVE, lambda i=i, src=src: nc.vector.tensor_copy(out=hT.t[:, :, i * 128:(i + 1) * 128], in_=src), reads=[q.tr], writes=[hT_tr[i]])
                    else:
                        P.op(ACT, lambda i=i, src=src: nc.scalar.copy(out=hT.t[:, :, i * 128:(i + 1) * 128], in_=src), reads=[q.tr], writes=[hT_tr[i]])
                P.barrier()

            with ExitStack() as s1:
                if "A" not in phases:
                    ITEMS_ = []
                else:
                    ITEMS_ = ITEMS if nitems is None else ITEMS[:nitems]
                cosT = sb(s1, "cosT", [128, LP], F32, True)
                sinT = sb(s1, "sinT", [128, LP], F32, True)
                wld = [sb(s1, f"wld{i}", [128, 2048], F32, True) for i in range(2)]
                wbf = [sb(s1, f"wbf{i}", [128, 2048], BF16) for i in range(2)]
                stage = [sb(s1, f"stage{i}", [128, LP], BF16, True) for i in range(2)]
                sig = [sb(s1, f"sig{i}", [128, 512], F32) for i in range(2)]
                tm1 = [sb(s1, f"tm1{i}", [128, 512], F32) for i in range(2)]
                tm2 = [sb(s1, f"tm2{i}", [128, 512], F32) for i in range(2)]
                vst = [sb(s1, f"vst{i}", [128, 8, 4, 65], BF16, True) for i in range(2)]
                wwif = sb(s1, "wwif", [128, 64], F32, True)
                wwib = sb(s1, "wwib", [128, 64], BF16)
                wisb = sb(s1, "wisb", [128, NT * 8], F32, True)

                P.dma(SP, wwif.sem, wwif.t[:], WWI[l], writes=[wwif.tr])
                P.op(POOL, lambda: nc.gpsimd.tensor_copy(out=wwib.t[:], in_=wwif.t[:]), reads=[wwif.tr], writes=[wwib.tr])
                for s in range(2):
                    P.op(POOL, lambda s=s: nc.gpsimd.memset(vst[s].t[:], 1.0), writes=[vst[s].tr])

                def load_w(it_idx):
                    kind, c0, nch, _ = ITEMS[it_idx]
                    s = it_idx % 2
                    P.dma(SP, wld[s].sem, wld[s].t[:, :nch * 1024].rearrange("p (c f) -> p c f", c=nch), WA[l, c0:c0 + nch].rearrange("c p f -> p c f"), writes=[wld[s].tr])
                    P.op(POOL, lambda s=s, nch=nch: nc.gpsimd.tensor_copy(out=wbf[s].t[:, :nch * 1024], in_=wld[s].t[:, :nch * 1024]),
                         reads=[wld[s].tr], writes=[wbf[s].tr])

                if ITEMS_:
                    load_w(0)
                P.dma(SP, cosT.sem, cosT.t[:], c_cos[:, :], writes=[cosT.tr])
                P.dma(SP, sinT.sem, sinT.t[:], c_sin[:, :], writes=[sinT.tr])
                gctr = 0
                st_ctr = 0
                for it_idx, (kind, c0, nch, pj) in enumerate(ITEMS_):
                    if it_idx + 1 < len(ITEMS_):
                        load_w(it_idx + 1)
                    ws = it_idx % 2
                    wv_ = wbf[ws]
                    if kind != "v":
                        stg = stage[st_ctr % 2]
                        st_ctr += 1
                        for (t0, n) in TCH:
                            bA = PS[(gctr % 2) * 2]
                            bB = PS[(gctr % 2) * 2 + 1]
                            g2 = gctr % 2
                            gctr += 1
                            hts = hT_tr[t0 // 128:(t0 + n) // 128]
                            for kc in range(8):
                                P.op(PE, lambda kc=kc, bA=bA, t0=t0, n=n: nc.tensor.matmul(bA.t[:, :n], lhsT=wv_.t[:, kc * 128:(kc + 1) * 128], rhs=hT.t[:, kc, t0:t0 + n], start=(kc == 0), stop=(kc == 7)),
                                     reads=[wv_.tr] + hts, writes=[bA.tr])
                            if nch == 2:
                                for kc in range(8):
                                    P.op(PE, lambda kc=kc, bB=bB, t0=t0, n=n: nc.tensor.matmul(bB.t[:, :n], lhsT=wv_.t[:, 1024 + kc * 128:1024 + (kc + 1) * 128], rhs=hT.t[:, kc, t0:t0 + n], start=(kc == 0), stop=(kc == 7)),
                                         reads=[wv_.tr] + hts, writes=[bB.tr])
                            if kind == "conv":
                                P.op(ACT, lambda bB=bB, g2=g2, n=n: nc.scalar.activation(out=sig[g2].t[:, :n], in_=bB.t[:, :n], func=AF.Sigmoid), reads=[bB.tr], writes=[sig[g2].tr])
                                P.op(DVE, lambda bA=bA, g2=g2, t0=t0, n=n, stg=stg: nc.vector.tensor_tensor(out=stg.t[:, t0:t0 + n], in0=bA.t[:, :n], in1=sig[g2].t[:, :n], op=ALU.mult),
                                     reads=[bA.tr, sig[g2].tr], writes=[stg.tr])
                            elif kind == "silu":
                                P.op(ACT, lambda bA=bA, t0=t0, n=n, stg=stg: nc.scalar.activation(out=stg.t[:, t0:t0 + n], in_=bA.t[:, :n], func=AF.Silu), reads=[bA.tr], writes=[stg.tr])
                            else:
                                P.op(DVE, lambda bA=bA, g2=g2, t0=t0, n=n: nc.vector.tensor_tensor(out=tm1[g2].t[:, :n], in0=bA.t[:, :n], in1=cosT.t[:, t0:t0 + n], op=ALU.mult),
                                     reads=[bA.tr, cosT.tr], writes=[tm1[g2].tr])
                                P.op(DVE, lambda bB=bB, g2=g2, t0=t0, n=n: nc.vector.tensor_tensor(out=tm2[g2].t[:, :n], in0=bB.t[:, :n], in1=sinT.t[:, t0:t0 + n], op=ALU.mult),
                                     reads=[bB.tr, sinT.tr], writes=[tm2[g2].tr])
                                P.op(POOL, lambda g2=g2, t0=t0, n=n, stg=stg: nc.gpsimd.tensor_tensor(out=stg.t[:, t0:t0 + n], in0=tm1[g2].t[:, :n], in1=tm2[g2].t[:, :n], op=ALU.add),
                                     reads=[tm1[g2].tr, tm2[g2].tr], writes=[stg.tr])
                        P.dma(POOL, stg.sem, PJ[pj], stg.t[:], reads=[stg.tr])
                    else:
                        hv = pj
                        wv3 = wv_.t[:, :].rearrange("p (k e) -> p k e", k=8)
                        for i in range(NT):
                            b = PS[(gctr % 2) * 2]
                            gctr += 1
                            gi, gs = i // 8, (i // 8) % 2
                            for kc in range(8):
                                P.op(PE, lambda kc=kc, b=b, i=i: nc.tensor.matmul(b.t[:, :256], lhsT=hT.t[:, kc, i * 128:(i + 1) * 128], rhs=wv3[:, kc, :], start=(kc == 0), stop=(kc == 7)),
                                     reads=[wv_.tr, hT_tr[i]], writes=[b.tr])
                            P.op(ACT, lambda b=b, i=i, gs=gs: nc.scalar.copy(out=vst[gs].t[:, i % 8, :, 0:64], in_=b.t[:, :256].rearrange("p (h d) -> p h d", h=4)),
                                 reads=[b.tr], writes=[vst[gs].tr])
                            if i % 8 == 7 or i == NT - 1:
                                ng = i % 8 + 1
                                i0 = gi * 8
                                dst = VA[i0:i0 + ng].rearrange("i p (v f) -> p i v f", v=2)[:, :, hv, :]
                                P.dma(POOL, vst[gs].sem, dst, vst[gs].t[:, :ng].rearrange("p i h d -> p i (h d)"), reads=[vst[gs].tr])
                            if hv == 0:
                                b5 = PS[4 + (i % 2)]
                                for kc in range(8):
                                    P.op(PE, lambda kc=kc, b5=b5, i=i: nc.tensor.matmul(b5.t[:, :8], lhsT=hT.t[:, kc, i * 128:(i + 1) * 128], rhs=wwib.t[:, kc * 8:(kc + 1) * 8], start=(kc == 0), stop=(kc == 7)),
                                         reads=[wwib.tr, hT_tr[i]], writes=[b5.tr])
                                P.op(DVE, lambda b5=b5, i=i: nc.vector.tensor_scalar(out=wisb.t[:, i * 8:(i + 1) * 8], in0=b5.t[:, :8], scalar1=WI_SCALE, scalar2=None, op0=ALU.mult),
                                     reads=[b5.tr], writes=[wisb.tr])
                P.dma(POOL, wisb.sem, WI[:, :], wisb.t[:], reads=[wisb.tr])
                P.barrier()

        with ExitStack() as sB:
            upad = [sb(sB, f"upad{j}", [128, 30 + LP], BF16, True) for j in range(4)]
            cp = sb(sB, "cp", [128, 136], F32, True)
            dg = sb(sB, "dg", [128, 4 * 31, 128], BF16)
            szc = [sb(sB, f"szc{i}", [128, 4, 512], BF16, True) for i in range(2)]
            yst = [sb(sB, f"yst{i}", [128, 4, 512], BF16, True) for i in range(2)]
            cf = [sb(sB, f"cf{j}", [128, 512], F32) for j in range(4)]
            sq = [sb(sB, f"sq{j}", [128, 512], F32) for j in range(4)]
            mean_sb = sb(sB, "mean_sb", [128, 512], F32)
            msq = sb(sB, "msq", [128, 512], F32)
            var = sb(sB, "var", [128, 512], F32)
            sd = sb(sB, "sd", [128, 512], F32)
            rstd = sb(sB, "rstd", [128, 512], F32)
            y1 = [sb(sB, f"y1{i}", [128, 512], F32) for i in range(2)]
            y2 = [sb(sB, f"y2{i}", [128, 512], F32) for i in range(2)]
            zz = [sb(sB, f"zz{i}", [128, 512], F32) for i in range(2)]

            P.dma(SP, cp.sem, cp.t[:], CP[l], writes=[cp.tr])
            for j in range(4):
                P.op(POOL, lambda j=j: nc.gpsimd.memset(upad[j].t[:, 0:30], 0.0), writes=[upad[j].tr])
                P.dma(SP, upad[j].sem, upad[j].t[:, 30:], PJ[PJ_U + j], reads=[upad[j].tr], writes=[upad[j].tr])
            dg_tr = [Tr() for _ in range(4 * 31)]
            for j in range(4):
                for tap in range(31):
                    di = j * 31 + tap
                    if di % 2 == 0:
                        P.op(DVE, lambda di=di: nc.vector.tensor_scalar(out=dg.t[:, di, :], in0=identb.t[:], scalar1=cp.t[:, di:di + 1], scalar2=None, op0=ALU.mult),
                             reads=[identb.tr, cp.tr], writes=[dg_tr[di]])
                    else:
                        P.op(POOL, lambda di=di: nc.gpsimd.tensor_scalar(out=dg.t[:, di, :], in0=identb.t[:], scalar1=cp.t[:, di:di + 1], scalar2=1.0, op0=ALU.mult, op1=ALU.mult),
                             reads=[identb.tr, cp.tr], writes=[dg_tr[di]])
            for ci, (t0, n) in enumerate(TCH if "B" in phases else []):
                s = ci % 2
                P.dma(SP, szc[s].sem, szc[s].t[:, :, :n], PJ[PJ_SZC:PJ_SZC + 4, :, t0:t0 + n].rearrange("j p t -> p j t"), writes=[szc[s].tr])
                for j in range(4):
                    for tap in range(31):
                        P.op(PE, lambda j=j, tap=tap, t0=t0, n=n: nc.tensor.matmul(PS[j].t[:, :n], lhsT=dg.t[:, j * 31 + tap, :], rhs=upad[j].t[:, t0 + tap:t0 + tap + n], start=(tap == 0), stop=(tap == 30)),
                             reads=[dg_tr[j * 31 + tap], upad[j].tr], writes=[PS[j].tr])
                for j in range(4):
                    P.op(ACT, lambda j=j, n=n: nc.scalar.activation(out=cf[j].t[:, :n], in_=PS[j].t[:, :n], func=AF.Identity, bias=cp.t[:, 124 + j:125 + j]), reads=[PS[j].tr, cp.tr], writes=[cf[j].tr])
                    P.op(ACT, lambda j=j, n=n: nc.scalar.activation(out=sq[j].t[:, :n], in_=PS[j].t[:, :n], func=AF.Square, bias=cp.t[:, 124 + j:125 + j]), reads=[PS[j].tr, cp.tr], writes=[sq[j].tr])
                for j in range(4):
                    P.op(PE, lambda j=j, n=n: nc.tensor.matmul(PS[4].t[:, :n], lhsT=onesm.t[:], rhs=cf[j].t[:, :n], start=(j == 0), stop=(j == 3)), reads=[onesm.tr, cf[j].tr], writes=[PS[4].tr])
                for j in range(4):
                    P.op(PE, lambda j=j, n=n: nc.tensor.matmul(PS[5].t[:, :n], lhsT=onesm.t[:], rhs=sq[j].t[:, :n], start=(j == 0), stop=(j == 3)), reads=[onesm.tr, sq[j].tr], writes=[PS[5].tr])
                P.op(ACT, lambda n=n: nc.scalar.copy(out=mean_sb.t[:, :n], in_=PS[4].t[:, :n]), reads=[PS[4].tr], writes=[mean_sb.tr])
                P.op(ACT, lambda n=n: nc.scalar.activation(out=msq.t[:, :n], in_=PS[4].t[:, :n], func=AF.Square), reads=[PS[4].tr], writes=[msq.tr])
                P.op(DVE, lambda n=n: nc.vector.tensor_tensor(out=var.t[:, :n], in0=PS[5].t[:, :n], in1=msq.t[:, :n], op=ALU.subtract), reads=[PS[5].tr, msq.tr], writes=[var.tr])
                P.op(ACT, lambda n=n: nc.scalar.activation(out=sd.t[:, :n], in_=var.t[:, :n], func=AF.Sqrt, bias=epst.t[:, 0:1]), reads=[var.tr, epst.tr], writes=[sd.tr])
                P.op(DVE, lambda n=n: nc.vector.reciprocal(out=rstd.t[:, :n], in_=sd.t[:, :n]), reads=[sd.tr], writes=[rstd.tr])
                for j in range(4):
                    k2 = j % 2
                    P.op(DVE, lambda j=j, n=n, k2=k2: nc.vector.tensor_tensor(out=y1[k2].t[:, :n], in0=cf[j].t[:, :n], in1=mean_sb.t[:, :n], op=ALU.subtract), reads=[cf[j].tr, mean_sb.tr], writes=[y1[k2].tr])
                    P.op(POOL, lambda n=n, k2=k2: nc.gpsimd.tensor_tensor(out=y2[k2].t[:, :n], in0=y1[k2].t[:, :n], in1=rstd.t[:, :n], op=ALU.mult), reads=[y1[k2].tr, rstd.tr], writes=[y2[k2].tr])
                    P.op(ACT, lambda j=j, n=n, k2=k2: nc.scalar.activation(out=zz[k2].t[:, :n], in_=y2[k2].t[:, :n], func=AF.Silu, scale=cp.t[:, 128 + j:129 + j], bias=cp.t[:, 132 + j:133 + j]),
                         reads=[y2[k2].tr, cp.tr], writes=[zz[k2].tr])
                    P.op(DVE, lambda j=j, n=n, k2=k2, s=s: nc.vector.tensor_tensor(out=yst[s].t[:, j, :n], in0=zz[k2].t[:, :n], in1=szc[s].t[:, j, :n], op=ALU.mult), reads=[zz[k2].tr, szc[s].tr], writes=[yst[s].tr])
                P.dma(POOL, yst[s].sem, YC[0:4, :, t0:t0 + n].rearrange("j p t -> p j t"), yst[s].t[:, :, :n], reads=[yst[s].tr])
            P.barrier()

        with ExitStack() as sC:
            kT = sb(sC, "kT", [128, 4, LP], BF16, True)
            kiT = sb(sC, "kiT", [128, LP], BF16, True)
            vaug = sb(sC, "vaug", [128, NT, 520], BF16, True)
            wi_all = sb(sC, "wi_all", [128, NT * 8], F32, True)
            qt = [sb(sC, f"qt{i}", [128, 4, 128], BF16, True) for i in range(2)]
            qit = [sb(sC, f"qit{i}", [128, 4, 128], BF16, True) for i in range(2)]
            szat = [sb(sC, f"szat{i}", [128, 4, 128], BF16, True) for i in range(2)]
            dgw = [sb(sC, f"dgw{i}", [128, 8, 128], BF16) for i in range(2)]
            score = [sb(sC, f"score{i}", [128, LP], F32) for i in range(2)]
            score_tr = [[Tr() for _ in range(9)] for _ in range(2)]
            junk = sb(sC, "junk", [128, LP], mybir.dt.uint8)
            mask = sb(sC, "mask", [128, LP], BF16)
            maskT = [sb(sC, f"maskT{i}", [128, LP], BF16) for i in range(2)]
            maskT_tr = [[Tr() for _ in range(5)] for _ in range(2)]
            Rr = [sb(sC, f"Rr{i}", [128, 512], BF16) for i in range(8)]
            Eb = [sb(sC, f"Eb{i}", [128, 512], BF16) for i in range(4)]
            PTb = [sb(sC, f"PTb{i}", [128, 512], BF16) for i in range(6)]
            hi = sb(sC, "hi", [128, 1], F32)
            lo = sb(sC, "lo", [128, 1], F32)
            w0 = sb(sC, "w0", [128, 1], F32)
            Wk = sb(sC, "Wk", [128, KBIS], F32)
            mid = [sb(sC, f"mid{i}", [128, 1], F32) for i in range(2)]
            cntb = sb(sC, "cntb", [128, 1], F32)
            dd = sb(sC, "dd", [128, 1], F32)
            rinv = sb(sC, "rinv", [128, 8], F32)
            rinv_tr = [Tr(), Tr()]
            rsum = sb(sC, "rsum", [128, 8], F32)
            rsum_tr = [Tr(), Tr()]
            osb = sb(sC, "osb", [128, 520], F32)
            osb_tr = [Tr(), Tr()]
            ya = sb(sC, "ya", [128, 512], BF16)
            ya_tr = [Tr() for _ in range(8)]
            yaT = sb(sC, "yaT", [128, 4, 128], BF16)
            yat = [sb(sC, f"yat{i}", [128, 4, 128], BF16, True) for i in range(2)]

            P.dma(SP, kiT.sem, kiT.t[:], PJ[PJ_KI], writes=[kiT.tr])
            P.dma(SP, wi_all.sem, wi_all.t[:], WI[:, :], writes=[wi_all.tr])

            def load_kv():
                for j in range(4):
                    P.dma(SP, kT.sem, kT.t[:, j, :], PJ[PJ_K + j], writes=[kT.tr])
                for i0 in range(0, NT, 11):
                    P.dma(SP, vaug.sem, vaug.t[:, i0:i0 + 11, :], VA[i0:i0 + 11].rearrange("i p f -> p i f"), writes=[vaug.tr])

            ctr = {"r": 0, "e": 0, "m": 0}
            NTC = (NT if ntiles is None else ntiles) if "C" in phases else 0

            def load_idx(i):
                s = i % 2
                c0, c1 = i * 128, (i + 1) * 128
                P.dma(SP, qit[s].sem, qit[s].t[:], PJ[PJ_QI:PJ_QI + 4, :, c0:c1].rearrange("j p t -> p j t"), writes=[qit[s].tr])

            def load_att(i):
                s = i % 2
                c0, c1 = i * 128, (i + 1) * 128
                P.dma(SP, qt[s].sem, qt[s].t[:], PJ[PJ_Q:PJ_Q + 4, :, c0:c1].rearrange("j p t -> p j t"), writes=[qt[s].tr])
                P.dma(SP, szat[s].sem, szat[s].t[:], PJ[PJ_SZA:PJ_SZA + 4, :, c0:c1].rearrange("j p t -> p j t"), writes=[szat[s].tr])

            def S1a(i):
                s = i % 2
                N = 128 * (i + 1)
                sc, sctr = score[s], score_tr[s]
                for h in range(8):
                    P.op(POOL, lambda h=h: nc.gpsimd.tensor_scalar(out=dgw[s].t[:, h, :], in0=identb.t[:], scalar1=wi_all.t[:, i * 8 + h:i * 8 + h + 1], scalar2=1.0, op0=ALU.mult, op1=ALU.mult),
                         reads=[identb.tr, wi_all.tr], writes=[dgw[s].tr])
                chunks = [(s0, min(512, N - s0)) for s0 in range(0, N, 512)]
                pendD = []

                def flush_diag():
                    c_, grp_, rbs_, s0_, n_ = pendD.pop(0)
                    for (h, rb) in rbs_:
                        P.op(PE, lambda: nc.tensor.matmul(PS[3].t[:, :n_], lhsT=dgw[s].t[:, h, :], rhs=rb.t[:, :n_], start=(h == 0), stop=(h == 7)),
                             reads=[dgw[s].tr, rb.tr], writes=[PS[3].tr])
                    if grp_ == 1:
                        P.op(ACT, lambda: nc.scalar.copy(out=sc.t[:, s0_:s0_ + n_], in_=PS[3].t[:, :n_]), reads=[PS[3].tr], writes=[sctr[c_]])

                for c, (s0, n) in enumerate(chunks):
                    for grp in range(2):
                        rbs = []
                        for hh in range(4):
                            h = grp * 4 + hh
                            Lb = LB[hh]
                            po = (h % 2) * 64
                            P.op(PE, lambda: nc.tensor.matmul(Lb.t[:, :n], lhsT=qit[s].t[po:po + 64, h // 2, :], rhs=kiT.t[po:po + 64, s0:s0 + n], start=True, stop=True),
                                 reads=[qit[s].tr, kiT.tr], writes=[Lb.tr])
                            rb = Rr[ctr["r"] % 8]
                            ctr["r"] += 1
                            P.op(ACT, lambda: nc.scalar.activation(out=rb.t[:, :n], in_=Lb.t[:, :n], func=AF.Relu), reads=[Lb.tr], writes=[rb.tr])
                            rbs.append((h, rb))
                        if pendD:
                            flush_diag()
                        pendD.append((c, grp, rbs, s0, n))
                while pendD:
                    flush_diag()
                nsc = len(chunks)
                P.op(POOL, lambda: nc.gpsimd.tensor_tensor(out=sc.t[:, N - 128:N], in0=sc.t[:, N - 128:N], in1=negmask.t[:], op=ALU.add),
                     reads=[sctr[nsc - 1], negmask.tr], writes=[sctr[nsc - 1]])

            def S1b(i):
                s = i % 2
                N = 128 * (i + 1)
                sc = score[s]
                sc_trs = score_tr[s][:(N + 511) // 512]
                if cstop < 2:
                    return
                if i < 2:
                    thr = thrneg
                else:
                    P.op(DVE, lambda: nc.vector.tensor_reduce(out=hi.t[:], in_=sc.t[:, :N], axis=AX.X, op=ALU.max), reads=sc_trs, writes=[hi.tr])
                    P.op(DVE, lambda: nc.vector.tensor_reduce(out=lo.t[:], in_=sc.t[:, :N - 128], axis=AX.X, op=ALU.min), reads=sc_trs, writes=[lo.tr])
                    P.op(DVE, lambda: nc.vector.tensor_tensor(out=w0.t[:], in0=hi.t[:], in1=lo.t[:], op=ALU.subtract), reads=[hi.tr, lo.tr], writes=[w0.tr])
                    P.op(DVE, lambda: nc.vector.tensor_scalar(out=Wk.t[:], in0=pw.t[:], scalar1=w0.t[:, 0:1], scalar2=None, op0=ALU.mult), reads=[pw.tr, w0.tr], writes=[Wk.tr])
                    P.op(DVE, lambda: nc.vector.tensor_tensor(out=mid[0].t[:], in0=lo.t[:], in1=Wk.t[:, 0:1], op=ALU.add), reads=[lo.tr, Wk.tr], writes=[mid[0].tr])
                    for k in range(KBIS):
                        mc, mn = mid[k % 2], mid[(k + 1) % 2]
                        kb = k + 1 if k < KBIS - 1 else k
                        P.op(DVE, lambda: nc.vector.tensor_scalar(out=junk.t[:, :N], in0=sc.t[:, :N], scalar1=mc.t[:, 0:1], scalar2=None, op0=ALU.is_ge, op1=ALU.add, accum_out=cntb.t[:, 0:1]),
                             reads=sc_trs + [mc.tr], writes=[junk.tr, cntb.tr])
                        P.op(DVE, lambda: nc.vector.tensor_scalar(out=dd.t[:], in0=cntb.t[:], scalar1=TOPK - 0.5, scalar2=Wk.t[:, k:k + 1], op0=ALU.is_ge, op1=ALU.mult),
                             reads=[cntb.tr, Wk.tr], writes=[dd.tr])
                        P.op(DVE, lambda: nc.vector.scalar_tensor_tensor(out=mn.t[:], in0=dd.t[:], scalar=Wk.t[:, kb:kb + 1], in1=mc.t[:], op0=ALU.subtract, op1=ALU.add),
                             reads=[dd.tr, Wk.tr, mc.tr], writes=[mn.tr])
                    thr = mid[KBIS % 2]
                P.op(DVE, lambda: nc.vector.tensor_scalar(out=mask.t[:, :N], in0=sc.t[:, :N], scalar1=thr.t[:, 0:1], scalar2=None, op0=ALU.is_ge),
                     reads=sc_trs + [thr.tr], writes=[mask.tr])

            def S1c(i):
                if cstop < 3:
                    return
                s = i % 2
                nb = i + 1
                for g in range((nb + 7) // 8):
                    q = PQ[0]
                    m = min(8, nb - g * 8)
                    for bi in range(m):
                        b = g * 8 + bi
                        P.op(PE, lambda: nc.tensor.transpose(out=q.t[:, bi * 128:(bi + 1) * 128], in_=mask.t[:, b * 128:(b + 1) * 128], identity=identb.t[:]),
                             reads=[mask.tr, identb.tr], writes=[q.tr])
                    P.op(ACT, lambda: nc.scalar.copy(out=maskT[s].t[:, g * 1024:g * 1024 + m * 128], in_=q.t[:, :m * 128]), reads=[q.tr], writes=[maskT_tr[s][g]])

            def S2(i):
                if cstop < 4:
                    return
                s = i % 2
                nb = i + 1
                c0, c1 = i * 128, (i + 1) * 128
                mT, mTtr = maskT[s], maskT_tr[s]
                units = [(hp, b0, min(4, nb - b0)) for hp in range(4) for b0 in range(0, nb, 4)]
                pend = []

                def pv(hp, b0, m, ptbs):
                    for hi_, ptb in enumerate(ptbs):
                        h = 2 * hp + hi_
                        Ob = PS[4 + hi_]
                        hh = hp
                        for bi in range(m):
                            b = b0 + bi
                            P.op(PE, lambda: nc.tensor.matmul(Ob.t[:, hh * 65:(hh + 1) * 65], lhsT=ptb.t[:, bi * 128:(bi + 1) * 128], rhs=vaug.t[:, b, h * 65:(h + 1) * 65], start=(b == 0), stop=(b == nb - 1)),
                                 reads=[ptb.tr, vaug.tr], writes=[Ob.tr])

                for (hp, b0, m) in units:
                    Sbs = [LB[ctr["m"] % 4], LB[(ctr["m"] + 1) % 4]]
                    ctr["m"] += 2
                    for bi in range(m):
                        b = b0 + bi
                        for hi_ in range(2):
                            po = hi_ * 64
                            Sb = Sbs[hi_]
                            P.op(PE, lambda: nc.tensor.matmul(Sb.t[:, bi * 128:(bi + 1) * 128], lhsT=kT.t[po:po + 64, hp, b * 128:(b + 1) * 128], rhs=qt[s].t[po:po + 64, hp, :], start=True, stop=True),
                                 reads=[kT.tr, qt[s].tr], writes=[Sb.tr])
                    mtrs = list({id(mTtr[b // 8]): mTtr[b // 8] for b in range(b0, b0 + m)}.values())
                    ptbs = []
                    for hi_ in range(2):
                        Sb = Sbs[hi_]
                        eb = Eb[ctr["e"] % 4]
                        ptb = PTb[ctr["e"] % 6]
                        ctr["e"] += 1
                        P.op(ACT, lambda: nc.scalar.activation(out=eb.t[:, :m * 128], in_=Sb.t[:, :m * 128], func=AF.Exp, scale=0.125), reads=[Sb.tr], writes=[eb.tr])
                        P.op(POOL, lambda: nc.gpsimd.tensor_tensor(out=ptb.t[:, :m * 128], in0=eb.t[:, :m * 128], in1=mT.t[:, b0 * 128:(b0 + m) * 128], op=ALU.mult),
                             reads=[eb.tr] + mtrs, writes=[ptb.tr])
                        ptbs.append(ptb)
                    pend.append((hp, b0, m, ptbs))
                    if len(pend) > 1:
                        pv(*pend.pop(0))
                while pend:
                    pv(*pend.pop(0))
                if cstop < 5:
                    return
                for half in range(2):
                    Ob = PS[4 + half]
                    P.op(ACT, lambda: nc.scalar.copy(out=osb.t[:, half * 260:(half + 1) * 260], in_=Ob.t[:, :260]), reads=[Ob.tr], writes=[osb_tr[half]])
                    P.op(POOL, lambda: nc.gpsimd.tensor_copy(out=rsum.t[:, half * 4:(half + 1) * 4], in_=osb.t[:, half * 260:(half + 1) * 260].rearrange("p (h d) -> p h d", h=4)[:, :, 64]),
                         reads=[osb_tr[half]], writes=[rsum_tr[half]])
                    P.op(POOL, lambda: nc.gpsimd.tensor_tensor(out=rinv.t[:, half * 4:(half + 1) * 4], in0=rsum.t[:, half * 4:(half + 1) * 4], in1=mones.t[:, 0:4], op=ALU.pow),
                         reads=[rsum_tr[half], mones.tr], writes=[rinv_tr[half]])
                for h in range(8):
                    oc = (h % 2) * 260 + (h // 2) * 65
                    ri = (h % 2) * 4 + h // 2
                    P.op(ACT, lambda: nc.scalar.activation(out=ya.t[:, h * 64:(h + 1) * 64], in_=osb.t[:, oc:oc + 64], func=AF.Identity, scale=rinv.t[:, ri:ri + 1]),
                         reads=[osb_tr[h % 2], rinv_tr[h % 2]], writes=[ya_tr[h]])
                if cstop < 6:
                    return
                q = PQ[0]
                for j in range(4):
                    P.op(PE, lambda: nc.tensor.transpose(out=q.t[:, j * 128:(j + 1) * 128], in_=ya.t[:, j * 128:(j + 1) * 128], identity=identb.t[:]),
                         reads=[ya_tr[2 * j], ya_tr[2 * j + 1], identb.tr], writes=[q.tr])
                P.op(ACT, lambda: nc.scalar.copy(out=yaT.t[:], in_=q.t[:, :512].rearrange("p (j t) -> p j t", j=4)), reads=[q.tr], writes=[yaT.tr])
                P.op(POOL, lambda: nc.gpsimd.tensor_tensor(out=yat[s].t[:], in0=yaT.t[:], in1=szat[s].t[:], op=ALU.mult),
                     reads=[yaT.tr, szat[s].tr], writes=[yat[s].tr])
                if cstop < 7:
                    return
                P.dma(POOL, yat[s].sem, YC[4:8, :, c0:c1].rearrange("j p t -> p j t"), yat[s].t[:], reads=[yat[s].tr])

            if NTC > 0:
                load_idx(0)
                if NTC > 1:
                    load_idx(1)
            load_kv()
            if NTC > 0:
                S1a(0)
            for i in range(NTC):
                if i + 2 < NTC:
                    load_idx(i + 2)
                load_att(i)
                if i + 1 < NTC:
                    S1a(i + 1)
                if i >= 1:
                    S2(i - 1)
                S1b(i)
                S1c(i)
            if NTC > 0:
                S2(NTC - 1)
            P.barrier()

        with ExitStack() as sD:
            yc = sb(sD, "yc", [128, 8, LP], BF16, True)
            wol = [sb(sD, f"wol{i}", [128, 1024], F32, True) for i in range(4)]
            wob = sb(sD, "wob", [128, 8, 1024], BF16)
            wob_tr = [Tr() for _ in range(8)]
            grep = sb(sD, "grep", [128, 1024], F32, True)
            brep = sb(sD, "brep", [128, 1024], F32, True)
            htl = [sb(sD, f"htl{i}", [128, 1024], F32, True) for i in range(3)]
            zt = [sb(sD, f"zt{i}", [128, 1024], F32) for i in range(3)]
            zn = [sb(sD, f"zn{i}", [128, 1024], F32) for i in range(3)]
            o1 = [sb(sD, f"o1{i}", [128, 1024], F32) for i in range(3)]
            ho = [sb(sD, f"ho{i}", [128, 1024], F32, True) for i in range(3)]
            st6 = [sb(sD, f"st6{i}", [128, 12], F32) for i in range(3)]
            mv = [sb(sD, f"mv{i}", [128, 2], F32) for i in range(3)]
            sdv = [sb(sD, f"sdv{i}", [128, 1], F32) for i in range(3)]
            rs = [sb(sD, f"rs{i}", [128, 1], F32) for i in range(3)]
            nmr = [sb(sD, f"nmr{i}", [128, 1], F32) for i in range(3)]

            for ec in range(8):
                P.dma(SP, yc.sem, yc.t[:, ec, :], YC[ec], writes=[yc.tr])
            P.dma(SP, grep.sem, grep.t[:], GB[l, 0], writes=[grep.tr])
            P.dma(SP, brep.sem, brep.t[:], GB[l, 1], writes=[brep.tr])
            for ec in range(8):
                s = ec % 4
                P.dma(SP, wol[s].sem, wol[s].t[:], WO[l, ec], writes=[wol[s].tr])
                if ec % 2 == 0:
                    P.op(POOL, lambda s=s, ec=ec: nc.gpsimd.tensor_copy(out=wob.t[:, ec, :], in_=wol[s].t[:]), reads=[wol[s].tr], writes=[wob_tr[ec]])
                else:
                    P.op(DVE, lambda s=s, ec=ec: nc.vector.tensor_copy(out=wob.t[:, ec, :], in_=wol[s].t[:]), reads=[wol[s].tr], writes=[wob_tr[ec]])
            for i in range(NT if "D" in phases else 0):
                s = i % 3
                P.dma(SP, htl[s].sem, htl[s].t[:], h_in[i * 128:(i + 1) * 128, :], writes=[htl[s].tr])
                for half in range(2):
                    b = PS[(i % 3) * 2 + half]
                    for ec in range(8):
                        P.op(PE, lambda b=b, ec=ec, half=half, i=i: nc.tensor.matmul(b.t[:, :], lhsT=yc.t[:, ec, i * 128:(i + 1) * 128], rhs=wob.t[:, ec, half * 512:(half + 1) * 512], start=(ec == 0), stop=(ec == 7)),
                             reads=[yc.tr, wob_tr[ec]], writes=[b.tr])
                    P.op(DVE, lambda b=b, half=half, s=s: nc.vector.scalar_tensor_tensor(out=zt[s].t[:, half * 512:(half + 1) * 512], in0=htl[s].t[:, half * 512:(half + 1) * 512], scalar=ALPHA, in1=b.t[:, :], op0=ALU.mult, op1=ALU.add),
                         reads=[htl[s].tr, b.tr], writes=[zt[s].tr])
                for half in range(2):
                    P.op(DVE, lambda half=half, s=s: nc.vector.bn_stats(out=st6[s].t[:, half * 6:(half + 1) * 6], in_=zt[s].t[:, half * 512:(half + 1) * 512]), reads=[zt[s].tr], writes=[st6[s].tr])
                P.op(DVE, lambda s=s: nc.vector.bn_aggr(out=mv[s].t[:], in_=st6[s].t[:]), reads=[st6[s].tr], writes=[mv[s].tr])
                P.op(ACT, lambda s=s: nc.scalar.activation(out=sdv[s].t[:], in_=mv[s].t[:, 1:2], func=AF.Sqrt, bias=epst.t[:, 0:1]), reads=[mv[s].tr, epst.tr], writes=[sdv[s].tr])
                P.op(DVE, lambda s=s: nc.vector.reciprocal(out=rs[s].t[:], in_=sdv[s].t[:]), reads=[sdv[s].tr], writes=[rs[s].tr])
                P.op(DVE, lambda s=s: nc.vector.scalar_tensor_tensor(out=nmr[s].t[:], in0=mv[s].t[:, 0:1], scalar=-1.0, in1=rs[s].t[:], op0=ALU.mult, op1=ALU.mult), reads=[mv[s].tr, rs[s].tr], writes=[nmr[s].tr])
                P.op(ACT, lambda s=s: nc.scalar.activation(out=zn[s].t[:], in_=zt[s].t[:], func=AF.Identity, scale=rs[s].t[:, 0:1], bias=nmr[s].t[:, 0:1]), reads=[zt[s].tr, rs[s].tr, nmr[s].tr], writes=[zn[s].tr])
                P.op(POOL, lambda s=s: nc.gpsimd.tensor_tensor(out=o1[s].t[:], in0=zn[s].t[:], in1=grep.t[:], op=ALU.mult), reads=[zn[s].tr, grep.tr], writes=[o1[s].tr])
                P.op(POOL, lambda s=s: nc.gpsimd.tensor_tensor(out=ho[s].t[:], in0=o1[s].t[:], in1=brep.t[:], op=ALU.add), reads=[o1[s].tr, brep.tr], writes=[ho[s].tr])
                P.dma(POOL, ho[s].sem, h_out[i * 128:(i + 1) * 128, :], ho[s].t[:], reads=[ho[s].tr])
            P.barrier()

    es.close()
    return nc, P.nins


def _rope_tables():
    inv_freq = (10000.0 ** (-np.arange(0, 64, 2, dtype=np.float32) / np.float32(64))).astype(np.float32)
    ang = np.arange(LP, dtype=np.float32)[:, None] * inv_freq[None, :]
    cos = np.cos(ang).astype(np.float32).T
    sin = np.sin(ang).astype(np.float32).T
    p = np.arange(128)
    d = p % 64
    cosT = cos[d % 32]
    sinT = np.where((d < 32)[:, None], -sin[d % 32], sin[d % 32])
    return np.ascontiguousarray(cosT, dtype=np.float32), np.ascontiguousarray(sinT, dtype=np.float32)


def _prep_weights(w_in, conv_w, conv_b, conv_ln_g, conv_ln_b, w_out, post_ln_g, post_ln_b):
    Ld = w_in.shape[0]
    cols = _weight_cols()
    WA = np.empty((Ld, NWCH, 128, 1024), np.float32)
    for l in range(Ld):
        for c, cc in enumerate(cols):
            blk = w_in[l][:, cc]
            WA[l, c] = blk.reshape(8, 128, 128).transpose(1, 0, 2).reshape(128, 1024)
        for hv in range(2):
            blk = w_in[l][:, O_V + hv * 256:O_V + (hv + 1) * 256]
            arr = blk.reshape(8, 128, 256).transpose(1, 0, 2).reshape(128, 2048)
            WA[l, len(cols) + 2 * hv] = arr[:, :1024]
            WA[l, len(cols) + 2 * hv + 1] = arr[:, 1024:]
    WWI = np.ascontiguousarray(w_in[:, :, O_WI:O_WI + 8].reshape(Ld, 8, 128, 8).transpose(0, 2, 1, 3).reshape(Ld, 128, 64))
    CPa = np.empty((Ld, 128, 136), np.float32)
    CPa[:, :, 0:124] = conv_w.reshape(Ld, 31, 4, 128).transpose(0, 3, 2, 1).reshape(Ld, 128, 124)
    CPa[:, :, 124:128] = conv_b.reshape(Ld, 4, 128).transpose(0, 2, 1)
    CPa[:, :, 128:132] = conv_ln_g.reshape(Ld, 4, 128).transpose(0, 2, 1)
    CPa[:, :, 132:136] = conv_ln_b.reshape(Ld, 4, 128).transpose(0, 2, 1)
    WOa = np.ascontiguousarray(w_out.reshape(Ld, 8, 128, 1024))
    GBa = np.empty((Ld, 2, 128, 1024), np.float32)
    GBa[:, 0] = post_ln_g[:, None, :]
    GBa[:, 1] = post_ln_b[:, None, :]
    return WA, WWI, CPa, WOa, GBa


def _consts():
    ident = np.eye(128, dtype=np.float32)
    t = np.arange(128)
    negmask = np.where(t[None, :] <= t[:, None], 0.0, NEG).astype(np.float32)
    cosT, sinT = _rope_tables()
    pwr = np.broadcast_to((0.5 ** np.arange(1, KBIS + 1)).astype(np.float32)[None, :], (128, KBIS)).copy()
    return {"c_ident": ident, "c_negmask": negmask, "c_cos": cosT, "c_sin": sinT, "c_pw": pwr}


_CACHE = {}
FUSED = True


def _get_prog(n_layers):
    if n_layers not in _CACHE:
        _CACHE[n_layers] = build_program(n_layers)[0]
    return _CACHE[n_layers]


def kernel(x, meta_tokens, w_in, conv_w, conv_b, conv_ln_g, conv_ln_b, w_out, post_ln_g, post_ln_b):
    x = np.asarray(x, np.float32)
    B = x.shape[0]
    f = lambda a: np.asarray(a, np.float32)
    WA, WWI, CPa, WOa, GBa = _prep_weights(f(w_in), f(conv_w), f(conv_b), f(conv_ln_g), f(conv_ln_b), f(w_out), f(post_ln_g), f(post_ln_b))
    consts = _consts()
    hs = []
    for b in range(B):
        h = np.zeros((LP, D_MODEL), np.float32)
        h[:N_META] = f(meta_tokens)
        h[N_META:N_META + SEQ] = x[b]
        hs.append(h)
    if FUSED:
        nc = _get_prog(DEPTH)
        in_maps = [dict(h0=hs[b], WA=WA, WWI=WWI, CP=CPa, WO=WOa, GB=GBa, **consts) for b in range(B)]
        res = run_bass_kernel_spmd(nc, in_maps, core_ids=list(range(B)))
        hs = [res.results[b]["hout"] for b in range(B)]
    else:
        nc = _get_prog(1)
        for l in range(DEPTH):
            in_maps = [dict(h0=hs[b], WA=WA[l:l + 1], WWI=WWI[l:l + 1], CP=CPa[l:l + 1], WO=WOa[l:l + 1], GB=GBa[l:l + 1], **consts) for b in range(B)]
            res = run_bass_kernel_spmd(nc, in_maps, core_ids=list(range(B)))
            hs = [res.results[b]["hout"] for b in range(B)]
    out = np.stack([hs[b][N_META:N_META + SEQ] for b in range(B)], axis=0)
    return np.ascontiguousarray(out, dtype=np.float32)
```

```python
import numpy as np
from contextlib import ExitStack
import concourse.bass as bass
import concourse.mybir as mybir
from concourse.bass_utils import run_bass_kernel_spmd

F32 = mybir.dt.float32
BF16 = mybir.dt.bfloat16
AF = mybir.ActivationFunctionType
ALU = mybir.AluOpType
AX = mybir.AxisListType

D_MODEL = 1024
SEQ = 4096
N_META = 16
LP = 4224
NT = LP // 128
DEPTH = 4
TOPK = 256
KBIS = 17
LN_EPS = 1e-5
ALPHA = (2.0 * DEPTH) ** 0.25
WI_SCALE = (64 ** -0.5) * (8 ** -0.5)
NEG = -1.0e30
TCH = [(i * 512, 512) for i in range(8)] + [(4096, 128)]

PJ_U, PJ_SZC, PJ_Q, PJ_K, PJ_SZA, PJ_QI, PJ_KI = 0, 4, 8, 12, 16, 20, 24
NPJ = 25
O_A, O_G, O_ZC, O_Q, O_K, O_V, O_ZA, O_QI, O_KI, O_WI = 0, 512, 1024, 1536, 2048, 2560, 3072, 3584, 4096, 4160


def _items():
    items = []
    c = 0
    for j in range(4):
        items.append(("conv", c, 2, PJ_U + j)); c += 2
    for j in range(4):
        items.append(("silu", c, 1, PJ_SZC + j)); c += 1
    for j in range(4):
        items.append(("rope", c, 2, PJ_Q + j)); c += 2
    for j in range(4):
        items.append(("rope", c, 2, PJ_K + j)); c += 2
    for j in range(4):
        items.append(("silu", c, 1, PJ_SZA + j)); c += 1
    for j in range(4):
        items.append(("rope", c, 2, PJ_QI + j)); c += 2
    items.append(("rope", c, 2, PJ_KI)); c += 2
    for hv in range(2):
        items.append(("v", c, 2, hv)); c += 2
    return items, c


ITEMS, NWCH = _items()


def _rot_cols(base, width):
    idx = np.arange(width)
    return base + (idx // 64) * 64 + ((idx % 64) + 32) % 64


def _weight_cols():
    cols = []
    for j in range(4):
        cols.append(O_A + j * 128 + np.arange(128)); cols.append(O_G + j * 128 + np.arange(128))
    for j in range(4):
        cols.append(O_ZC + j * 128 + np.arange(128))
    for base in (O_Q, O_K):
        for j in range(4):
            cols.append(base + j * 128 + np.arange(128))
            cols.append(_rot_cols(base, 512)[j * 128:(j + 1) * 128])
    za = [O_ZA + j * 128 + np.arange(128) for j in range(4)]
    qi = []
    for j in range(4):
        qi.append(O_QI + j * 128 + np.arange(128))
        qi.append(_rot_cols(O_QI, 512)[j * 128:(j + 1) * 128])
    cols = cols + za + qi
    kic = O_KI + np.arange(64)
    kir = _rot_cols(O_KI, 64)
    cols.append(np.concatenate([kic, kic])); cols.append(np.concatenate([kir, kir]))
    return cols


class Sem:
    __slots__ = ("h", "val")

    def __init__(self, h):
        self.h = h
        self.val = 0


class Tr:
    __slots__ = ("w", "r")

    def __init__(self):
        self.w = {}
        self.r = {}


class Eng:
    def __init__(self, name, e, sem, sync_self):
        self.name = name
        self.e = e
        self.sem = sem
        self.sync_self = sync_self
        self.waited = {}


class Prog:
    def __init__(self, nc, es):
        self.nc = nc
        self.es = es
        self.sems = []
        self.PE = Eng("pe", nc.tensor, self.new_sem("s_pe"), False)
        self.ACT = Eng("act", nc.scalar, self.new_sem("s_act"), True)
        self.DVE = Eng("dve", nc.vector, self.new_sem("s_dve"), True)
        self.POOL = Eng("pool", nc.gpsimd, self.new_sem("s_pool"), True)
        self.SP = Eng("sp", nc.sync, self.new_sem("s_sp"), False)
        self.engs = [self.PE, self.ACT, self.DVE, self.POOL, self.SP]
        self.nins = 0

    def new_sem(self, name):
        s = Sem(self.es.enter_context(self.nc.semaphore(name)))
        self.sems.append(s)
        return s

    def _wait(self, E, deps):
        for s, v in deps.items():
            if s is E.sem and not E.sync_self:
                continue
            if E.waited.get(s, 0) < v:
                E.e.wait_ge(s.h, v)
                E.waited[s] = v

    @staticmethod
    def _deps(reads, writes):
        deps = {}
        for t in reads:
            for s, v in t.w.items():
                if deps.get(s, 0) < v:
                    deps[s] = v
        for t in writes:
            for dd in (t.w, t.r):
                for s, v in dd.items():
                    if deps.get(s, 0) < v:
                        deps[s] = v
        return deps

    def op(self, E, ins_fn, reads=(), writes=()):
        self._wait(E, self._deps(reads, writes))
        ins = ins_fn()
        E.sem.val += 1
        ins.then_inc(E.sem.h, 1)
        v = E.sem.val
        for t in writes:
            t.w = {E.sem: v}
            t.r = {}
        for t in reads:
            t.r[E.sem] = v
        self.nins += 1

    def dma(self, Q, sem, out, in_, reads=(), writes=()):
        deps = self._deps(reads, writes)
        deps.pop(sem, None)
        self._wait(Q, deps)
        ins = Q.e.dma_start(out=out, in_=in_)
        sem.val += 16
        ins.then_inc(sem.h, 16)
        for t in writes:
            t.w = {sem: sem.val}
            t.r = {}
        for t in reads:
            t.r[sem] = sem.val
        self.nins += 1

    def barrier(self):
        allv = {s: s.val for s in self.sems if s.val > 0}
        for E in self.engs:
            for s, v in allv.items():
                if s is E.sem:
                    continue
                if E.waited.get(s, 0) < v:
                    E.e.wait_ge(s.h, v)
                    E.waited[s] = v


class Buf:
    def __init__(self, t, tr=None, sem=None):
        self.t = t
        self.tr = tr if tr is not None else Tr()
        self.sem = sem


def build_program(n_layers, dbg=False, phases="ABCD", nitems=None, ntiles=None, cstop=9):
    nc = bass.Bass("TRN2", target_bir_lowering=False)
    es = ExitStack()
    P = Prog(nc, es)
    PE, ACT, DVE, POOL, SP = P.PE, P.ACT, P.DVE, P.POOL, P.SP
    L = n_layers

    def din(name, shape, dt=F32):
        return nc.dram_tensor(name, shape, dt, kind="ExternalInput").ap()

    skind = "ExternalOutput" if dbg else "Internal"
    h0 = din("h0", [LP, D_MODEL])
    WA = din("WA", [L, NWCH, 128, 1024])
    WWI = din("WWI", [L, 128, 64])
    CP = din("CP", [L, 128, 136])
    WO = din("WO", [L, 8, 128, 1024])
    GB = din("GB", [L, 2, 128, 1024])
    c_ident = din("c_ident", [128, 128])
    c_negmask = din("c_negmask", [128, 128])
    c_cos = din("c_cos", [128, LP])
    c_sin = din("c_sin", [128, LP])
    c_pw = din("c_pw", [128, KBIS])
    hout = nc.dram_tensor("hout", [LP, D_MODEL], F32, kind="ExternalOutput").ap()
    hbufs = [nc.dram_tensor(f"hbuf{i}", [LP, D_MODEL], F32, kind="Internal").ap() for i in range(2)] if L > 1 else []
    PJ = nc.dram_tensor("PJ", [NPJ, 128, LP], BF16, kind=skind).ap()
    VA = nc.dram_tensor("VA", [NT, 128, 520], BF16, kind=skind).ap()
    WI = nc.dram_tensor("WI", [128, NT * 8], F32, kind=skind).ap()
    YC = nc.dram_tensor("YC", [8, 128, LP], BF16, kind=skind).ap()

    semcache = {}
    cur = {"l": "g"}

    def sb(stack, name, shape, dt, dma_sem=False):
        t = stack.enter_context(nc.sbuf_tensor(f"{name}_{cur['l']}", shape, dt))
        sem = None
        if dma_sem:
            if name not in semcache:
                semcache[name] = P.new_sem("d_" + name)
            sem = semcache[name]
        return Buf(t, sem=sem)

    PS = [Buf(es.enter_context(nc.psum_tensor(f"ps{i}", [128, 512], F32))) for i in range(6)]
    PQ = [Buf(es.enter_context(nc.psum_tensor(f"pq{i}", [128, 1024], BF16))) for i in range(2)]
    identf = sb(es, "identf", [128, 128], F32, True)
    identb = sb(es, "identb", [128, 128], BF16)
    negmask = sb(es, "negmask", [128, 128], F32, True)
    pw = sb(es, "pw", [128, KBIS], F32, True)
    epst = sb(es, "epst", [128, 1], F32)
    thrneg = sb(es, "thrneg", [128, 1], F32)
    onesm = sb(es, "onesm", [128, 128], F32)
    mones = sb(es, "mones", [128, 8], F32)

    P.dma(SP, identf.sem, identf.t[:], c_ident[:, :], writes=[identf.tr])
    P.dma(SP, negmask.sem, negmask.t[:], c_negmask[:, :], writes=[negmask.tr])
    P.dma(SP, pw.sem, pw.t[:], c_pw[:, :], writes=[pw.tr])
    P.op(DVE, lambda: nc.vector.tensor_copy(out=identb.t[:], in_=identf.t[:]), reads=[identf.tr], writes=[identb.tr])
    P.op(DVE, lambda: nc.vector.memset(epst.t[:], LN_EPS), writes=[epst.tr])
    P.op(DVE, lambda: nc.vector.memset(thrneg.t[:], -1.0e29), writes=[thrneg.tr])
    P.op(DVE, lambda: nc.vector.memset(onesm.t[:], 1.0 / 512.0), writes=[onesm.tr])
    P.op(DVE, lambda: nc.vector.memset(mones.t[:], -1.0), writes=[mones.tr])

    cnt = {"ps": 0}
    LB = [PS[0], PS[1], PS[2], Buf(PQ[1].t[:, :].bitcast(F32), tr=PQ[1].tr)]

    def next_ps3():
        b = PS[cnt["ps"] % 3]
        cnt["ps"] += 1
        return b

    for l in range(L):
        cur["l"] = f"L{l}"
        h_in = h0 if l == 0 else hbufs[(l - 1) % 2]
        h_out = hout if l == L - 1 else hbufs[l % 2]

        with ExitStack() as sA:
            hT = sb(sA, "hT", [128, 8, LP], BF16)
            hT_tr = [Tr() for _ in range(NT)]
            with ExitStack() as s0:
                hld = [sb(s0, f"hld{i}", [128, 1024], F32, True) for i in range(2)]
                hb = [sb(s0, f"hb{i}", [128, 1024], BF16) for i in range(2)]
                for i in range(NT):
                    s = i % 2
                    P.dma(SP, hld[s].sem, hld[s].t[:], h_in[i * 128:(i + 1) * 128, :], writes=[hld[s].tr])
                    if i % 2 == 0:
                        P.op(ACT, lambda s=s: nc.scalar.copy(out=hb[s].t[:], in_=hld[s].t[:]), reads=[hld[s].tr], writes=[hb[s].tr])
                    else:
                        P.op(DVE, lambda s=s: nc.vector.tensor_copy(out=hb[s].t[:], in_=hld[s].t[:]), reads=[hld[s].tr], writes=[hb[s].tr])
                    q = PQ[i % 2]
                    for kc in range(8):
                        P.op(PE, lambda s=s, kc=kc, q=q: nc.tensor.transpose(out=q.t[:, kc * 128:(kc + 1) * 128], in_=hb[s].t[:, kc * 128:(kc + 1) * 128], identity=identb.t[:]),
                             reads=[hb[s].tr, identb.tr], writes=[q.tr])
                    src = q.t[:, :].rearrange("p (k t) -> p k t", k=8)
                    if i % 2 == 0:
                        P.op(DVE, lambda i=i, src=src: nc.vector.tensor_copy(out=hT.t[:, :, i * 128:(i + 1) * 128], in_=src), reads=[q.tr], writes=[hT_tr[i]])
                    else:
                        P.op(ACT, lambda i=i, src=src: nc.scalar.copy(out=hT.t[:, :, i * 128:(i + 1) * 128], in_=src), reads=[q.tr], writes=[hT_tr[i]])
                P.barrier()

            with ExitStack() as s1:
                if "A" not in phases:
                    ITEMS_ = []
                else:
                    ITEMS_ = ITEMS if nitems is None else ITEMS[:nitems]
                cosT = sb(s1, "cosT", [128, LP], F32, True)
                sinT = sb(s1, "sinT", [128, LP], F32, True)
                wld = [sb(s1, f"wld{i}", [128, 2048], F32, True) for i in range(2)]
                wbf = [sb(s1, f"wbf{i}", [128, 2048], BF16) for i in range(2)]
                stage = [sb(s1, f"stage{i}", [128, LP], BF16, True) for i in range(2)]
                sig = [sb(s1, f"sig{i}", [128, 512], F32) for i in range(2)]
                tm1 = [sb(s1, f"tm1{i}", [128, 512], F32) for i in range(2)]
                tm2 = [sb(s1, f"tm2{i}", [128, 512], F32) for i in range(2)]
                vst = [sb(s1, f"vst{i}", [128, 8, 4, 65], BF16, True) for i in range(2)]
                wwif = sb(s1, "wwif", [128, 64], F32, True)
                wwib = sb(s1, "wwib", [128, 64], BF16)
                wisb = sb(s1, "wisb", [128, NT * 8], F32, True)

                P.dma(SP, wwif.sem, wwif.t[:], WWI[l], writes=[wwif.tr])
                P.op(POOL, lambda: nc.gpsimd.tensor_copy(out=wwib.t[:], in_=wwif.t[:]), reads=[wwif.tr], writes=[wwib.tr])
                for s in range(2):
                    P.op(POOL, lambda s=s: nc.gpsimd.memset(vst[s].t[:], 1.0), writes=[vst[s].tr])

                def load_w(it_idx):
                    kind, c0, nch, _ = ITEMS[it_idx]
                    s = it_idx % 2
                    P.dma(SP, wld[s].sem, wld[s].t[:, :nch * 1024].rearrange("p (c f) -> p c f", c=nch), WA[l, c0:c0 + nch].rearrange("c p f -> p c f"), writes=[wld[s].tr])
                    P.op(POOL, lambda s=s, nch=nch: nc.gpsimd.tensor_copy(out=wbf[s].t[:, :nch * 1024], in_=wld[s].t[:, :nch * 1024]),
                         reads=[wld[s].tr], writes=[wbf[s].tr])

                if ITEMS_:
                    load_w(0)
                P.dma(SP, cosT.sem, cosT.t[:], c_cos[:, :], writes=[cosT.tr])
                P.dma(SP, sinT.sem, sinT.t[:], c_sin[:, :], writes=[sinT.tr])
                gctr = 0
                st_ctr = 0
                for it_idx, (kind, c0, nch, pj) in enumerate(ITEMS_):
                    if it_idx + 1 < len(ITEMS_):
                        load_w(it_idx + 1)
                    ws = it_idx % 2
                    wv_ = wbf[ws]
                    if kind != "v":
                        stg = stage[st_ctr % 2]
                        st_ctr += 1
                        for (t0, n) in TCH:
                            bA = PS[(gctr % 2) * 2]
                            bB = PS[(gctr % 2) * 2 + 1]
                            g2 = gctr % 2
                            gctr += 1
                            hts = hT_tr[t0 // 128:(t0 + n) // 128]
                            for kc in range(8):
                                P.op(PE, lambda kc=kc, bA=bA, t0=t0, n=n: nc.tensor.matmul(bA.t[:, :n], lhsT=wv_.t[:, kc * 128:(kc + 1) * 128], rhs=hT.t[:, kc, t0:t0 + n], start=(kc == 0), stop=(kc == 7)),
                                     reads=[wv_.tr] + hts, writes=[bA.tr])
                            if nch == 2:
                                for kc in range(8):
                                    P.op(PE, lambda kc=kc, bB=bB, t0=t0, n=n: nc.tensor.matmul(bB.t[:, :n], lhsT=wv_.t[:, 1024 + kc * 128:1024 + (kc + 1) * 128], rhs=hT.t[:, kc, t0:t0 + n], start=(kc == 0), stop=(kc == 7)),
                                         reads=[wv_.tr] + hts, writes=[bB.tr])
                            if kind == "conv":
                                P.op(ACT, lambda bB=bB, g2=g2, n=n: nc.scalar.activation(out=sig[g2].t[:, :n], in_=bB.t[:, :n], func=AF.Sigmoid), reads=[bB.tr], writes=[sig[g2].tr])
                                P.op(DVE, lambda bA=bA, g2=g2, t0=t0, n=n, stg=stg: nc.vector.tensor_tensor(out=stg.t[:, t0:t0 + n], in0=bA.t[:, :n], in1=sig[g2].t[:, :n], op=ALU.mult),
                                     reads=[bA.tr, sig[g2].tr], writes=[stg.tr])
                            elif kind == "silu":
                                P.op(ACT, lambda bA=bA, t0=t0, n=n, stg=stg: nc.scalar.activation(out=stg.t[:, t0:t0 + n], in_=bA.t[:, :n], func=AF.Silu), reads=[bA.tr], writes=[stg.tr])
                            else:
                                P.op(DVE, lambda bA=bA, g2=g2, t0=t0, n=n: nc.vector.tensor_tensor(out=tm1[g2].t[:, :n], in0=bA.t[:, :n], in1=cosT.t[:, t0:t0 + n], op=ALU.mult),
                                     reads=[bA.tr, cosT.tr], writes=[tm1[g2].tr])
                                P.op(DVE, lambda bB=bB, g2=g2, t0=t0, n=n: nc.vector.tensor_tensor(out=tm2[g2].t[:, :n], in0=bB.t[:, :n], in1=sinT.t[:, t0:t0 + n], op=ALU.mult),
                                     reads=[bB.tr, sinT.tr], writes=[tm2[g2].tr])
                                P.op(POOL, lambda g2=g2, t0=t0, n=n, stg=stg: nc.gpsimd.tensor_tensor(out=stg.t[:, t0:t0 + n], in0=tm1[g2].t[:, :n], in1=tm2[g2].t[:, :n], op=ALU.add),
                                     reads=[tm1[g2].tr, tm2[g2].tr], writes=[stg.tr])
                        P.dma(POOL, stg.sem, PJ[pj], stg.t[:], reads=[stg.tr])
                    else:
                        hv = pj
                        wv3 = wv_.t[:, :].rearrange("p (k e) -> p k e", k=8)
                        for i in range(NT):
                            b = PS[(gctr % 2) * 2]
                            gctr += 1
                            gi, gs = i // 8, (i // 8) % 2
                            for kc in range(8):
                                P.op(PE, lambda kc=kc, b=b, i=i: nc.tensor.matmul(b.t[:, :256], lhsT=hT.t[:, kc, i * 128:(i + 1) * 128], rhs=wv3[:, kc, :], start=(kc == 0), stop=(kc == 7)),
                                     reads=[wv_.tr, hT_tr[i]], writes=[b.tr])
                            P.op(ACT, lambda b=b, i=i, gs=gs: nc.scalar.copy(out=vst[gs].t[:, i % 8, :, 0:64], in_=b.t[:, :256].rearrange("p (h d) -> p h d", h=4)),
                                 reads=[b.tr], writes=[vst[gs].tr])
                            if i % 8 == 7 or i == NT - 1:
                                ng = i % 8 + 1
                                i0 = gi * 8
                                dst = VA[i0:i0 + ng].rearrange("i p (v f) -> p i v f", v=2)[:, :, hv, :]
                                P.dma(POOL, vst[gs].sem, dst, vst[gs].t[:, :ng].rearrange("p i h d -> p i (h d)"), reads=[vst[gs].tr])
                            if hv == 0:
                                b5 = PS[4 + (i % 2)]
                                for kc in range(8):
                                    P.op(PE, lambda kc=kc, b5=b5, i=i: nc.tensor.matmul(b5.t[:, :8], lhsT=hT.t[:, kc, i * 128:(i + 1) * 128], rhs=wwib.t[:, kc * 8:(kc + 1) * 8], start=(kc == 0), stop=(kc == 7)),
                                         reads=[wwib.tr, hT_tr[i]], writes=[b5.tr])
                                P.op(DVE, lambda b5=b5, i=i: nc.vector.tensor_scalar(out=wisb.t[:, i * 8:(i + 1) * 8], in0=b5.t[:, :8], scalar1=WI_SCALE, scalar2=None, op0=ALU.mult),
                                     reads=[b5.tr], writes=[wisb.tr])
                P.dma(POOL, wisb.sem, WI[:, :], wisb.t[:], reads=[wisb.tr])
                P.barrier()

        with ExitStack() as sB:
            upad = [sb(sB, f"upad{j}", [128, 30 + LP], BF16, True) for j in range(4)]
            cp = sb(sB, "cp", [128, 136], F32, True)
            dg = sb(sB, "dg", [128, 4 * 31, 128], BF16)
            szc = [sb(sB, f"szc{i}", [128, 4, 512], BF16, True) for i in range(2)]
            yst = [sb(sB, f"yst{i}", [128, 4, 512], BF16, True) for i in range(2)]
            cf = [sb(sB, f"cf{j}", [128, 512], F32) for j in range(4)]
            sq = [sb(sB, f"sq{j}", [128, 512], F32) for j in range(4)]
            mean_sb = sb(sB, "mean_sb", [128, 512], F32)
            msq = sb(sB, "msq", [128, 512], F32)
            var = sb(sB, "var", [128, 512], F32)
            sd = sb(sB, "sd", [128, 512], F32)
            rstd = sb(sB, "rstd", [128, 512], F32)
            y1 = [sb(sB, f"y1{i}", [128, 512], F32) for i in range(2)]
            y2 = [sb(sB, f"y2{i}", [128, 512], F32) for i in range(2)]
            zz = [sb(sB, f"zz{i}", [128, 512], F32) for i in range(2)]

            P.dma(SP, cp.sem, cp.t[:], CP[l], writes=[cp.tr])
            for j in range(4):
                P.op(POOL, lambda j=j: nc.gpsimd.memset(upad[j].t[:, 0:30], 0.0), writes=[upad[j].tr])
                P.dma(SP, upad[j].sem, upad[j].t[:, 30:], PJ[PJ_U + j], reads=[upad[j].tr], writes=[upad[j].tr])
            dg_tr = [Tr() for _ in range(4 * 31)]
            for j in range(4):
                for tap in range(31):
                    di = j * 31 + tap
                    if di % 2 == 0:
                        P.op(DVE, lambda di=di: nc.vector.tensor_scalar(out=dg.t[:, di, :], in0=identb.t[:], scalar1=cp.t[:, di:di + 1], scalar2=None, op0=ALU.mult),
                             reads=[identb.tr, cp.tr], writes=[dg_tr[di]])
                    else:
                        P.op(POOL, lambda di=di: nc.gpsimd.tensor_scalar(out=dg.t[:, di, :], in0=identb.t[:], scalar1=cp.t[:, di:di + 1], scalar2=1.0, op0=ALU.mult, op1=ALU.mult),
                             reads=[identb.tr, cp.tr], writes=[dg_tr[di]])
            for ci, (t0, n) in enumerate(TCH if "B" in phases else []):
                s = ci % 2
                P.dma(SP, szc[s].sem, szc[s].t[:, :, :n], PJ[PJ_SZC:PJ_SZC + 4, :, t0:t0 + n].rearrange("j p t -> p j t"), writes=[szc[s].tr])
                for j in range(4):
                    for tap in range(31):
                        P.op(PE, lambda j=j, tap=tap, t0=t0, n=n: nc.tensor.matmul(PS[j].t[:, :n], lhsT=dg.t[:, j * 31 + tap, :], rhs=upad[j].t[:, t0 + tap:t0 + tap + n], start=(tap == 0), stop=(tap == 30)),
                             reads=[dg_tr[j * 31 + tap], upad[j].tr], writes=[PS[j].tr])
                for j in range(4):
                    P.op(ACT, lambda j=j, n=n: nc.scalar.activation(out=cf[j].t[:, :n], in_=PS[j].t[:, :n], func=AF.Identity, bias=cp.t[:, 124 + j:125 + j]), reads=[PS[j].tr, cp.tr], writes=[cf[j].tr])
                    P.op(ACT, lambda j=j, n=n: nc.scalar.activation(out=sq[j].t[:, :n], in_=PS[j].t[:, :n], func=AF.Square, bias=cp.t[:, 124 + j:125 + j]), reads=[PS[j].tr, cp.tr], writes=[sq[j].tr])
                for j in range(4):
                    P.op(PE, lambda j=j, n=n: nc.tensor.matmul(PS[4].t[:, :n], lhsT=onesm.t[:], rhs=cf[j].t[:, :n], start=(j == 0), stop=(j == 3)), reads=[onesm.tr, cf[j].tr], writes=[PS[4].tr])
                for j in range(4):
                    P.op(PE, lambda j=j, n=n: nc.tensor.matmul(PS[5].t[:, :n], lhsT=onesm.t[:], rhs=sq[j].t[:, :n], start=(j == 0), stop=(j == 3)), reads=[onesm.tr, sq[j].tr], writes=[PS[5].tr])
                P.op(ACT, lambda n=n: nc.scalar.copy(out=mean_sb.t[:, :n], in_=PS[4].t[:, :n]), reads=[PS[4].tr], writes=[mean_sb.tr])
                P.op(ACT, lambda n=n: nc.scalar.activation(out=msq.t[:, :n], in_=PS[4].t[:, :n], func=AF.Square), reads=[PS[4].tr], writes=[msq.tr])
                P.op(DVE, lambda n=n: nc.vector.tensor_tensor(out=var.t[:, :n], in0=PS[5].t[:, :n], in1=msq.t[:, :n], op=ALU.subtract), reads=[PS[5].tr, msq.tr], writes=[var.tr])
                P.op(ACT, lambda n=n: nc.scalar.activation(out=sd.t[:, :n], in_=var.t[:, :n], func=AF.Sqrt, bias=epst.t[:, 0:1]), reads=[var.tr, epst.tr], writes=[sd.tr])
                P.op(DVE, lambda n=n: nc.vector.reciprocal(out=rstd.t[:, :n], in_=sd.t[:, :n]), reads=[sd.tr], writes=[rstd.tr])
                for j in range(4):
                    k2 = j % 2
                    P.op(DVE, lambda j=j, n=n, k2=k2: nc.vector.tensor_tensor(out=y1[k2].t[:, :n], in0=cf[j].t[:, :n], in1=mean_sb.t[:, :n], op=ALU.subtract), reads=[cf[j].tr, mean_sb.tr], writes=[y1[k2].tr])
                    P.op(POOL, lambda n=n, k2=k2: nc.gpsimd.tensor_tensor(out=y2[k2].t[:, :n], in0=y1[k2].t[:, :n], in1=rstd.t[:, :n], op=ALU.mult), reads=[y1[k2].tr, rstd.tr], writes=[y2[k2].tr])
                    P.op(ACT, lambda j=j, n=n, k2=k2: nc.scalar.activation(out=zz[k2].t[:, :n], in_=y2[k2].t[:, :n], func=AF.Silu, scale=cp.t[:, 128 + j:129 + j], bias=cp.t[:, 132 + j:133 + j]),
                         reads=[y2[k2].tr, cp.tr], writes=[zz[k2].tr])
                    P.op(DVE, lambda j=j, n=n, k2=k2, s=s: nc.vector.tensor_tensor(out=yst[s].t[:, j, :n], in0=zz[k2].t[:, :n], in1=szc[s].t[:, j, :n], op=ALU.mult), reads=[zz[k2].tr, szc[s].tr], writes=[yst[s].tr])
                P.dma(POOL, yst[s].sem, YC[0:4, :, t0:t0 + n].rearrange("j p t -> p j t"), yst[s].t[:, :, :n], reads=[yst[s].tr])
            P.barrier()

        with ExitStack() as sC:
            kT = sb(sC, "kT", [128, 4, LP], BF16, True)
            kiT = sb(sC, "kiT", [128, LP], BF16, True)
            vaug = sb(sC, "vaug", [128, NT, 520], BF16, True)
            wi_all = sb(sC, "wi_all", [128, NT * 8], F32, True)
            qt = [sb(sC, f"qt{i}", [128, 4, 128], BF16, True) for i in range(2)]
            qit = [sb(sC, f"qit{i}", [128, 4, 128], BF16, True) for i in range(2)]
            szat = [sb(sC, f"szat{i}", [128, 4, 128], BF16, True) for i in range(2)]
            dgw = [sb(sC, f"dgw{i}", [128, 8, 128], BF16) for i in range(2)]
            score = [sb(sC, f"score{i}", [128, LP], F32) for i in range(2)]
            score_tr = [[Tr() for _ in range(9)] for _ in range(2)]
            junk = sb(sC, "junk", [128, LP], mybir.dt.uint8)
            mask = sb(sC, "mask", [128, LP], BF16)
            maskT = [sb(sC, f"maskT{i}", [128, LP], BF16) for i in range(2)]
            maskT_tr = [[Tr() for _ in range(5)] for _ in range(2)]
            Rr = [sb(sC, f"Rr{i}", [128, 512], BF16) for i in range(8)]
            Eb = [sb(sC, f"Eb{i}", [128, 512], BF16) for i in range(4)]
            PTb = [sb(sC, f"PTb{i}", [128, 512], BF16) for i in range(6)]
            hi = sb(sC, "hi", [128, 1], F32)
            lo = sb(sC, "lo", [128, 1], F32)
            w0 = sb(sC, "w0", [128, 1], F32)
            Wk = sb(sC, "Wk", [128, KBIS], F32)
            mid = [sb(sC, f"mid{i}", [128, 1], F32) for i in range(2)]
            cntb = sb(sC, "cntb", [128, 1], F32)
            dd = sb(sC, "dd", [128, 1], F32)
            rinv = sb(sC, "rinv", [128, 8], F32)
            rinv_tr = [Tr(), Tr()]
            rsum = sb(sC, "rsum", [128, 8], F32)
            rsum_tr = [Tr(), Tr()]
            osb = sb(sC, "osb", [128, 520], F32)
            osb_tr = [Tr(), Tr()]
            ya = sb(sC, "ya", [128, 512], BF16)
            ya_tr = [Tr() for _ in range(8)]
            yaT = sb(sC, "yaT", [128, 4, 128], BF16)
            yat = [sb(sC, f"yat{i}", [128, 4, 128], BF16, True) for i in range(2)]

            P.dma(SP, kiT.sem, kiT.t[:], PJ[PJ_KI], writes=[kiT.tr])
            P.dma(SP, wi_all.sem, wi_all.t[:], WI[:, :], writes=[wi_all.tr])

            def load_kv():
                for j in range(4):
                    P.dma(SP, kT.sem, kT.t[:, j, :], PJ[PJ_K + j], writes=[kT.tr])
                for i0 in range(0, NT, 11):
                    P.dma(SP, vaug.sem, vaug.t[:, i0:i0 + 11, :], VA[i0:i0 + 11].rearrange("i p f -> p i f"), writes=[vaug.tr])

            ctr = {"r": 0, "e": 0, "m": 0}
            NTC = (NT if ntiles is None else ntiles) if "C" in phases else 0

            def load_idx(i):
                s = i % 2
                c0, c1 = i * 128, (i + 1) * 128
                P.dma(SP, qit[s].sem, qit[s].t[:], PJ[PJ_QI:PJ_QI + 4, :, c0:c1].rearrange("j p t -> p j t"), writes=[qit[s].tr])

            def load_att(i):
                s = i % 2
                c0, c1 = i * 128, (i + 1) * 128
                P.dma(SP, qt[s].sem, qt[s].t[:], PJ[PJ_Q:PJ_Q + 4, :, c0:c1].rearrange("j p t -> p j t"), writes=[qt[s].tr])
                P.dma(SP, szat[s].sem, szat[s].t[:], PJ[PJ_SZA:PJ_SZA + 4, :, c0:c1].rearrange("j p t -> p j t"), writes=[szat[s].tr])

            def S1a(i):
                s = i % 2
                N = 128 * (i + 1)
                sc, sctr = score[s], score_tr[s]
                for h in range(8):
                    P.op(POOL, lambda h=h: nc.gpsimd.tensor_scalar(out=dgw[s].t[:, h, :], in0=identb.t[:], scalar1=wi_all.t[:, i * 8 + h:i * 8 + h + 1], scalar2=1.0, op0=ALU.mult, op1=ALU.mult),
                         reads=[identb.tr, wi_all.tr], writes=[dgw[s].tr])
                chunks = [(s0, min(512, N - s0)) for s0 in range(0, N, 512)]
                pendD = []

                def flush_diag():
                    c_, grp_, rbs_, s0_, n_ = pendD.pop(0)
                    for (h, rb) in rbs_:
                        P.op(PE, lambda: nc.tensor.matmul(PS[3].t[:, :n_], lhsT=dgw[s].t[:, h, :], rhs=rb.t[:, :n_], start=(h == 0), stop=(h == 7)),
                             reads=[dgw[s].tr, rb.tr], writes=[PS[3].tr])
                    if grp_ == 1:
                        P.op(ACT, lambda: nc.scalar.copy(out=sc.t[:, s0_:s0_ + n_], in_=PS[3].t[:, :n_]), reads=[PS[3].tr], writes=[sctr[c_]])

                for c, (s0, n) in enumerate(chunks):
                    for grp in range(2):
                        rbs = []
                        for hh in range(4):
                            h = grp * 4 + hh
                            Lb = LB[hh]
                            po = (h % 2) * 64
                            P.op(PE, lambda: nc.tensor.matmul(Lb.t[:, :n], lhsT=qit[s].t[po:po + 64, h // 2, :], rhs=kiT.t[po:po + 64, s0:s0 + n], start=True, stop=True),
                                 reads=[qit[s].tr, kiT.tr], writes=[Lb.tr])
                            rb = Rr[ctr["r"] % 8]
                            ctr["r"] += 1
                            P.op(ACT, lambda: nc.scalar.activation(out=rb.t[:, :n], in_=Lb.t[:, :n], func=AF.Relu), reads=[Lb.tr], writes=[rb.tr])
                            rbs.append((h, rb))
                        if pendD:
                            flush_diag()
                        pendD.append((c, grp, rbs, s0, n))
                while pendD:
                    flush_diag()
                nsc = len(chunks)
                P.op(POOL, lambda: nc.gpsimd.tensor_tensor(out=sc.t[:, N - 128:N], in0=sc.t[:, N - 128:N], in1=negmask.t[:], op=ALU.add),
                     reads=[sctr[nsc - 1], negmask.tr], writes=[sctr[nsc - 1]])

            def S1b(i):
                s = i % 2
                N = 128 * (i + 1)
                sc = score[s]
                sc_trs = score_tr[s][:(N + 511) // 512]
                if cstop < 2:
                    return
                if i < 2:
                    thr = thrneg
                else:
                    P.op(DVE, lambda: nc.vector.tensor_reduce(out=hi.t[:], in_=sc.t[:, :N], axis=AX.X, op=ALU.max), reads=sc_trs, writes=[hi.tr])
                    P.op(DVE, lambda: nc.vector.tensor_reduce(out=lo.t[:], in_=sc.t[:, :256], axis=AX.X, op=ALU.min), reads=sc_trs, writes=[lo.tr])
                    P.op(DVE, lambda: nc.vector.tensor_tensor(out=w0.t[:], in0=hi.t[:], in1=lo.t[:], op=ALU.subtract), reads=[hi.tr, lo.tr], writes=[w0.tr])
                    P.op(DVE, lambda: nc.vector.tensor_scalar(out=Wk.t[:], in0=pw.t[:], scalar1=w0.t[:, 0:1], scalar2=None, op0=ALU.mult), reads=[pw.tr, w0.tr], writes=[Wk.tr])
                    P.op(DVE, lambda: nc.vector.tensor_tensor(out=mid[0].t[:], in0=lo.t[:], in1=Wk.t[:, 0:1], op=ALU.add), reads=[lo.tr, Wk.tr], writes=[mid[0].tr])
                    KB = max(12, KBIS - int(np.floor(np.log2(LP / N))))
                    for k in range(KB):
                        mc, mn = mid[k % 2], mid[(k + 1) % 2]
                        kb = k + 1 if k < KB - 1 else k
                        P.op(DVE, lambda: nc.vector.tensor_scalar(out=junk.t[:, :N], in0=sc.t[:, :N], scalar1=mc.t[:, 0:1], scalar2=None, op0=ALU.is_ge, op1=ALU.add, accum_out=cntb.t[:, 0:1]),
                             reads=sc_trs + [mc.tr], writes=[junk.tr, cntb.tr])
                        P.op(DVE, lambda: nc.vector.tensor_scalar(out=dd.t[:], in0=cntb.t[:], scalar1=TOPK - 0.5, scalar2=Wk.t[:, k:k + 1], op0=ALU.is_ge, op1=ALU.mult),
                             reads=[cntb.tr, Wk.tr], writes=[dd.tr])
                        P.op(DVE, lambda: nc.vector.scalar_tensor_tensor(out=mn.t[:], in0=dd.t[:], scalar=Wk.t[:, kb:kb + 1], in1=mc.t[:], op0=ALU.subtract, op1=ALU.add),
                             reads=[dd.tr, Wk.tr, mc.tr], writes=[mn.tr])
                    thr = mid[KB % 2]
                P.op(DVE, lambda: nc.vector.tensor_scalar(out=mask.t[:, :N], in0=sc.t[:, :N], scalar1=thr.t[:, 0:1], scalar2=None, op0=ALU.is_ge),
                     reads=sc_trs + [thr.tr], writes=[mask.tr])

            def S1c(i):
                if cstop < 3:
                    return
                s = i % 2
                nb = i + 1
                for g in range((nb + 7) // 8):
                    q = PQ[0]
                    m = min(8, nb - g * 8)
                    for bi in range(m):
                        b = g * 8 + bi
                        P.op(PE, lambda: nc.tensor.transpose(out=q.t[:, bi * 128:(bi + 1) * 128], in_=mask.t[:, b * 128:(b + 1) * 128], identity=identb.t[:]),
                             reads=[mask.tr, identb.tr], writes=[q.tr])
                    P.op(ACT, lambda: nc.scalar.copy(out=maskT[s].t[:, g * 1024:g * 1024 + m * 128], in_=q.t[:, :m * 128]), reads=[q.tr], writes=[maskT_tr[s][g]])

            def S2(i):
                if cstop < 4:
                    return
                s = i % 2
                nb = i + 1
                c0, c1 = i * 128, (i + 1) * 128
                mT, mTtr = maskT[s], maskT_tr[s]
                units = [(hp, b0, min(4, nb - b0)) for hp in range(4) for b0 in range(0, nb, 4)]
                pend = []

                def pv(hp, b0, m, ptbs):
                    for hi_, ptb in enumerate(ptbs):
                        h = 2 * hp + hi_
                        Ob = PS[4 + hi_]
                        hh = hp
                        for bi in range(m):
                            b = b0 + bi
                            P.op(PE, lambda: nc.tensor.matmul(Ob.t[:, hh * 65:(hh + 1) * 65], lhsT=ptb.t[:, bi * 128:(bi + 1) * 128], rhs=vaug.t[:, b, h * 65:(h + 1) * 65], start=(b == 0), stop=(b == nb - 1)),
                                 reads=[ptb.tr, vaug.tr], writes=[Ob.tr])

                for (hp, b0, m) in units:
                    Sbs = [LB[ctr["m"] % 4], LB[(ctr["m"] + 1) % 4]]
                    ctr["m"] += 2
                    for bi in range(m):
                        b = b0 + bi
                        for hi_ in range(2):
                            po = hi_ * 64
                            Sb = Sbs[hi_]
                            P.op(PE, lambda: nc.tensor.matmul(Sb.t[:, bi * 128:(bi + 1) * 128], lhsT=kT.t[po:po + 64, hp, b * 128:(b + 1) * 128], rhs=qt[s].t[po:po + 64, hp, :], start=True, stop=True),
                                 reads=[kT.tr, qt[s].tr], writes=[Sb.tr])
                    mtrs = list({id(mTtr[b // 8]): mTtr[b // 8] for b in range(b0, b0 + m)}.values())
                    ptbs = []
                    for hi_ in range(2):
                        Sb = Sbs[hi_]
                        eb = Eb[ctr["e"] % 4]
                        ptb = PTb[ctr["e"] % 6]
                        ctr["e"] += 1
                        P.op(ACT, lambda: nc.scalar.activation(out=eb.t[:, :m * 128], in_=Sb.t[:, :m * 128], func=AF.Exp, scale=0.125), reads=[Sb.tr], writes=[eb.tr])
                        P.op(POOL, lambda: nc.gpsimd.tensor_tensor(out=ptb.t[:, :m * 128], in0=eb.t[:, :m * 128], in1=mT.t[:, b0 * 128:(b0 + m) * 128], op=ALU.mult),
                             reads=[eb.tr] + mtrs, writes=[ptb.tr])
                        ptbs.append(ptb)
                    pend.append((hp, b0, m, ptbs))
                    if len(pend) > 1:
                        pv(*pend.pop(0))
                while pend:
                    pv(*pend.pop(0))
                if cstop < 5:
                    return
                for half in range(2):
                    Ob = PS[4 + half]
                    P.op(ACT, lambda: nc.scalar.copy(out=osb.t[:, half * 260:(half + 1) * 260], in_=Ob.t[:, :260]), reads=[Ob.tr], writes=[osb_tr[half]])
                    P.op(POOL, lambda: nc.gpsimd.tensor_copy(out=rsum.t[:, half * 4:(half + 1) * 4], in_=osb.t[:, half * 260:(half + 1) * 260].rearrange("p (h d) -> p h d", h=4)[:, :, 64]),
                         reads=[osb_tr[half]], writes=[rsum_tr[half]])
                    P.op(POOL, lambda: nc.gpsimd.tensor_tensor(out=rinv.t[:, half * 4:(half + 1) * 4], in0=rsum.t[:, half * 4:(half + 1) * 4], in1=mones.t[:, 0:4], op=ALU.pow),
                         reads=[rsum_tr[half], mones.tr], writes=[rinv_tr[half]])
                for h in range(8):
                    oc = (h % 2) * 260 + (h // 2) * 65
                    ri = (h % 2) * 4 + h // 2
                    P.op(ACT, lambda: nc.scalar.activation(out=ya.t[:, h * 64:(h + 1) * 64], in_=osb.t[:, oc:oc + 64], func=AF.Identity, scale=rinv.t[:, ri:ri + 1]),
                         reads=[osb_tr[h % 2], rinv_tr[h % 2]], writes=[ya_tr[h]])
                if cstop < 6:
                    return
                q = PQ[0]
                for j in range(4):
                    P.op(PE, lambda: nc.tensor.transpose(out=q.t[:, j * 128:(j + 1) * 128], in_=ya.t[:, j * 128:(j + 1) * 128], identity=identb.t[:]),
                         reads=[ya_tr[2 * j], ya_tr[2 * j + 1], identb.tr], writes=[q.tr])
                P.op(ACT, lambda: nc.scalar.copy(out=yaT.t[:], in_=q.t[:, :512].rearrange("p (j t) -> p j t", j=4)), reads=[q.tr], writes=[yaT.tr])
                P.op(POOL, lambda: nc.gpsimd.tensor_tensor(out=yat[s].t[:], in0=yaT.t[:], in1=szat[s].t[:], op=ALU.mult),
                     reads=[yaT.tr, szat[s].tr], writes=[yat[s].tr])
                if cstop < 7:
                    return
                P.dma(POOL, yat[s].sem, YC[4:8, :, c0:c1].rearrange("j p t -> p j t"), yat[s].t[:], reads=[yat[s].tr])

            if NTC > 0:
                load_idx(0)
                if NTC > 1:
                    load_idx(1)
            load_kv()
            if NTC > 0:
                S1a(0)
            for i in range(NTC):
                if i + 2 < NTC:
                    load_idx(i + 2)
                load_att(i)
                if i + 1 < NTC:
                    S1a(i + 1)
                if i >= 1:
                    S2(i - 1)
                S1b(i)
                S1c(i)
            if NTC > 0:
                S2(NTC - 1)
            P.barrier()

        with ExitStack() as sD:
            yc = sb(sD, "yc", [128, 8, LP], BF16, True)
            wol = [sb(sD, f"wol{i}", [128, 1024], F32, True) for i in range(4)]
            wob = sb(sD, "wob", [128, 8, 1024], BF16)
            wob_tr = [Tr() for _ in range(8)]
            grep = sb(sD, "grep", [128, 1024], F32, True)
            brep = sb(sD, "brep", [128, 1024], F32, True)
            htl = [sb(sD, f"htl{i}", [128, 1024], F32, True) for i in range(3)]
            zt = [sb(sD, f"zt{i}", [128, 1024], F32) for i in range(3)]
            zn = [sb(sD, f"zn{i}", [128, 1024], F32) for i in range(3)]
            o1 = [sb(sD, f"o1{i}", [128, 1024], F32) for i in range(3)]
            ho = [sb(sD, f"ho{i}", [128, 1024], F32, True) for i in range(3)]
            st6 = [sb(sD, f"st6{i}", [128, 12], F32) for i in range(3)]
            mv = [sb(sD, f"mv{i}", [128, 2], F32) for i in range(3)]
            sdv = [sb(sD, f"sdv{i}", [128, 1], F32) for i in range(3)]
            rs = [sb(sD, f"rs{i}", [128, 1], F32) for i in range(3)]
            nmr = [sb(sD, f"nmr{i}", [128, 1], F32) for i in range(3)]

            for ec in range(8):
                P.dma(SP, yc.sem, yc.t[:, ec, :], YC[ec], writes=[yc.tr])
            P.dma(SP, grep.sem, grep.t[:], GB[l, 0], writes=[grep.tr])
            P.dma(SP, brep.sem, brep.t[:], GB[l, 1], writes=[brep.tr])
            for ec in range(8):
                s = ec % 4
                P.dma(SP, wol[s].sem, wol[s].t[:], WO[l, ec], writes=[wol[s].tr])
                if ec % 2 == 0:
                    P.op(POOL, lambda s=s, ec=ec: nc.gpsimd.tensor_copy(out=wob.t[:, ec, :], in_=wol[s].t[:]), reads=[wol[s].tr], writes=[wob_tr[ec]])
                else:
                    P.op(DVE, lambda s=s, ec=ec: nc.vector.tensor_copy(out=wob.t[:, ec, :], in_=wol[s].t[:]), reads=[wol[s].tr], writes=[wob_tr[ec]])
            for i in range(NT if "D" in phases else 0):
                s = i % 3
                P.dma(SP, htl[s].sem, htl[s].t[:], h_in[i * 128:(i + 1) * 128, :], writes=[htl[s].tr])
                for half in range(2):
                    b = PS[(i % 3) * 2 + half]
                    for ec in range(8):
                        P.op(PE, lambda b=b, ec=ec, half=half, i=i: nc.tensor.matmul(b.t[:, :], lhsT=yc.t[:, ec, i * 128:(i + 1) * 128], rhs=wob.t[:, ec, half * 512:(half + 1) * 512], start=(ec == 0), stop=(ec == 7)),
                             reads=[yc.tr, wob_tr[ec]], writes=[b.tr])
                    P.op(DVE, lambda b=b, half=half, s=s: nc.vector.scalar_tensor_tensor(out=zt[s].t[:, half * 512:(half + 1) * 512], in0=htl[s].t[:, half * 512:(half + 1) * 512], scalar=ALPHA, in1=b.t[:, :], op0=ALU.mult, op1=ALU.add),
                         reads=[htl[s].tr, b.tr], writes=[zt[s].tr])
                for half in range(2):
                    P.op(DVE, lambda half=half, s=s: nc.vector.bn_stats(out=st6[s].t[:, half * 6:(half + 1) * 6], in_=zt[s].t[:, half * 512:(half + 1) * 512]), reads=[zt[s].tr], writes=[st6[s].tr])
                P.op(DVE, lambda s=s: nc.vector.bn_aggr(out=mv[s].t[:], in_=st6[s].t[:]), reads=[st6[s].tr], writes=[mv[s].tr])
                P.op(ACT, lambda s=s: nc.scalar.activation(out=sdv[s].t[:], in_=mv[s].t[:, 1:2], func=AF.Sqrt, bias=epst.t[:, 0:1]), reads=[mv[s].tr, epst.tr], writes=[sdv[s].tr])
                P.op(DVE, lambda s=s: nc.vector.reciprocal(out=rs[s].t[:], in_=sdv[s].t[:]), reads=[sdv[s].tr], writes=[rs[s].tr])
                P.op(DVE, lambda s=s: nc.vector.scalar_tensor_tensor(out=nmr[s].t[:], in0=mv[s].t[:, 0:1], scalar=-1.0, in1=rs[s].t[:], op0=ALU.mult, op1=ALU.mult), reads=[mv[s].tr, rs[s].tr], writes=[nmr[s].tr])
                P.op(ACT, lambda s=s: nc.scalar.activation(out=zn[s].t[:], in_=zt[s].t[:], func=AF.Identity, scale=rs[s].t[:, 0:1], bias=nmr[s].t[:, 0:1]), reads=[zt[s].tr, rs[s].tr, nmr[s].tr], writes=[zn[s].tr])
                P.op(POOL, lambda s=s: nc.gpsimd.tensor_tensor(out=o1[s].t[:], in0=zn[s].t[:], in1=grep.t[:], op=ALU.mult), reads=[zn[s].tr, grep.tr], writes=[o1[s].tr])
                P.op(POOL, lambda s=s: nc.gpsimd.tensor_tensor(out=ho[s].t[:], in0=o1[s].t[:], in1=brep.t[:], op=ALU.add), reads=[o1[s].tr, brep.tr], writes=[ho[s].tr])
                P.dma(POOL, ho[s].sem, h_out[i * 128:(i + 1) * 128, :], ho[s].t[:], reads=[ho[s].tr])
            P.barrier()

    es.close()
    return nc, P.nins


def _rope_tables():
    inv_freq = (10000.0 ** (-np.arange(0, 64, 2, dtype=np.float32) / np.float32(64))).astype(np.float32)
    ang = np.arange(LP, dtype=np.float32)[:, None] * inv_freq[None, :]
    cos = np.cos(ang).astype(np.float32).T
    sin = np.sin(ang).astype(np.float32).T
    p = np.arange(128)
    d = p % 64
    cosT = cos[d % 32]
    sinT = np.where((d < 32)[:, None], -sin[d % 32], sin[d % 32])
    return np.ascontiguousarray(cosT, dtype=np.float32), np.ascontiguousarray(sinT, dtype=np.float32)


def _prep_weights(w_in, conv_w, conv_b, conv_ln_g, conv_ln_b, w_out, post_ln_g, post_ln_b):
    Ld = w_in.shape[0]
    cols = _weight_cols()
    WA = np.empty((Ld, NWCH, 128, 1024), np.float32)
    for l in range(Ld):
        for c, cc in enumerate(cols):
            blk = w_in[l][:, cc]
            WA[l, c] = blk.reshape(8, 128, 128).transpose(1, 0, 2).reshape(128, 1024)
        for hv in range(2):
            blk = w_in[l][:, O_V + hv * 256:O_V + (hv + 1) * 256]
            arr = blk.reshape(8, 128, 256).transpose(1, 0, 2).reshape(128, 2048)
            WA[l, len(cols) + 2 * hv] = arr[:, :1024]
            WA[l, len(cols) + 2 * hv + 1] = arr[:, 1024:]
    WWI = np.ascontiguousarray(w_in[:, :, O_WI:O_WI + 8].reshape(Ld, 8, 128, 8).transpose(0, 2, 1, 3).reshape(Ld, 128, 64))
    CPa = np.empty((Ld, 128, 136), np.float32)
    CPa[:, :, 0:124] = conv_w.reshape(Ld, 31, 4, 128).transpose(0, 3, 2, 1).reshape(Ld, 128, 124)
    CPa[:, :, 124:128] = conv_b.reshape(Ld, 4, 128).transpose(0, 2, 1)
    CPa[:, :, 128:132] = conv_ln_g.reshape(Ld, 4, 128).transpose(0, 2, 1)
    CPa[:, :, 132:136] = conv_ln_b.reshape(Ld, 4, 128).transpose(0, 2, 1)
    WOa = np.ascontiguousarray(w_out.reshape(Ld, 8, 128, 1024))
    GBa = np.empty((Ld, 2, 128, 1024), np.float32)
    GBa[:, 0] = post_ln_g[:, None, :]
    GBa[:, 1] = post_ln_b[:, None, :]
    return WA, WWI, CPa, WOa, GBa


def _consts():
    ident = np.eye(128, dtype=np.float32)
    t = np.arange(128)
    negmask = np.where(t[None, :] <= t[:, None], 0.0, NEG).astype(np.float32)
    cosT, sinT = _rope_tables()
    pwr = np.broadcast_to((0.5 ** np.arange(1, KBIS + 1)).astype(np.float32)[None, :], (128, KBIS)).copy()
    return {"c_ident": ident, "c_negmask": negmask, "c_cos": cosT, "c_sin": sinT, "c_pw": pwr}


_CACHE = {}
FUSED = True


def _get_prog(n_layers):
    if n_layers not in _CACHE:
        _CACHE[n_layers] = build_program(n_layers)[0]
    return _CACHE[n_layers]


def kernel(x, meta_tokens, w_in, conv_w, conv_b, conv_ln_g, conv_ln_b, w_out, post_ln_g, post_ln_b):
    x = np.asarray(x, np.float32)
    B = x.shape[0]
    f = lambda a: np.asarray(a, np.float32)
    WA, WWI, CPa, WOa, GBa = _prep_weights(f(w_in), f(conv_w), f(conv_b), f(conv_ln_g), f(conv_ln_b), f(w_out), f(post_ln_g), f(post_ln_b))
    consts = _consts()
    hs = []
    for b in range(B):
        h = np.zeros((LP, D_MODEL), np.float32)
        h[:N_META] = f(meta_tokens)
        h[N_META:N_META + SEQ] = x[b]
        hs.append(h)
    if FUSED:
        nc = _get_prog(DEPTH)
        in_maps = [dict(h0=hs[b], WA=WA, WWI=WWI, CP=CPa, WO=WOa, GB=GBa, **consts) for b in range(B)]
        res = run_bass_kernel_spmd(nc, in_maps, core_ids=list(range(B)))
        hs = [res.results[b]["hout"] for b in range(B)]
    else:
        nc = _get_prog(1)
        for l in range(DEPTH):
            in_maps = [dict(h0=hs[b], WA=WA[l:l + 1], WWI=WWI[l:l + 1], CP=CPa[l:l + 1], WO=WOa[l:l + 1], GB=GBa[l:l + 1], **consts) for b in range(B)]
            res = run_bass_kernel_spmd(nc, in_maps, core_ids=list(range(B)))
            hs = [res.results[b]["hout"] for b in range(B)]
    out = np.stack([hs[b][N_META:N_META + SEQ] for b in range(B)], axis=0)
    return np.ascontiguousarray(out, dtype=np.float32)
```
